# Optimizing a Trainium2 kernel written in Bass

```python
import math
import jax
import jax.numpy as jnp
from jax import lax
import numpy as np

D_MODEL = 2048
BATCH = 16
SEQ = 2048
DEPTH = 2

CHUNK = 64
NEG_INF = -1e30
NORM_EPS = 1e-6

HEAD_DIM = 128
A_HEADS = 8
A_WIDTH = A_HEADS * HEAD_DIM
A_LEFT_CHUNKS = 8
A_BAND = (A_LEFT_CHUNKS + 1) * CHUNK
A_MAX_REL = 128
B_HEADS = 4
B_QK_DIM = HEAD_DIM
B_V_DIM = 2 * HEAD_DIM
B_QK_WIDTH = B_HEADS * 2 * B_QK_DIM
B_WIDTH = B_HEADS * B_V_DIM
Q_BLOCK = 128
ATTN_SPLITS = (A_WIDTH, A_WIDTH, A_WIDTH, A_WIDTH, B_QK_WIDTH, B_QK_WIDTH, B_WIDTH, B_WIDTH)
ATTN_IN = sum(ATTN_SPLITS)
ATTN_MIX = A_WIDTH + B_WIDTH

LRU_BLOCKS = 6
LRU_BLOCK_W = 256
LRU_WIDTH = LRU_BLOCKS * LRU_BLOCK_W
CONV_W = 4
LRU_C = 8.0
SSM_GROUP = 16
SSM_GROUPS = 32
SSM_WIDTH = SSM_GROUPS * SSM_GROUP
SSM_STATE = 64
REC_SPLITS = (LRU_WIDTH, LRU_WIDTH, SSM_WIDTH, SSM_WIDTH)
REC_IN = sum(REC_SPLITS)
REC_MIX = LRU_WIDTH + SSM_WIDTH

N_EVEN = (DEPTH + 1) // 2
N_ODD = DEPTH // 2

kernel_name = 'chunk_causal_hybrid_relpos_diffattn_rglru_s5'


def rms_norm(x, gain):
    xf = x.astype(jnp.float32)
    y = xf * lax.rsqrt(jnp.mean(xf * xf, axis=-1, keepdims=True) + NORM_EPS)
    return (y * gain.astype(jnp.float32)).astype(x.dtype)


def split_cols(t, sizes):
    outs, start = [], 0
    for n in sizes:
        outs.append(t[..., start:start + n])
        start += n
    return outs


def chunked_relpos_attention(q, k, v, rel_bias):
    b, s, h, dh = q.shape
    n_chunks = s // CHUNK
    pad = A_LEFT_CHUNKS * CHUNK
    kp = jnp.pad(k, ((0, 0), (pad, 0), (0, 0), (0, 0)))
    vp = jnp.pad(v, ((0, 0), (pad, 0), (0, 0), (0, 0)))
    qi = jnp.arange(CHUNK)[:, None]
    kj = jnp.arange(A_BAND)[None, :]
    rel = jnp.clip(kj - pad - qi, -A_MAX_REL, A_MAX_REL) + A_MAX_REL
    bias = rel_bias.astype(jnp.float32)[:, rel]
    q_chunks = q.reshape(b, n_chunks, CHUNK, h, dh).transpose(1, 0, 2, 3, 4)
    scale = dh ** -0.5

    def one_chunk(args):
        qc, c = args
        start = c * CHUNK
        kc = lax.dynamic_slice_in_dim(kp, start, A_BAND, axis=1)
        vc = lax.dynamic_slice_in_dim(vp, start, A_BAND, axis=1)
        logits = jnp.einsum('bqhd,bkhd->bhqk', qc, kc).astype(jnp.float32) * scale + bias
        valid = (start - pad + jnp.arange(A_BAND)) >= 0
        logits = jnp.where(valid, logits, NEG_INF)
        p = jax.nn.softmax(logits, axis=-1).astype(vc.dtype)
        return jnp.einsum('bhqk,bkhd->bqhd', p, vc)

    out = lax.map(one_chunk, (q_chunks, jnp.arange(n_chunks)))
    return out.transpose(1, 0, 2, 3, 4).reshape(b, s, h, dh)


def differential_attention(q, k, v, lam):
    b, s, h, _, dk = q.shape
    n_blocks = s // Q_BLOCK
    q_blocks = q.reshape(b, n_blocks, Q_BLOCK, h, 2, dk).transpose(1, 0, 2, 3, 4, 5)
    k_pos = jnp.arange(s)
    slopes = 2.0 ** (-8.0 * jnp.arange(1, h + 1, dtype=jnp.float32) / h)
    scale = dk ** -0.5

    def one_block(args):
        qb, blk = args
        q_pos = blk * Q_BLOCK + jnp.arange(Q_BLOCK)
        allowed = (k_pos[None, :] // CHUNK) <= (q_pos[:, None] // CHUNK)
        dist = jnp.abs(q_pos[:, None] - k_pos[None, :]).astype(jnp.float32)
        bias = -slopes[:, None, None] * dist[None]
        logits = jnp.einsum('bqhcd,bkhcd->bchqk', qb, k).astype(jnp.float32) * scale + bias
        logits = jnp.where(allowed, logits, NEG_INF)
        p = jax.nn.softmax(logits, axis=-1)
        w = (p[:, 0] - lam * p[:, 1]).astype(v.dtype)
        return jnp.einsum('bhqk,bkhd->bqhd', w, v)

    out = lax.map(one_block, (q_blocks, jnp.arange(n_blocks)))
    return out.transpose(1, 0, 2, 3, 4).reshape(b, s, h, v.shape[-1])


def causal_depthwise_conv(u, w, bias):
    s = u.shape[1]
    up = jnp.pad(u, ((0, 0), (CONV_W - 1, 0), (0, 0)))
    return sum(w[tap] * up[:, tap:tap + s] for tap in range(CONV_W)) + bias


def linear_scan(a, b):
    def combine(left, right):
        a_l, b_l = left
        a_r, b_r = right
        return a_l * a_r, a_r * b_l + b_r
    _, h = lax.associative_scan(combine, (a, b), axis=1)
    return h


def rg_lru(xr, w_a, b_a, w_x, b_x, lam):
    b, s, _ = xr.shape
    f32 = jnp.float32
    xf = xr.astype(f32)
    xb = xf.reshape(b, s, LRU_BLOCKS, LRU_BLOCK_W)
    r = jax.nn.sigmoid(jnp.einsum('bsni,nij->bsnj', xb, w_a.astype(f32)) + b_a.astype(f32))
    i = jax.nn.sigmoid(jnp.einsum('bsni,nij->bsnj', xb, w_x.astype(f32)) + b_x.astype(f32))
    r = r.reshape(b, s, LRU_WIDTH)
    i = i.reshape(b, s, LRU_WIDTH)
    log_a = -LRU_C * r * jax.nn.softplus(-lam.astype(f32))
    a = jnp.exp(log_a)
    mult = jnp.sqrt(-jnp.expm1(2.0 * log_a))
    h = linear_scan(a, mult * (i * xf))
    return h.astype(xr.dtype)


def s5_ssm(u, a_re, a_im, b_re, b_im, c_re, c_im, d_skip, log_dt):
    b, s, _ = u.shape
    f32 = jnp.float32
    uf = u.astype(f32).reshape(b, s, SSM_GROUPS, SSM_GROUP)
    lam = lax.complex(a_re.astype(f32), a_im.astype(f32))
    dt = jnp.exp(log_dt.astype(f32))[:, None]
    a_bar = jnp.exp(lam * dt)
    b_mat = lax.complex(b_re.astype(f32), b_im.astype(f32))
    b_bar = ((a_bar - 1.0) / lam)[..., None] * b_mat
    bu = jnp.einsum('bsgh,gph->bsgp', uf.astype(jnp.complex64), b_bar)
    a_seq = jnp.broadcast_to(a_bar, (1, s) + a_bar.shape)
    states = linear_scan(a_seq, bu)
    c_mat = lax.complex(c_re.astype(f32), c_im.astype(f32))
    y = jnp.einsum('bsgp,ghp->bsgh', states, c_mat).real
    y = y + d_skip.astype(f32).reshape(SSM_GROUPS, SSM_GROUP) * uf
    return y.reshape(b, s, SSM_WIDTH)


def attention_layer(x, norm_g, w_in, q_g_a, k_g_a, rel_bias, q_g_b, k_g_b,
                    lq1, lk1, lq2, lk2, subln_g, w_out, layer_idx):
    b, s, _ = x.shape
    h = rms_norm(x, norm_g)
    aq, ak, av, ag, bq, bk, bv, bg = split_cols(h @ w_in, ATTN_SPLITS)
    aq = rms_norm(aq.reshape(b, s, A_HEADS, HEAD_DIM), q_g_a)
    ak = rms_norm(ak.reshape(b, s, A_HEADS, HEAD_DIM), k_g_a)
    av = av.reshape(b, s, A_HEADS, HEAD_DIM)
    ya = chunked_relpos_attention(aq, ak, av, rel_bias).reshape(b, s, A_WIDTH) * jax.nn.silu(ag)
    bq = rms_norm(bq.reshape(b, s, B_HEADS, 2, B_QK_DIM), q_g_b)
    bk = rms_norm(bk.reshape(b, s, B_HEADS, 2, B_QK_DIM), k_g_b)
    bv = bv.reshape(b, s, B_HEADS, B_V_DIM)
    f32 = jnp.float32
    lam_init = 0.8 - 0.6 * math.exp(-0.3 * layer_idx)
    lam = (jnp.exp(jnp.sum(lq1.astype(f32) * lk1.astype(f32)))
           - jnp.exp(jnp.sum(lq2.astype(f32) * lk2.astype(f32))) + lam_init)
    yb = differential_attention(bq, bk, bv, lam)
    yb = (rms_norm(yb, subln_g) * (1.0 - lam_init)).reshape(b, s, B_WIDTH) * jax.nn.silu(bg)
    return x + jnp.concatenate([ya, yb], axis=-1) @ w_out


def recurrent_layer(x, norm_g, w_in, conv_w, conv_b, w_a, b_a, w_x, b_x, lru_lam,
                    a_re, a_im, b_re, b_im, c_re, c_im, d_skip, log_dt, w_glu, b_glu, w_out):
    h = rms_norm(x, norm_g)
    c_in, c_gate, d_in, d_gate = split_cols(h @ w_in, REC_SPLITS)
    yc = rg_lru(causal_depthwise_conv(c_in, conv_w, conv_b), w_a, b_a, w_x, b_x, lru_lam)
    yc = yc * jax.nn.silu(c_gate)
    yd = jax.nn.gelu(s5_ssm(d_in, a_re, a_im, b_re, b_im, c_re, c_im, d_skip, log_dt))
    yd = yd * jax.nn.sigmoid(yd @ w_glu.astype(jnp.float32) + b_glu.astype(jnp.float32))
    yd = yd.astype(x.dtype) * jax.nn.silu(d_gate)
    return x + jnp.concatenate([yc, yd], axis=-1) @ w_out


def setup_inputs(seed: int = 0) -> dict:
    key = jax.random.key(seed)
    ks = list(jax.random.split(key, 48))
    f32 = jnp.float32

    def nrm(shape, std):
        return std * jax.random.normal(ks.pop(), shape, f32)

    def gain(shape):
        return 1.0 + nrm(shape, 0.02)

    ne, no = N_EVEN, N_ODD
    x = nrm((BATCH, SEQ, D_MODEL), 1.0)
    attn_norm_g = gain((ne, D_MODEL))
    attn_w_in = nrm((ne, D_MODEL, ATTN_IN), D_MODEL ** -0.5)
    a_q_g = gain((ne, HEAD_DIM))
    a_k_g = gain((ne, HEAD_DIM))
    a_rel_bias = nrm((ne, A_HEADS, 2 * A_MAX_REL + 1), 0.5)
    b_q_g = gain((ne, 2, B_QK_DIM))
    b_k_g = gain((ne, 2, B_QK_DIM))
    b_lam_q1 = nrm((ne, B_QK_DIM), 0.1)
    b_lam_k1 = nrm((ne, B_QK_DIM), 0.1)
    b_lam_q2 = nrm((ne, B_QK_DIM), 0.1)
    b_lam_k2 = nrm((ne, B_QK_DIM), 0.1)
    b_subln_g = gain((ne, B_V_DIM))
    attn_w_out = nrm((ne, ATTN_MIX, D_MODEL), ATTN_MIX ** -0.5)
    rec_norm_g = gain((no, D_MODEL))
    rec_w_in = nrm((no, D_MODEL, REC_IN), D_MODEL ** -0.5)
    lru_conv_w = nrm((no, CONV_W, LRU_WIDTH), CONV_W ** -0.5)
    lru_conv_b = nrm((no, LRU_WIDTH), 0.02)
    lru_w_a = nrm((no, LRU_BLOCKS, LRU_BLOCK_W, LRU_BLOCK_W), LRU_BLOCK_W ** -0.5)
    lru_b_a = nrm((no, LRU_BLOCKS, LRU_BLOCK_W), 0.02)
    lru_w_x = nrm((no, LRU_BLOCKS, LRU_BLOCK_W, LRU_BLOCK_W), LRU_BLOCK_W ** -0.5)
    lru_b_x = nrm((no, LRU_BLOCKS, LRU_BLOCK_W), 0.02)
    a_pow = jax.random.uniform(ks.pop(), (no, LRU_WIDTH), f32, 0.9, 0.999)
    sig = a_pow ** (1.0 / LRU_C)
    lru_lambda = jnp.log(sig) - jnp.log1p(-sig)
    ssm_a_re = -0.5 + nrm((no, SSM_GROUPS, SSM_STATE), 0.01)
    ssm_a_im = math.pi * jnp.arange(SSM_STATE, dtype=f32) + nrm((no, SSM_GROUPS, SSM_STATE), 0.01)
    ssm_b_re = nrm((no, SSM_GROUPS, SSM_STATE, SSM_GROUP), (0.5 / SSM_GROUP) ** 0.5)
    ssm_b_im = nrm((no, SSM_GROUPS, SSM_STATE, SSM_GROUP), (0.5 / SSM_GROUP) ** 0.5)
    ssm_c_re = nrm((no, SSM_GROUPS, SSM_GROUP, SSM_STATE), (0.5 / SSM_STATE) ** 0.5)
    ssm_c_im = nrm((no, SSM_GROUPS, SSM_GROUP, SSM_STATE), (0.5 / SSM_STATE) ** 0.5)
    ssm_d = nrm((no, SSM_WIDTH), 1.0)
    ssm_log_dt = jax.random.uniform(ks.pop(), (no, SSM_GROUPS), f32, math.log(1e-3), math.log(1e-1))
    ssm_w_glu = nrm((no, SSM_WIDTH, SSM_WIDTH), SSM_WIDTH ** -0.5)
    ssm_b_glu = nrm((no, SSM_WIDTH), 0.02)
    rec_w_out = nrm((no, REC_MIX, D_MODEL), REC_MIX ** -0.5)
    return {'x': x, 'attn_norm_g': attn_norm_g, 'attn_w_in': attn_w_in, 'a_q_g': a_q_g,
            'a_k_g': a_k_g, 'a_rel_bias': a_rel_bias, 'b_q_g': b_q_g, 'b_k_g': b_k_g,
            'b_lam_q1': b_lam_q1, 'b_lam_k1': b_lam_k1, 'b_lam_q2': b_lam_q2, 'b_lam_k2': b_lam_k2,
            'b_subln_g': b_subln_g, 'attn_w_out': attn_w_out, 'rec_norm_g': rec_norm_g,
            'rec_w_in': rec_w_in, 'lru_conv_w': lru_conv_w, 'lru_conv_b': lru_conv_b,
            'lru_w_a': lru_w_a, 'lru_b_a': lru_b_a, 'lru_w_x': lru_w_x, 'lru_b_x': lru_b_x,
            'lru_lambda': lru_lambda, 'ssm_a_re': ssm_a_re, 'ssm_a_im': ssm_a_im,
            'ssm_b_re': ssm_b_re, 'ssm_b_im': ssm_b_im, 'ssm_c_re': ssm_c_re, 'ssm_c_im': ssm_c_im,
            'ssm_d': ssm_d, 'ssm_log_dt': ssm_log_dt, 'ssm_w_glu': ssm_w_glu, 'ssm_b_glu': ssm_b_glu,
            'rec_w_out': rec_w_out}


def reference(x, attn_norm_g, attn_w_in, a_q_g, a_k_g, a_rel_bias, b_q_g, b_k_g,
              b_lam_q1, b_lam_k1, b_lam_q2, b_lam_k2, b_subln_g, attn_w_out,
              rec_norm_g, rec_w_in, lru_conv_w, lru_conv_b, lru_w_a, lru_b_a, lru_w_x, lru_b_x,
              lru_lambda, ssm_a_re, ssm_a_im, ssm_b_re, ssm_b_im, ssm_c_re, ssm_c_im,
              ssm_d, ssm_log_dt, ssm_w_glu, ssm_b_glu, rec_w_out):
    for layer in range(DEPTH):
        j = layer // 2
        if layer % 2 == 0:
            x = attention_layer(x, attn_norm_g[j], attn_w_in[j], a_q_g[j], a_k_g[j], a_rel_bias[j],
                                b_q_g[j], b_k_g[j], b_lam_q1[j], b_lam_k1[j], b_lam_q2[j], b_lam_k2[j],
                                b_subln_g[j], attn_w_out[j], layer)
        else:
            x = recurrent_layer(x, rec_norm_g[j], rec_w_in[j], lru_conv_w[j], lru_conv_b[j],
                                lru_w_a[j], lru_b_a[j], lru_w_x[j], lru_b_x[j], lru_lambda[j],
                                ssm_a_re[j], ssm_a_im[j], ssm_b_re[j], ssm_b_im[j],
                                ssm_c_re[j], ssm_c_im[j], ssm_d[j], ssm_log_dt[j],
                                ssm_w_glu[j], ssm_b_glu[j], rec_w_out[j])
    return x
```

```python
import math
import numpy as np
from contextlib import ExitStack
import concourse.bass as bass
import concourse.mybir as mybir
from concourse.bass_utils import run_bass_kernel_spmd

F32 = mybir.dt.float32
BF16 = mybir.dt.bfloat16
AF = mybir.ActivationFunctionType
ALU = mybir.AluOpType

D = 2048
KT = 16
EPS = 1e-6
NEG = -30000.0
N_CORES = 8
SEQ = 2048
BATCH = 16


class Buf:
    __slots__ = ("name", "w", "r", "excl")

    def __init__(self, name="", excl=False):
        self.name = name
        self.w = None
        self.r = {}
        self.excl = excl


class Eng:
    def __init__(self, name, eng):
        self.name = name
        self.eng = eng
        self.sem = None
        self.count = 0
        self.seen = {}
        self.pend_r = []
        self.pend_w = []


class FW:
    def __init__(self, nc, stack, n_dma_sems=12):
        self.nc = nc
        self.E = {}
        for name, eng in (("pe", nc.tensor), ("act", nc.scalar), ("dve", nc.vector),
                          ("pool", nc.gpsimd), ("sp", nc.sync)):
            e = Eng(name, eng)
            e.sem = stack.enter_context(nc.semaphore("s_" + name))
            self.E[name] = e
        self.dma_sems = {}
        for q, nq in (("sp", n_dma_sems), ("pool", 4)):
            sems = [stack.enter_context(nc.semaphore(f"d_{q}{i}")) for i in range(nq)]
            self.dma_sems[q] = {"sems": sems, "vals": [0] * nq, "next": 0}
        self.same_eng_sync = True
        self.out_events = []

    def _wait(self, e, ev, raw=True):
        if ev is None:
            return
        key, val, sem = ev
        if key == e.name and (not raw or e.name == "pe" or not self.same_eng_sync):
            return
        if e.seen.get(key, 0) >= val:
            return
        e.eng.wait_ge(sem, val)
        e.seen[key] = val

    def _deps(self, e, reads, writes, extra):
        for b in reads:
            self._wait(e, b.w)
            if b.excl:
                for ev in b.r.values():
                    if ev[0] != e.name:
                        self._wait(e, ev)
        for b in writes:
            self._wait(e, b.w, raw=False)
            for ev in b.r.values():
                self._wait(e, ev, raw=False)
        for ev in extra:
            self._wait(e, ev)

    def _record(self, ev, reads, writes):
        for b in reads:
            old = b.r.get(ev[0])
            if old is None or old[1] < ev[1]:
                b.r[ev[0]] = ev
        for b in writes:
            b.w = ev
            b.r = {}

    def op(self, engname, fn, reads=(), writes=(), extra=(), inc=True):
        e = self.E[engname]
        self._deps(e, reads, writes, extra)
        ins = fn(e.eng)
        if inc:
            e.count += 1
            ins.then_inc(e.sem, 1)
            ev = (e.name, e.count, e.sem)
            self._record(ev, list(reads) + e.pend_r, list(writes) + e.pend_w)
            e.pend_r = []
            e.pend_w = []
            return ev
        e.pend_r.extend(reads)
        e.pend_w.extend(writes)
        return None

    def dma(self, q, out, in_, reads=(), writes=(), extra=(), is_output=False, **kw):
        e = self.E[q]
        pool = self.dma_sems[q]
        i = pool["next"]
        pool["next"] = (i + 1) % len(pool["sems"])
        sem = pool["sems"][i]
        key = f"d_{q}{i}"
        if pool["vals"][i] > 0:
            self._wait(e, (key, pool["vals"][i], sem))
        self._deps(e, reads, writes, extra)
        pool["vals"][i] += 16
        e.eng.dma_start(out=out, in_=in_, **kw).then_inc(sem, 16)
        ev = (key, pool["vals"][i], sem)
        self._record(ev, reads, writes)
        if is_output:
            self.out_events.append(ev)
        return ev

    def barrier(self):
        evs = []
        for name, e in self.E.items():
            if e.count > 0:
                evs.append((name, e.count, e.sem))
        for q, pool in self.dma_sems.items():
            for i, sem in enumerate(pool["sems"]):
                if pool["vals"][i] > 0:
                    evs.append((f"d_{q}{i}", pool["vals"][i], sem))
        for name, e in self.E.items():
            for ev in evs:
                if ev[0] != name:
                    self._wait(e, ev)

    def finish(self):
        e = self.E["sp"]
        for q, pool in self.dma_sems.items():
            for i, sem in enumerate(pool["sems"]):
                if pool["vals"][i] > 0:
                    self._wait(e, (f"d_{q}{i}", pool["vals"][i], sem))


def _tables_A(rel_bias):
    ki = np.arange(128)[:, None, None]
    jp = np.arange(6)[None, :, None]
    qc = np.arange(256)[None, None, :]
    bq = qc // 128
    qi = qc % 128
    j = jp - bq
    rel = 128 * (j - 4) + ki - qi
    idx = np.clip(rel, -128, 128) + 128
    dchunk = 2 * (j - 4) + (ki >= 64).astype(np.int64) - (qi >= 64).astype(np.int64)
    ok = (j >= 0) & (j <= 4) & (dchunk >= -8) & (dchunk <= 0)
    MA = np.where(ok, 0.0, NEG).astype(np.float32)
    GA = np.ascontiguousarray(np.transpose(rel_bias[:, idx], (1, 0, 2, 3))).astype(np.float32)
    return GA, np.ascontiguousarray(MA)


def _tables_B():
    slopes = 2.0 ** (-8.0 * np.arange(1, 5) / 4.0)
    ki = np.arange(128)[:, None]
    c = np.arange(512)[None, :]
    T = np.zeros((128, 4, 2, 512), np.float32)
    for h in range(4):
        b0 = -slopes[h] * np.abs(c - ki)
        b0 = np.where((ki >= 64) & (c < 64), NEG, b0)
        b1 = -slopes[h] * (128 + c - ki)
        T[:, h, 0] = b0
        T[:, h, 1] = b1
    return T, slopes


class Prog:
    def __init__(self, S, NSEQ, do_l0=True, do_l1=True):
        self.S = S
        self.NSEQ = NSEQ
        self.do_l0 = do_l0
        self.do_l1 = do_l1
        self.NTT = S // 128
        self.NTB = S // 512

    def build(self):
        S, NSEQ = self.S, self.NSEQ
        nc = bass.Bass("TRN2", target_bir_lowering=False)
        self.nc = nc
        T = S * NSEQ

        def din(name, shape, dt=F32):
            return nc.dram_tensor(name, list(shape), dt, kind="ExternalInput").ap()

        self.x = din("x", [T, D])
        self.out = nc.dram_tensor("out", [T, D], F32, kind="ExternalOutput").ap()
        self.ident = din("ident", [128, 128])
        if self.do_l0:
            self.w_in0 = din("w_in0", [D, 8192])
            self.w_out0 = din("w_out0", [D, D])
            self.g0 = din("g0", [128, 16])
            self.qkg = din("qkg", [128, 8])
            self.subg = din("subg", [128, 2])
            self.lamv = din("lamv", [128, 4])
            self.GA = din("GA", [128, 8, 6, 256])
            self.MA = din("MA", [128, 6, 256])
            self.TB = din("TB", [128, 4, 2, 512])
        if self.do_l1:
            self.w_in1 = din("w_in1", [D, 4096])
            self.w_out1 = din("w_out1", [D, D])
            self.g1 = din("g1", [128, 16])
            self.cvw_d = din("cvw", [128, 12, 4])
            self.cl_d = din("cl", [128, 4, 12])
            self.lru_wa_d = din("lru_wa", [6, 256, 256])
            self.lru_wx_d = din("lru_wx", [6, 256, 256])
            self.s5in_d = din("s5in", [128, 3, 32])
            self.sgn_d = din("sgn", [128, 1])
            self.Cst_d = din("Cst", [128, 32, 16])
            self.Csw_d = din("Csw", [128, 32, 16])
            self.W1_d = din("W1", [4, 128, 1024])
            self.W2_d = din("W2", [4, 128, 1024])
            self.dsk_d = din("dsk", [128, 8])
            self.wglu_d = din("wglu", [512, 512])
        if self.do_l0 and self.do_l1:
            self.x1 = nc.dram_tensor("x1", [T, D], F32, kind="Internal").ap()
        elif self.do_l0:
            self.x1 = self.out
        else:
            self.x1 = self.x
        self.mixs = nc.dram_tensor("mixs", [NSEQ, KT, 128, S], BF16, kind="Internal").ap()

        with ExitStack() as st:
            self.st = st
            self.block = st.enter_context(nc.Block())
            self.fw = FW(nc, st)
            self.cur = st
            self._alloc_common()
            self._load_consts()
            if self.do_l0:
                with ExitStack() as l0st:
                    self.cur = l0st
                    self._alloc_l0()
                    self._prep_l0()
                    for s in range(NSEQ):
                        self.layer0(s)
                    self.fw.barrier()
                self.cur = st
            if self.do_l1:
                with ExitStack() as l1st:
                    self.cur = l1st
                    self._alloc_l1()
                    self._prep_l1()
                    for s in range(NSEQ):
                        self.layer1(s)
                    self.fw.barrier()
                self.cur = st
            self.fw.finish()
        return nc

    def sb(self, name, shape, dt):
        self._uid = getattr(self, "_uid", 0) + 1
        return self.cur.enter_context(self.nc.sbuf_tensor(f"sb{self._uid}_{name}", list(shape), dt))

    def ps(self, name, shape, dt):
        return self.st.enter_context(self.nc.psum_tensor("ps_" + name, list(shape), dt))

    def _alloc_common(self):
        S = self.S
        self.mixs_b = [Buf(f"mixs{i}") for i in range(KT)]
        self.x1_b = {(s_, t_): Buf(f"x1_{s_}_{t_}") for s_ in range(self.NSEQ) for t_ in range(self.NTT)}
        self.big = self.sb("big", [128, KT, S], BF16)
        self.big_b = [Buf(f"big{i}") for i in range(self.NTB)]
        self.wraw = self.sb("wraw", [128, 4 * KT * 256], BF16)
        self.w_b = [Buf(f"w{i}") for i in range(4)]
        self.epsc = self.sb("epsc", [128, 1], F32)
        self.epsc_b = Buf("epsc")
        self.onec = self.sb("onec", [128, 1], F32)
        self.onec_b = Buf("onec")
        self.col = self.sb("col", [128, 8], F32)
        self.col_b = [Buf(f"col{i}") for i in range(8)]
        self.identb = self.sb("identb", [128, 128], BF16)
        self.identb_b = Buf("identb")
        self.ones = self.sb("ones", [128, 128], BF16)
        self.ones_b = Buf("ones")
        self.onesf = self.sb("onesf", [128, 128], F32)
        self.onesf_b = Buf("onesf")
        self.psS = [self.ps(f"psS{i}", [128, 512], F32) for i in range(2)]
        self.psS_b = [Buf(f"psS{i}", excl=True) for i in range(2)]
        self.psO = [self.ps(f"psO{i}", [128, 512], F32) for i in range(3)]
        self.psO_b = [Buf(f"psO{i}", excl=True) for i in range(3)]
        self.psP = [self.ps(f"psP{i}", [128, 512], F32) for i in range(2)]
        self.psP_b = [Buf(f"psP{i}", excl=True) for i in range(2)]
        self.psX = self.ps("psX", [128, 512], F32)
        self.psX_b = Buf("psX", excl=True)
        self.psT = self.psX[:, :].bitcast(BF16).rearrange("p (j c) -> p j c", c=128)
        self.psT_b = self.psX_b

    def wslot(self, i):
        return self.wraw[:, i * 4096:(i + 1) * 4096].rearrange("p (k c) -> p k c", c=256)

    def wbig(self, j):
        return self.wraw[:, j * 8192:(j + 1) * 8192].rearrange("p (k c) -> p k c", c=512)

    def _load_consts(self):
        fw = self.fw
        fw.dma("pool", self.identb[:], self.ident, writes=[self.identb_b])
        fw.op("dve", lambda e: e.memset(self.ones[:], 1.0), writes=[self.ones_b])
        fw.op("dve", lambda e: e.memset(self.onesf[:], 1.0), writes=[self.onesf_b])
        fw.op("dve", lambda e: e.memset(self.epsc[:], EPS), writes=[self.epsc_b])
        fw.op("dve", lambda e: e.memset(self.onec[:], 1.0), writes=[self.onec_b])

    def norm_phase(self, xsrc, g16, g16_b, src_bufs=None):
        fw, S = self.fw, self.S
        for tt in range(self.NTT):
            i = tt % 2
            xs, xsb = self.xst[i], self.xst_b[i]
            rd = [src_bufs[tt]] if src_bufs is not None else []
            fw.dma("sp", xs[:, 0:D], xsrc[tt * 128:(tt + 1) * 128, :], reads=rd, writes=[xsb])
            xn, xn_b = self.xn2[tt % 2]
            c3 = 3 * (tt % 2)
            fw.op("act", lambda e: e.activation(out=xn[:, 0:D], in_=xs[:, 0:D], func=AF.Square,
                                                accum_out=self.col[:, c3:c3 + 1]),
                  reads=[xsb], writes=[xn_b, self.col_b[c3]])
            fw.op("act", lambda e: e.activation(out=self.col[:, c3 + 1:c3 + 2], in_=self.col[:, c3:c3 + 1], func=AF.Sqrt,
                                                scale=1.0 / D, bias=self.epsc[:, 0:1]),
                  reads=[self.col_b[c3], self.epsc_b], writes=[self.col_b[c3 + 1]])
            fw.op("dve", lambda e: e.reciprocal(out=self.col[:, c3 + 2:c3 + 3], in_=self.col[:, c3 + 1:c3 + 2]),
                  reads=[self.col_b[c3 + 1]], writes=[self.col_b[c3 + 2]])
            fw.op("act", lambda e: e.activation(out=xn[:, 0:D], in_=xs[:, 0:D], func=AF.Copy,
                                                scale=self.col[:, c3 + 2:c3 + 3]),
                  reads=[xsb, self.col_b[c3 + 2]], writes=[xn_b])
            tb = tt // 4
            for half in range(2):
                for j in range(8):
                    kt = half * 8 + j
                    fw.op("pe", lambda e: e.transpose(out=self.psT[:, j, :], in_=xn[:, kt * 128:(kt + 1) * 128],
                                                      identity=self.identb[:]),
                          reads=[xn_b, self.identb_b], writes=[self.psT_b], inc=(j == 7))
                dst = self.big[:, half * 8:(half + 1) * 8, tt * 128:(tt + 1) * 128]
                gsl = g16[:, half * 8:(half + 1) * 8].unsqueeze(2).to_broadcast([128, 8, 128])
                fw.op("dve", lambda e: e.tensor_tensor(out=dst, in0=self.psT[:], in1=gsl, op=ALU.mult),
                      reads=[self.psT_b, g16_b], writes=[self.big_b[tb]])

    def load_w(self, slot, wsrc, col0):
        src = wsrc[:, col0:col0 + 256].rearrange("(k p) c -> p k c", p=128)
        return self.fw.dma("pool", self.wslot(slot), src, writes=[self.w_b[slot]])

    def proj_fm(self, ps, psb, slot, sel, tb):
        w = self.wslot(slot)
        for kt in range(KT):
            self.fw.op("pe", lambda e: e.matmul(ps[:], lhsT=w[:, kt, sel * 128:(sel + 1) * 128],
                                                rhs=self.big[:, kt, tb * 512:(tb + 1) * 512],
                                                start=(kt == 0), stop=(kt == KT - 1)),
                       reads=[self.w_b[slot], self.big_b[tb]], writes=[psb], inc=(kt == KT - 1))

    def outproj_phase(self, s, wsrc, xsrc, dst, is_output, src_bufs=None):
        fw, S = self.fw, self.S
        for ft in range(KT):
            fw.dma("sp", self.big[:, ft, :], self.mixs[s, ft], reads=[self.mixs_b[ft]],
                   writes=self.big_b)
        steps = [(ch, tt) for ch in range(4) for tt in range(self.NTT)]

        def load_x(k):
            ch, tt = steps[k]
            i = k % 2
            fw.dma("sp", self.xst[i][:, 0:512], xsrc[tt * 128:(tt + 1) * 128, ch * 512:(ch + 1) * 512],
                   reads=([src_bufs[tt]] if src_bufs is not None else []), writes=[self.xst_b[i]])

        load_x(0)
        for k, (ch, tt) in enumerate(steps):
            j = ch % 2
            if tt == 0:
                src = wsrc[:, ch * 512:(ch + 1) * 512].rearrange("(k p) c -> p k c", p=128)
                fw.dma("pool", self.wbig(j), src, writes=[self.w_b[2 * j], self.w_b[2 * j + 1]])
            w = self.wbig(j)
            i = k % 2
            xs, xsb = self.xst[i], self.xst_b[i]
            if k + 1 < len(steps):
                load_x(k + 1)
            pp, ppb = self.psP[i], self.psP_b[i]
            for kt in range(KT):
                fw.op("pe", lambda e: e.matmul(pp[:], lhsT=self.big[:, kt, tt * 128:(tt + 1) * 128],
                                               rhs=w[:, kt, :], start=(kt == 0), stop=(kt == KT - 1)),
                      reads=[self.big_b[tt // 4], self.w_b[2 * j], self.w_b[2 * j + 1]], writes=[ppb],
                      inc=(kt == KT - 1))
            fw.op("dve", lambda e: e.tensor_tensor(out=xs[:, 512:1024], in0=pp[:], in1=xs[:, 0:512], op=ALU.add),
                  reads=[ppb, xsb], writes=[xsb])
            fw.dma("sp", dst[tt * 128:(tt + 1) * 128, ch * 512:(ch + 1) * 512], xs[:, 512:1024],
                   reads=[xsb], writes=([] if is_output else [self.x1_b[(s, tt)]]), is_output=is_output)

    def _alloc_l0(self):
        S = self.S
        self.cbt = self.sb("cbt", [128, 64], F32)
        self.cb_b = Buf("cbt")
        self.xst = [self.sb(f"xst{i}", [128, D], F32) for i in range(2)]
        self.xst_b = [Buf(f"xst{i}") for i in range(2)]
        self.xn = self.sb("xn", [128, D], BF16)
        self.xn_b = Buf("xn")
        self.xnB = self.sb("xnB", [128, D], BF16)
        self.xn2 = [(self.xn, self.xn_b), (self.xnB, Buf("xnB"))]
        self.g16_0 = self.sb("g16_0", [128, 16], F32)
        self.g16_0_b = Buf("g16_0")
        self.qT = self.sb("qT", [128, 2, S], BF16)
        self.kT = self.sb("kT", [128, 2, S], BF16)
        self.vv = self.sb("vv", [128, self.NTT, 256], BF16)
        self.gT = self.sb("gT", [128, 2, S], BF16)
        self.qT_b, self.kT_b, self.vv_b, self.gT_b = Buf("qT"), Buf("kT"), Buf("vv"), Buf("gT")
        self.vv2 = self.sb("vv2", [128, self.NTT, 256], BF16)
        self.vv2_b = Buf("vv2")
        self.sqF = self.sb("sqF", [128, 512], BF16)
        self.sqF_b = Buf("sqF")
        v2 = lambda t: t[:, :].bitcast(BF16).rearrange("p (c s) -> p c s", c=2)[:, :, 0:S]
        self.qTs = [(self.qT, self.qT_b), (v2(self.xst[0]), self.xst_b[0])]
        self.kTs = [(self.kT, self.kT_b), (v2(self.xst[1]), self.xst_b[1])]
        self.vvs = [(self.vv, self.vv_b), (self.vv2, self.vv2_b)]
        self.gTs = [([self.gT[:, 0, :], self.gT[:, 1, :]], [self.gT_b, self.gT_b]),
                    ([self.xn2[0][0][:, 0:S], self.xn2[1][0][:, 0:S]], [self.xn2[0][1], self.xn2[1][1]])]
        self.filler = []
        self.sq = [self.sb(f"sq{i}", [128, 512], BF16) for i in range(2)]
        self.sq_b = [Buf(f"sq{i}") for i in range(2)]
        self.sd = [self.sb(f"sd{i}", [128, 512], F32) for i in range(2)]
        self.sd_b = [Buf(f"sd{i}") for i in range(2)]
        self.tmp = [self.sb(f"tmp{i}", [128, 512], F32) for i in range(2)]
        self.tmp_b = [Buf(f"tmp{i}") for i in range(2)]
        self.pt = [self.sb(f"pt{i}", [128, 512], BF16) for i in range(3)]
        self.pt_b = [Buf(f"pt{i}") for i in range(3)]
        self.biasA = self.sb("biasA", [128, 6, 256], F32)
        self.biasA_b = Buf("biasA")
        self.maskA = self.sb("maskA", [128, 6, 256], F32)
        self.maskA_b = Buf("maskA")
        self.tabB = self.sb("tabB", [128, 2, 512], F32)
        self.tabB_b = Buf("tabB")
        self.t0 = self.sb("t0", [128, 2, 512], F32)
        self.t0_b = Buf("t0")
        self.dd = self.sb("dd", [128, 2, 512], F32)
        self.dd_b = Buf("dd")
        self.rs = self.sb("rs", [128, 512], F32)
        self.rs_b = Buf("rs")
        self.yst = [self.sb(f"yst{i}", [128, 512], BF16) for i in range(2)]
        self.yst_b = [Buf(f"yst{i}") for i in range(2)]
        self.c0 = self.sb("c0", [128, 16], F32)
        self.c0_b = Buf("c0")
        self.lam4 = self.sb("lam4", [128, 4], F32)
        self.lam4_b = Buf("lam4")
        self.subgc = self.sb("subgc", [128, 2], F32)
        self.subgc_b = Buf("subgc")
        self.yctr = 0

    def _prep_l0(self):
        fw = self.fw
        fw.dma("sp", self.g16_0[:], self.g0, writes=[self.g16_0_b])
        fw.dma("sp", self.c0[:, 0:8], self.qkg, writes=[self.c0_b])
        fw.dma("sp", self.lam4[:], self.lamv, writes=[self.lam4_b])
        fw.dma("sp", self.subgc[:], self.subg, writes=[self.subgc_b])
        fw.dma("sp", self.maskA[:], self.MA, writes=[self.maskA_b])
        c0 = self.c0
        fw.op("dve", lambda e: e.tensor_tensor(out=c0[:, 9:10], in0=self.lam4[:, 0:1], in1=self.lam4[:, 1:2], op=ALU.mult),
              reads=[self.lam4_b], writes=[self.c0_b])
        fw.op("dve", lambda e: e.tensor_tensor(out=c0[:, 10:11], in0=self.lam4[:, 2:3], in1=self.lam4[:, 3:4], op=ALU.mult),
              reads=[self.lam4_b], writes=[self.c0_b])
        pp, ppb = self.psP[0], self.psP_b[0]
        fw.op("pe", lambda e: e.matmul(pp[:, 0:2], lhsT=self.onesf[:], rhs=c0[:, 9:11], start=True, stop=True),
              reads=[self.onesf_b, self.c0_b], writes=[ppb])
        fw.op("act", lambda e: e.activation(out=c0[:, 11:13], in_=pp[:, 0:2], func=AF.Exp),
              reads=[ppb], writes=[self.c0_b])
        fw.op("dve", lambda e: e.scalar_tensor_tensor(out=c0[:, 8:9], in0=c0[:, 12:13], scalar=-0.2, in1=c0[:, 11:12],
                                                      op0=ALU.add, op1=ALU.subtract),
              reads=[self.c0_b], writes=[self.c0_b])
        fw.op("dve", lambda e: e.tensor_scalar(out=self.subgc[:], in0=self.subgc[:], scalar1=0.8, scalar2=None,
                                               op0=ALU.mult),
              reads=[self.subgc_b], writes=[self.subgc_b])

    def run_deferred(self, keep=0):
        q = self.__dict__.setdefault("_defq", [])
        while len(q) > keep:
            q.pop(0)()

    def defer(self, fn):
        self.__dict__.setdefault("_defq", []).append(fn)

    def proj_bank(self):
        k = getattr(self, "_pbk", 0)
        self._pbk = k + 1
        banks = [(self.psP[0], self.psP_b[0]), (self.psP[1], self.psP_b[1]),
                 (self.psO[0], self.psO_b[0]), (self.psO[1], self.psO_b[1])]
        return banks[k % 4]

    def qk_unit(self, slot, sel, gcol, dstT, dst_b):
        fw = self.fw
        for tb in range(self.NTB):
            k = getattr(self, "_qkk", 0)
            self._qkk = k + 1
            i = k % 2
            pp, ppb = self.proj_bank()
            self.proj_fm(pp, ppb, slot, sel, tb)
            fw.op("act", lambda e: e.activation(out=self.sq[i][:], in_=pp[:], func=AF.Square),
                  reads=[ppb], writes=[self.sq_b[i]])
            self.run_deferred()

            def tail(i=i, pp=pp, ppb=ppb, tb=tb, sel=sel, gcol=gcol, dstT=dstT, dst_b=dst_b):
                pq, pqb = self.psS[i], self.psS_b[i]
                fw.op("pe", lambda e: e.matmul(pq[:], lhsT=self.ones[:], rhs=self.sq[i][:], start=True, stop=True),
                      reads=[self.ones_b, self.sq_b[i]], writes=[pqb])
                fw.op("act", lambda e: e.activation(out=self.sd[i][:], in_=pq[:], func=AF.Sqrt, scale=1.0 / 128,
                                                    bias=self.epsc[:, 0:1]),
                      reads=[pqb, self.epsc_b], writes=[self.sd_b[i]])
                fw.op("dve", lambda e: e.reciprocal(out=self.sd[i][:], in_=self.sd[i][:]),
                      reads=[self.sd_b[i]], writes=[self.sd_b[i]])
                fw.op("dve", lambda e: e.scalar_tensor_tensor(out=dstT[:, sel, tb * 512:(tb + 1) * 512], in0=pp[:],
                                                              scalar=self.c0[:, gcol:gcol + 1], in1=self.sd[i][:],
                                                              op0=ALU.mult, op1=ALU.mult),
                      reads=[ppb, self.c0_b, self.sd_b[i]], writes=[dst_b])
            self.defer(tail)

    def gate_unit(self, slot, sel):
        fw = self.fw
        for tb in range(self.NTB):
            pp, ppb = self.proj_bank()
            self.proj_fm(pp, ppb, slot, sel, tb)
            self.run_deferred()
            fw.op("act", lambda e: e.activation(out=self.gT[:, sel, tb * 512:(tb + 1) * 512], in_=pp[:], func=AF.Silu),
                  reads=[ppb], writes=[self.gT_b])

    def v_unit(self, slot):
        fw = self.fw
        w = self.wslot(slot)
        for tt in range(self.NTT):
            i = tt % 2
            pp, ppb = self.proj_bank()
            for kt in range(KT):
                fw.op("pe", lambda e: e.matmul(pp[:, 0:256], lhsT=self.big[:, kt, tt * 128:(tt + 1) * 128],
                                               rhs=w[:, kt, :], start=(kt == 0), stop=(kt == KT - 1)),
                      reads=[self.big_b[tt // 4], self.w_b[slot]], writes=[ppb], inc=(kt == KT - 1))
            self.run_deferred()
            if i == 0:
                fw.op("act", lambda e: e.activation(out=self.vv[:, tt, :], in_=pp[:, 0:256], func=AF.Copy),
                      reads=[ppb], writes=[self.vv_b])
            else:
                fw.op("dve", lambda e: e.tensor_copy(out=self.vv[:, tt, :], in_=pp[:, 0:256]),
                      reads=[ppb], writes=[self.vv_b])

    def spill_y(self, s, ft, col0, ncols, yi):
        self.fw.dma("sp", self.mixs[s, ft, :, col0:col0 + ncols], self.yst[yi][:, 0:ncols],
                    reads=[self.yst_b[yi]], writes=[self.mixs_b[ft]])

    def attn_A(self, s, pair):
        fw, S = self.fw, self.S
        scale = 128.0 ** -0.5
        for hh in range(2):
            h = 2 * pair + hh
            fw.dma("sp", self.biasA[:], self.GA[:, h], writes=[self.biasA_b])
            fw.op("dve", lambda e: e.tensor_tensor(out=self.biasA[:], in0=self.biasA[:], in1=self.maskA[:], op=ALU.add),
                  reads=[self.biasA_b, self.maskA_b], writes=[self.biasA_b])
            cnt = 0
            for Bq in range(S // 256):
                q0 = Bq * 256
                kts = [(jp, 2 * Bq - 4 + jp) for jp in range(6) if 2 * Bq - 4 + jp >= 0]
                ab = (Bq % 2)
                po, pob = (self.psO[0], self.psO_b[0]) if ab == 0 else (self.psO[2], self.psO_b[2])
                psm, psmb = (self.psO[1], self.psO_b[1]) if ab == 0 else (self.psP[0], self.psP_b[0])
                for n, (jp, kt) in enumerate(kts):
                    i = cnt % 3
                    cnt += 1
                    i2 = (cnt - 1) % 2
                    pS, pSb = self.sbank(i2)
                    fw.op("pe", lambda e: e.matmul(pS[:, 0:256], lhsT=self.ak[:, hh, kt * 128:(kt + 1) * 128],
                                                   rhs=self.aq[:, hh, q0:q0 + 256], start=True, stop=True),
                          reads=[self.ak_b, self.aq_b], writes=[pSb])
                    fw.op("dve", lambda e: e.scalar_tensor_tensor(out=self.tmp[i2][:, 0:256], in0=pS[:, 0:256], scalar=scale,
                                                                  in1=self.biasA[:, jp, :], op0=ALU.mult, op1=ALU.add),
                          reads=[pSb, self.biasA_b], writes=[self.tmp_b[i2]])
                    fw.op("act", lambda e: e.activation(out=self.pt[i][:, 0:256], in_=self.tmp[i2][:, 0:256], func=AF.Exp),
                          reads=[self.tmp_b[i2]], writes=[self.pt_b[i]])
                    self.run_deferred(keep=1)
                    self.fill_tick()
                    first, last = (n == 0), (n == len(kts) - 1)

                    def tail(i=i, kt=kt, first=first, last=last, po=po, pob=pob, psm=psm, psmb=psmb, q0=q0, hh=hh, h=h):
                        fw.op("pe", lambda e: e.matmul(po[:, 0:256], lhsT=self.av[:, kt, hh * 128:(hh + 1) * 128],
                                                       rhs=self.pt[i][:, 0:256], start=first, stop=last),
                              reads=[self.av_b, self.pt_b[i]], writes=[pob], inc=False)
                        fw.op("pe", lambda e: e.matmul(psm[:, 0:256], lhsT=self.ones[:], rhs=self.pt[i][:, 0:256],
                                                       start=first, stop=last),
                              reads=[self.ones_b, self.pt_b[i]], writes=[psmb])
                        if not last:
                            return
                        yi = self.yctr % 2
                        self.yctr += 1
                        fw.op("dve", lambda e: e.reciprocal(out=self.rs[:, 0:256], in_=psm[:, 0:256]),
                              reads=[psmb], writes=[self.rs_b])
                        fw.op("dve", lambda e: e.tensor_tensor(out=self.rs[:, 256:512], in0=po[:, 0:256], in1=self.rs[:, 0:256], op=ALU.mult),
                              reads=[pob, self.rs_b], writes=[self.rs_b])
                        fw.op("dve", lambda e: e.tensor_tensor(out=self.yst[yi][:, 0:256], in0=self.rs[:, 256:512],
                                                               in1=self.ag[hh][:, q0:q0 + 256], op=ALU.mult),
                              reads=[self.rs_b, self.ag_b[hh]], writes=[self.yst_b[yi]])
                        self.spill_y(s, h, q0, 256, yi)
                    self.defer(tail)
            self.run_deferred()

    def attn_B(self, s, h, slopes):
        fw, S = self.fw, self.S
        scale = 128.0 ** -0.5
        fw.dma("sp", self.tabB[:], self.TB[:, h], writes=[self.tabB_b])
        cnt = 0
        for Q in range(S // 512):
            q0 = Q * 512
            nkt = 4 * Q + 4
            for c in range(2):
                for kt in range(nkt):
                    i = cnt % 3
                    cnt += 1
                    i2 = (cnt - 1) % 2
                    pS, pSb = self.sbank(i2)
                    if kt < 4 * Q:
                        lo = 0
                        strip = self.tabB[:, 1, :]
                        cb = float(-slopes[h] * 128.0 * (4 * Q - kt - 1))
                    else:
                        lo = 128 * (kt - 4 * Q)
                        strip = self.tabB[:, 0, 0:512 - lo]
                        cb = 0.0
                    fw.op("pe", lambda e: e.matmul(pS[:, lo:512], lhsT=self.ak[:, c, kt * 128:(kt + 1) * 128],
                                                   rhs=self.aq[:, c, q0 + lo:q0 + 512], start=True, stop=True),
                          reads=[self.ak_b, self.aq_b], writes=[pSb])
                    fw.op("dve", lambda e: e.scalar_tensor_tensor(out=self.tmp[i2][:, lo:512], in0=pS[:, lo:512], scalar=scale,
                                                                  in1=strip, op0=ALU.mult, op1=ALU.add),
                          reads=[pSb, self.tabB_b], writes=[self.tmp_b[i2]])
                    if cb != 0.0:
                        cbi = self.cbias_col(cb)
                        fw.op("act", lambda e: e.activation(out=self.pt[i][:, lo:512], in_=self.tmp[i2][:, lo:512], func=AF.Exp,
                                                            bias=cbi),
                              reads=[self.tmp_b[i2], self.cb_b], writes=[self.pt_b[i]])
                    else:
                        fw.op("act", lambda e: e.activation(out=self.pt[i][:, lo:512], in_=self.tmp[i2][:, lo:512], func=AF.Exp),
                              reads=[self.tmp_b[i2]], writes=[self.pt_b[i]])
                    self.run_deferred(keep=1)
                    self.fill_tick()
                    first, last = (kt == 0), (kt == nkt - 1)

                    def tail(i=i, kt=kt, lo=lo, first=first, last=last, c=c, q0=q0, Q=Q):
                        for sl in range(2):
                            fw.op("pe", lambda e: e.matmul(self.psO[sl][:, lo:512], lhsT=self.av[:, kt, sl * 128:(sl + 1) * 128],
                                                           rhs=self.pt[i][:, lo:512], start=first, stop=last),
                                  reads=[self.av_b, self.pt_b[i]], writes=[self.psO_b[sl]], inc=False)
                        fw.op("pe", lambda e: e.matmul(self.psO[2][:, lo:512], lhsT=self.ones[:], rhs=self.pt[i][:, lo:512],
                                                       start=first, stop=last),
                              reads=[self.ones_b, self.pt_b[i]], writes=[self.psO_b[2]])
                        if last:
                            self.B_epilogue(s, h, c, q0)
                    self.defer(tail)
        self.run_deferred()

    def sbank(self, i):
        return [(self.psS[0], self.psS_b[0]), (self.psS[1], self.psS_b[1])][i]

    def fill_tick(self):
        self._ftick = getattr(self, "_ftick", 0) + 1
        if self.filler:
            self.filler.pop(0)()

    def fill_drain(self):
        while self.filler:
            self.filler.pop(0)()

    def set_attn_bufs(self, p):
        (self.aq, self.aq_b), (self.ak, self.ak_b) = self.qTs[p], self.kTs[p]
        (self.av, self.av_b) = self.vvs[p]
        self.ag, self.ag_b = self.gTs[p]

    def filler_steps(self, kind, idx, p, next_loads):
        fw = self.fw
        gq, gk = {"A": (0, 2), "B": (4, 6)}[kind]
        qd, qb = self.qTs[p]
        kd, kb = self.kTs[p]
        vd, vb = self.vvs[p]
        gd, gb = self.gTs[p]
        pp, ppb = self.psP[1], self.psP_b[1]
        px, pxb = self.psX, self.psX_b
        steps = []

        def qkA(slot, sel, tb):
            self.proj_fm(pp, ppb, slot, sel, tb)
            fw.op("act", lambda e: e.activation(out=self.sqF[:], in_=pp[:], func=AF.Square),
                  reads=[ppb], writes=[self.sqF_b])

        def qk(slot, sel, gcol, dstT, dst_b, tb):
            fw.op("pe", lambda e: e.matmul(px[:], lhsT=self.ones[:], rhs=self.sqF[:], start=True, stop=True),
                  reads=[self.ones_b, self.sqF_b], writes=[pxb])
            fw.op("act", lambda e: e.activation(out=self.sd[1][:], in_=px[:], func=AF.Sqrt, scale=1.0 / 128,
                                                bias=self.epsc[:, 0:1]),
                  reads=[pxb, self.epsc_b], writes=[self.sd_b[1]])
            fw.op("dve", lambda e: e.reciprocal(out=self.sd[1][:], in_=self.sd[1][:]),
                  reads=[self.sd_b[1]], writes=[self.sd_b[1]])
            fw.op("dve", lambda e: e.scalar_tensor_tensor(out=dstT[:, sel, tb * 512:(tb + 1) * 512], in0=pp[:],
                                                          scalar=self.c0[:, gcol:gcol + 1], in1=self.sd[1][:],
                                                          op0=ALU.mult, op1=ALU.mult),
                  reads=[ppb, self.c0_b, self.sd_b[1]], writes=[dst_b])

        def vstep(tt):
            w = self.wslot(2)
            for kt in range(KT):
                fw.op("pe", lambda e: e.matmul(pp[:, 0:256], lhsT=self.big[:, kt, tt * 128:(tt + 1) * 128],
                                               rhs=w[:, kt, :], start=(kt == 0), stop=(kt == KT - 1)),
                      reads=[self.big_b[tt // 4], self.w_b[2]], writes=[ppb], inc=(kt == KT - 1))
            fw.op("act", lambda e: e.activation(out=vd[:, tt, :], in_=pp[:, 0:256], func=AF.Copy),
                  reads=[ppb], writes=[vb])

        def gstep(sel, tb):
            self.proj_fm(pp, ppb, 3, sel, tb)
            fw.op("act", lambda e: e.activation(out=gd[sel][:, tb * 512:(tb + 1) * 512], in_=pp[:], func=AF.Silu),
                  reads=[ppb], writes=[gb[sel]])

        for sel in range(2):
            for tb in range(self.NTB):
                steps.append(lambda sel=sel, tb=tb: qkA(0, sel, tb))
                steps.append(lambda sel=sel, tb=tb: qk(0, sel, gq + sel, qd, qb, tb))
        steps.append(lambda: next_loads(0))
        for sel in range(2):
            for tb in range(self.NTB):
                steps.append(lambda sel=sel, tb=tb: qkA(1, sel, tb))
                steps.append(lambda sel=sel, tb=tb: qk(1, sel, gk + sel, kd, kb, tb))
        steps.append(lambda: next_loads(1))
        for tt in range(self.NTT):
            steps.append(lambda tt=tt: vstep(tt))
        steps.append(lambda: next_loads(2))
        for sel in range(2):
            for tb in range(self.NTB):
                steps.append(lambda sel=sel, tb=tb: gstep(sel, tb))
        steps.append(lambda: next_loads(3))
        return steps

    def cbias_col(self, cb):
        if not hasattr(self, "_cbcols"):
            self._cbcols = {}
        if cb not in self._cbcols:
            j = len(self._cbcols)
            assert j < self.cbt.shape[1]
            self.fw.op("pool", lambda e: e.memset(self.cbt[:, j:j + 1], cb), writes=[self.cb_b])
            self._cbcols[cb] = j
        j = self._cbcols[cb]
        return self.cbt[:, j:j + 1]

    def B_epilogue(self, s, h, c, q0):
        fw = self.fw
        fw.op("dve", lambda e: e.reciprocal(out=self.rs[:], in_=self.psO[2][:]),
              reads=[self.psO_b[2]], writes=[self.rs_b])
        for sl in range(2):
            if c == 0:
                fw.op("dve", lambda e: e.tensor_tensor(out=self.t0[:, sl, :], in0=self.psO[sl][:], in1=self.rs[:], op=ALU.mult),
                      reads=[self.psO_b[sl], self.rs_b], writes=[self.t0_b])
            else:
                fw.op("dve", lambda e: e.tensor_tensor(out=self.dd[:, sl, :], in0=self.psO[sl][:], in1=self.rs[:], op=ALU.mult),
                      reads=[self.psO_b[sl], self.rs_b], writes=[self.dd_b])
                fw.op("dve", lambda e: e.scalar_tensor_tensor(out=self.dd[:, sl, :], in0=self.dd[:, sl, :],
                                                              scalar=self.c0[:, 8:9], in1=self.t0[:, sl, :],
                                                              op0=ALU.mult, op1=ALU.add),
                      reads=[self.dd_b, self.c0_b, self.t0_b], writes=[self.dd_b])
        if c == 0:
            return
        pq, pqb = self.psP[0], self.psP_b[0]
        for sl in range(2):
            fw.op("act", lambda e: e.activation(out=self.sq[sl][:], in_=self.dd[:, sl, :], func=AF.Square),
                  reads=[self.dd_b], writes=[self.sq_b[sl]])
            fw.op("pe", lambda e: e.matmul(pq[:], lhsT=self.ones[:], rhs=self.sq[sl][:], start=(sl == 0), stop=(sl == 1)),
                  reads=[self.ones_b, self.sq_b[sl]], writes=[pqb], inc=(sl == 1))
        fw.op("act", lambda e: e.activation(out=self.sd[0][:], in_=pq[:], func=AF.Sqrt, scale=1.0 / 256,
                                            bias=self.epsc[:, 0:1]),
              reads=[pqb, self.epsc_b], writes=[self.sd_b[0]])
        fw.op("dve", lambda e: e.reciprocal(out=self.sd[0][:], in_=self.sd[0][:]),
              reads=[self.sd_b[0]], writes=[self.sd_b[0]])
        for sl in range(2):
            yi = self.yctr % 2
            self.yctr += 1
            fw.op("dve", lambda e: e.scalar_tensor_tensor(out=self.dd[:, sl, :], in0=self.dd[:, sl, :],
                                                          scalar=self.subgc[:, sl:sl + 1], in1=self.sd[0][:],
                                                          op0=ALU.mult, op1=ALU.mult),
                  reads=[self.dd_b, self.subgc_b, self.sd_b[0]], writes=[self.dd_b])
            fw.op("dve", lambda e: e.tensor_tensor(out=self.yst[yi][:], in0=self.dd[:, sl, :],
                                                   in1=self.ag[sl][:, q0:q0 + 512], op=ALU.mult),
                  reads=[self.dd_b, self.ag_b[sl]], writes=[self.yst_b[yi]])
            self.spill_y(s, 8 + 2 * h + sl, q0, 512, yi)

    def layer0(self, s):
        S = self.S
        xsrc = self.x[s * S:(s + 1) * S, :]
        self.norm_phase(xsrc, self.g16_0, self.g16_0_b)
        _, slopes = _tables_B()
        groups = [("A", i) for i in range(4)] + [("B", h) for h in range(4)]
        offs = {"A": (0, 1024, 2048, 3072), "B": (4096, 5120, 6144, 7168)}
        gcols = {"A": (0, 2), "B": (4, 6)}

        def issue_loads(g):
            kind, idx = groups[g]
            for u in range(4):
                self.load_w(u, self.w_in0, offs[kind][u] + idx * 256)

        def load_unit(g, u):
            if g < len(groups):
                kind, idx = groups[g]
                self.load_w(u, self.w_in0, offs[kind][u] + idx * 256)

        issue_loads(0)
        kind, idx = groups[0]
        gq, gk = gcols[kind]
        for sel in range(2):
            self.qk_unit(0, sel, gq + sel, self.qT, self.qT_b)
        load_unit(1, 0)
        for sel in range(2):
            self.qk_unit(1, sel, gk + sel, self.kT, self.kT_b)
        load_unit(1, 1)
        self.v_unit(2)
        load_unit(1, 2)
        for sel in range(2):
            self.gate_unit(3, sel)
        load_unit(1, 3)
        self.run_deferred()
        for g, (kind, idx) in enumerate(groups):
            p = g % 2
            if g + 1 < len(groups):
                k2, i2 = groups[g + 1]
                self.filler = self.filler_steps(k2, i2, (g + 1) % 2, lambda u, g=g: load_unit(g + 2, u))
            self.set_attn_bufs(p)
            if kind == "A":
                self.attn_A(s, idx)
            else:
                self.attn_B(s, idx, slopes)
            self.fill_drain()
        self.run_deferred()
        dst = self.x1[s * S:(s + 1) * S, :]
        self.outproj_phase(s, self.w_out0, xsrc, dst, is_output=(not self.do_l1))


    def _alloc_l1(self):
        S = self.S
        FWID = max(S, D) + 8
        self.F = [self.sb(f"F{i}", [128, FWID], F32) for i in range(6)]
        self.F_b = [Buf(f"F{i}") for i in range(6)]
        self.H = [self.sb(f"H{i}", [128, max(S, D)], BF16) for i in range(7)]
        self.H_b = [Buf(f"H{i}") for i in range(7)]
        self.xst = [self.F[0], self.F[1]]
        self.xst_b = [self.F_b[0], self.F_b[1]]
        self.xn = self.H[0]
        self.xn_b = self.H_b[0]
        self.xn2 = [(self.H[0], self.H_b[0]), (self.H[1], self.H_b[1])]
        self.t5 = [self.sb(f"t5_{i}", [128, 512], F32) for i in range(3)]
        self.t5_b = [Buf(f"t5_{i}") for i in range(3)]
        self.yst = [self.sb(f"yst{i}", [128, 512], BF16) for i in range(2)]
        self.yst_b = [Buf(f"yst{i}") for i in range(2)]
        self.yctr = 0
        self.g16_1 = self.sb("g16_1", [128, 16], F32)
        self.g16_1_b = Buf("g16_1")
        self.cvw = self.sb("cvw", [128, 12, 4], F32)
        self.cl = self.sb("cl", [128, 8, 12], F32)
        self.cl_b = Buf("cl")
        self.wax = self.sb("wax", [128, 2, 2, 256], BF16)
        self.wax_b = Buf("wax")
        self.s5in = self.sb("s5in", [128, 3, 32], F32)
        self.s5w = self.F[4][:, 0:768].rearrange("p (r g) -> p r g", g=32)
        self.s5_b = Buf("s5")
        self.sgn = self.sb("sgn", [128, 1], F32)
        self.csK = self.sb("csK", [128, 12, 32], F32)
        self.snK = self.sb("snK", [128, 12, 32], F32)
        self.Cst = self.F[2][:, 0:512].rearrange("p (g h) -> p g h", h=16)
        self.Csw = self.F[3][:, 0:512].rearrange("p (g h) -> p g h", h=16)
        self.absA = self.sb("absA", [128, 32], F32)
        self.LA = self.sb("LA", [128, 32, 16], F32)
        self.LB = self.sb("LB", [128, 32, 16], F32)
        self.LT = self.t5[2][:, :].rearrange("p (g h) -> p g h", h=16)
        self.LAp = self.sb("LAp", [128, 1152], BF16)
        self.LBp = self.sb("LBp", [128, 1152], BF16)
        self.Lp_b = Buf("Lp")
        self.W12 = self.sb("W12", [128, 2, 1024], BF16)
        self.W12_b = Buf("W12")
        self.dsk = self.sb("dsk", [128, 8], F32)
        self.dsk_b = Buf("dsk")
        self.wglu = self.sb("wglu", [128, 4, 512], BF16)
        self.wglu_b = Buf("wglu")

    def _prep_l1(self):
        fw = self.fw
        fw.dma("sp", self.g16_1[:], self.g1, writes=[self.g16_1_b])
        fw.dma("sp", self.cvw[:], self.cvw_d, writes=[self.cl_b])
        fw.dma("sp", self.cl[:, 0:4, :], self.cl_d, writes=[self.cl_b])
        fw.dma("sp", self.s5in[:], self.s5in_d, writes=[self.s5_b])
        fw.dma("sp", self.sgn[:], self.sgn_d, writes=[self.s5_b])
        fw.dma("sp", self.Cst, self.Cst_d, writes=[self.s5_b])
        fw.dma("sp", self.Csw, self.Csw_d, writes=[self.s5_b])
        fw.dma("sp", self.dsk[:], self.dsk_d, writes=[self.dsk_b])
        fw.dma("pool", self.wglu[:], self.wglu_d.rearrange("(k p) c -> p k c", p=128), writes=[self.wglu_b])
        fw.op("dve", lambda e: e.memset(self.LAp[:], 0.0), writes=[self.Lp_b])
        fw.op("dve", lambda e: e.memset(self.LBp[:], 0.0), writes=[self.Lp_b])
        cl, clb = self.cl, [self.cl_b]
        PI = math.pi

        def act(out, in_, func, b, scale=1.0, bias=None):
            kw = {}
            rd = list(b)
            if bias is not None:
                kw["bias"] = bias
                rd.append(self.onec_b)
            fw.op("act", lambda e: e.activation(out=out, in_=in_, func=func, scale=scale, **kw), reads=rd, writes=b)

        def ts(out, in0, s1, op0, b, s2=None, op1=None):
            if op1 is None:
                fw.op("dve", lambda e: e.tensor_scalar(out=out, in0=in0, scalar1=s1, scalar2=None, op0=op0), reads=b, writes=b)
            else:
                fw.op("dve", lambda e: e.tensor_scalar(out=out, in0=in0, scalar1=s1, scalar2=s2, op0=op0, op1=op1), reads=b, writes=b)

        def tt(out, in0, in1, op, b):
            fw.op("dve", lambda e: e.tensor_tensor(out=out, in0=in0, in1=in1, op=op), reads=b, writes=b)

        lam, z, w, cc, cc2, t6, t7 = cl[:, 3, :], cl[:, 4, :], cl[:, 5, :], cl[:, 4, :], cl[:, 5, :], cl[:, 6, :], cl[:, 7, :]
        act(t6, lam, AF.Exp, clb, scale=-1.0)
        ts(t7, t6, 1.0, ALU.add, clb)
        act(w, t7, AF.Ln, clb)
        ts(t7, t7, -1.0, ALU.add, clb, 1e-30, ALU.max)
        fw.op("dve", lambda e: e.reciprocal(out=t7, in_=t7), reads=clb, writes=clb)
        tt(t7, t7, t6, ALU.mult, clb)
        tt(t7, t7, w, ALU.mult, clb)
        ts(cc, t7, -8.0, ALU.mult, clb)
        ts(cc2, t7, -16.0, ALU.mult, clb)

        sw, sb_ = self.s5w, [self.s5_b]
        are, aim, ldt = self.s5in[:, 0, :], self.s5in[:, 1, :], self.s5in[:, 2, :]
        R = lambda i: sw[:, i, :]
        dt, th, ea, y1, y2, tq, sn, cs, nr, ni, den, cr, ci, kA1, nci, kB2, nrm = [R(i) for i in range(17)]
        act(dt, ldt, AF.Exp, sb_)
        tt(th, aim, dt, ALU.mult, sb_)
        tt(tq, are, dt, ALU.mult, sb_)
        act(ea, tq, AF.Exp, sb_)

        def reduce_to(dst, add):
            ts(dst, th, add, ALU.add, sb_)
            ts(R(17), dst, 0.0, ALU.add, sb_)
            for m in range(1, 12):
                ts(tq, R(17), (2 * m - 1) * PI, ALU.is_gt, sb_, -2.0 * PI, ALU.mult)
                tt(dst, dst, tq, ALU.add, sb_)
            ts(dst, dst, -3.141592, ALU.max, sb_, 3.141592, ALU.min)

        reduce_to(y1, 0.0)
        reduce_to(y2, PI / 2)
        act(sn, y1, AF.Sin, sb_)
        act(cs, y2, AF.Sin, sb_)
        tt(nr, ea, cs, ALU.mult, sb_)
        ts(nr, nr, -1.0, ALU.add, sb_)
        tt(ni, ea, sn, ALU.mult, sb_)
        tt(den, are, are, ALU.mult, sb_)
        tt(tq, aim, aim, ALU.mult, sb_)
        tt(den, den, tq, ALU.add, sb_)
        fw.op("dve", lambda e: e.reciprocal(out=den, in_=den), reads=sb_, writes=sb_)
        tt(cr, nr, are, ALU.mult, sb_)
        tt(tq, ni, aim, ALU.mult, sb_)
        tt(cr, cr, tq, ALU.add, sb_)
        tt(cr, cr, den, ALU.mult, sb_)
        tt(ci, ni, are, ALU.mult, sb_)
        tt(tq, nr, aim, ALU.mult, sb_)
        tt(ci, ci, tq, ALU.subtract, sb_)
        tt(ci, ci, den, ALU.mult, sb_)
        fw.op("dve", lambda e: e.tensor_scalar(out=kA1, in0=cr, scalar1=self.sgn[:, 0:1], scalar2=None, op0=ALU.mult), reads=sb_, writes=sb_)
        ts(nci, ci, -1.0, ALU.mult, sb_)
        ts(kB2, kA1, -1.0, ALU.mult, sb_)
        bc = lambda a: a.unsqueeze(2).to_broadcast([128, 32, 16])
        tt(self.LA[:], self.Cst, bc(kA1), ALU.mult, sb_)
        tt(self.LT, self.Csw, bc(nci), ALU.mult, sb_)
        tt(self.LA[:], self.LA[:], self.LT, ALU.add, sb_)
        tt(self.LB[:], self.Cst, bc(nci), ALU.mult, sb_)
        tt(self.LT, self.Csw, bc(kB2), ALU.mult, sb_)
        tt(self.LB[:], self.LB[:], self.LT, ALU.add, sb_)
        c0, s0 = self.csK[:, 0, :], self.snK[:, 0, :]
        fw.op("dve", lambda e: e.tensor_scalar(out=s0, in0=sn, scalar1=self.sgn[:, 0:1], scalar2=None, op0=ALU.mult), reads=sb_, writes=sb_)
        tt(nrm, cs, cs, ALU.mult, sb_)
        tt(tq, sn, sn, ALU.mult, sb_)
        tt(nrm, nrm, tq, ALU.add, sb_)
        act(nrm, nrm, AF.Sqrt, sb_)
        fw.op("dve", lambda e: e.reciprocal(out=nrm, in_=nrm), reads=sb_, writes=sb_)
        tt(c0, cs, nrm, ALU.mult, sb_)
        tt(s0, s0, nrm, ALU.mult, sb_)
        self.NLEV = int(round(math.log2(self.S)))
        for k in range(self.NLEV - 1):
            ck, sk = self.csK[:, k, :], self.snK[:, k, :]
            cn, sn_ = self.csK[:, k + 1, :], self.snK[:, k + 1, :]
            tt(cn, ck, ck, ALU.mult, sb_)
            tt(tq, sk, sk, ALU.mult, sb_)
            tt(cn, cn, tq, ALU.subtract, sb_)
            tt(sn_, ck, sk, ALU.mult, sb_)
            ts(sn_, sn_, 2.0, ALU.mult, sb_)
        ts(self.absA[:], ea, 0.0, ALU.add, sb_)
        fw.barrier()

    def mixer_C(self, s, n):
        fw, S = self.fw, self.S
        F, Fb, H, Hb = self.F, self.F_b, self.H, self.H_b
        cinp, cinp_b = F[0], Fb[0]
        U, U_b = [F[1], F[2]], [Fb[1], Fb[2]]
        r, r_b, ii, ii_b, a, a_b = F[3], Fb[3], F[4], Fb[4], F[5], Fb[5]
        UB, UB_b = [H[1], H[2]], [Hb[1], Hb[2]]
        yb, yb_b = H[3], Hb[3]
        cl, clb = self.cl, self.cl_b
        import os
        kc = int(os.environ.get("KC", "99"))
        if kc not in (-1, -4):
          fw.dma("pool", self.wax[:, 0], self.lru_wa_d[n].rearrange("(k p) j -> p k j", p=128), writes=[self.wax_b])
          fw.dma("pool", self.wax[:, 1], self.lru_wx_d[n].rearrange("(k p) j -> p k j", p=128), writes=[self.wax_b])
        if kc not in (-2, -4):
            fw.op("dve", lambda e: e.memset(cinp[:, 0:3], 0.0), writes=[cinp_b])
        if kc in (-3, -4):
            return
        for j in range(2):
            ct = 2 * n + j
            for tb in range(self.NTB):
                i = tb % 2
                pp, ppb = self.psP[i], self.psP_b[i]
                self.proj_fm(pp, ppb, 0, j, tb)
                fw.op("act", lambda e: e.activation(out=cinp[:, 3 + tb * 512:3 + (tb + 1) * 512], in_=pp[:], func=AF.Copy),
                      reads=[ppb], writes=[cinp_b])
            u = U[j]
            if kc < 1:
                continue
            fw.op("dve", lambda e: e.tensor_scalar(out=u[:, 0:S], in0=cinp[:, 0:S], scalar1=self.cvw[:, ct, 0:1],
                                                   scalar2=cl[:, 0, ct:ct + 1], op0=ALU.mult, op1=ALU.add),
                  reads=[cinp_b, clb], writes=[U_b[j]])
            if kc < 2:
                continue
            for tap in range(1, 4):
                fw.op("dve", lambda e: e.scalar_tensor_tensor(out=u[:, 0:S], in0=cinp[:, tap:tap + S],
                                                              scalar=self.cvw[:, ct, tap:tap + 1], in1=u[:, 0:S],
                                                              op0=ALU.mult, op1=ALU.add),
                      reads=[cinp_b, clb, U_b[j]], writes=[U_b[j]])
            fw.op("act", lambda e: e.activation(out=UB[j][:, 0:S], in_=u[:, 0:S], func=AF.Copy),
                  reads=[U_b[j]], writes=[UB_b[j]])
        if kc < 3:
            return
        for jo in range(2):
            ct = 2 * n + jo
            for gi, (dst, dst_b, brow) in enumerate(((r, r_b, 1), (ii, ii_b, 2))):
                for tb in range(self.NTB):
                    i = tb % 2
                    pp, ppb = self.psS[i], self.psS_b[i]
                    for k in range(2):
                        fw.op("pe", lambda e: e.matmul(pp[:], lhsT=self.wax[:, gi, k, jo * 128:(jo + 1) * 128],
                                                       rhs=UB[k][:, tb * 512:(tb + 1) * 512], start=(k == 0), stop=(k == 1)),
                              reads=[self.wax_b, UB_b[k]], writes=[ppb], inc=(k == 1))
                    fw.op("act", lambda e: e.activation(out=dst[:, tb * 512:(tb + 1) * 512], in_=pp[:], func=AF.Sigmoid,
                                                        bias=cl[:, brow, ct:ct + 1]),
                          reads=[ppb, clb], writes=[dst_b])
            if kc < 4:
                continue
            fw.op("act", lambda e: e.activation(out=a[:, 0:S], in_=r[:, 0:S], func=AF.Exp, scale=cl[:, 4, ct:ct + 1]),
                  reads=[r_b, clb], writes=[a_b])
            fw.op("act", lambda e: e.activation(out=r[:, 0:S], in_=r[:, 0:S], func=AF.Exp, scale=cl[:, 5, ct:ct + 1]),
                  reads=[r_b, clb], writes=[r_b])
            fw.op("act", lambda e: e.activation(out=r[:, 0:S], in_=r[:, 0:S], func=AF.Sqrt, scale=-1.0, bias=self.onec[:, 0:1]),
                  reads=[r_b, self.onec_b], writes=[r_b])
            if kc < 5:
                continue
            fw.op("dve", lambda e: e.tensor_tensor(out=ii[:, 0:S], in0=ii[:, 0:S], in1=U[jo][:, 0:S], op=ALU.mult),
                  reads=[ii_b, U_b[jo]], writes=[ii_b])
            fw.op("dve", lambda e: e.tensor_tensor(out=ii[:, 0:S], in0=ii[:, 0:S], in1=r[:, 0:S], op=ALU.mult),
                  reads=[ii_b, r_b], writes=[ii_b])
            fw.op("dve", lambda e: e.tensor_tensor_scan(out=r[:, 0:S], data0=a[:, 0:S], data1=ii[:, 0:S], initial=0.0,
                                                        op0=ALU.mult, op1=ALU.add),
                  reads=[a_b, ii_b, r_b], writes=[r_b])
            for tb in range(self.NTB):
                i = tb % 2
                pp, ppb = self.psP[i], self.psP_b[i]
                self.proj_fm(pp, ppb, 1, jo, tb)
                fw.op("act", lambda e: e.activation(out=a[:, tb * 512:(tb + 1) * 512], in_=pp[:], func=AF.Silu),
                      reads=[ppb], writes=[a_b])
            fw.op("dve", lambda e: e.tensor_tensor(out=yb[:, 0:S], in0=r[:, 0:S], in1=a[:, 0:S], op=ALU.mult),
                  reads=[r_b, a_b], writes=[yb_b])
            fw.dma("sp", self.mixs[s, ct], yb[:, 0:S], reads=[yb_b], writes=[self.mixs_b[ct]])

    def mixer_D_tile(self, s, ct):
        fw, S = self.fw, self.S
        F, Fb, H, Hb = self.F, self.F_b, self.H, self.H_b
        dinf, dinf_b = F[0], Fb[0]
        COS, COS_b, SIN, SIN_b = F[1], Fb[1], F[2], Fb[2]
        bu, bu_b, Rr, Rr_b = F[3], Fb[3], F[4], Fb[4]
        dinb, dinb_b = H[1], Hb[1]
        Zc, Zc_b, Zs, Zs_b = H[2], Hb[2], H[5], Hb[5]
        gel, gel_b = self.gelb[ct], self.gelb_b[ct]
        t5, t5b = self.t5, self.t5_b
        yacc = [self.psO[0], self.psO[1], self.psO[2], self.psS[1]]
        yacc_b = [self.psO_b[0], self.psO_b[1], self.psO_b[2], self.psS_b[1]]
        import os
        for tb in range(self.NTB):
            i = tb % 2
            pp, ppb = self.psP[i], self.psP_b[i]
            ke = os.environ.get("KE", "")
            self.proj_fm(pp, ppb, 0 if "s" in ke else 2, ct % 2, tb)
            if "c" not in ke:
                fw.op("act", lambda e: e.activation(out=dinf[:, tb * 512:(tb + 1) * 512], in_=pp[:], func=AF.Copy),
                      reads=[ppb], writes=[dinf_b])
            if "d" not in ke:
                fw.op("dve", lambda e: e.tensor_copy(out=dinb[:, tb * 512:(tb + 1) * 512], in_=pp[:]),
                      reads=[ppb], writes=[dinb_b])
        import os
        kd = int(os.environ.get("KD", "99"))
        if kd < 1:
            return
        fw.dma("pool", self.W12[:, 0, :], self.W1_d[ct], writes=[self.W12_b])
        fw.dma("pool", self.W12[:, 1, :], self.W2_d[ct], writes=[self.W12_b])
        if kd < 2:
            return
        diagA = self.LAp[:, 0:8 * 144].rearrange("p (g c) -> p g c", c=144)[:, :, 0:16]
        diagB = self.LBp[:, 0:8 * 144].rearrange("p (g c) -> p g c", c=144)[:, :, 0:16]
        fw.op("dve", lambda e: e.tensor_copy(out=diagA, in_=self.LA[:, 8 * ct:8 * ct + 8, :]),
              reads=[self.s5_b], writes=[self.Lp_b])
        fw.op("dve", lambda e: e.tensor_copy(out=diagB, in_=self.LB[:, 8 * ct:8 * ct + 8, :]),
              reads=[self.s5_b], writes=[self.Lp_b])
        if kd < 3:
            return
        for gl in range(8):
            g = 8 * ct + gl
            fw.op("dve", lambda e: e.memset(COS[:, 0:1], 1.0), writes=[COS_b])
            fw.op("dve", lambda e: e.memset(SIN[:, 0:1], 0.0), writes=[SIN_b])
            for k in range(self.NLEV):
                nn = 1 << k
                ck = self.csK[:, k, g:g + 1]
                sk = self.snK[:, k, g:g + 1]
                fw.op("dve", lambda e: e.tensor_scalar(out=Rr[:, 0:nn], in0=SIN[:, 0:nn], scalar1=sk, scalar2=None, op0=ALU.mult),
                      reads=[SIN_b, self.s5_b], writes=[Rr_b])
                fw.op("dve", lambda e: e.tensor_scalar(out=bu[:, 0:nn], in0=COS[:, 0:nn], scalar1=sk, scalar2=None, op0=ALU.mult),
                      reads=[COS_b, self.s5_b], writes=[bu_b])
                fw.op("dve", lambda e: e.scalar_tensor_tensor(out=COS[:, nn:2 * nn], in0=COS[:, 0:nn], scalar=ck, in1=Rr[:, 0:nn],
                                                              op0=ALU.mult, op1=ALU.subtract),
                      reads=[COS_b, Rr_b, self.s5_b], writes=[COS_b])
                fw.op("dve", lambda e: e.scalar_tensor_tensor(out=SIN[:, nn:2 * nn], in0=SIN[:, 0:nn], scalar=ck, in1=bu[:, 0:nn],
                                                              op0=ALU.mult, op1=ALU.add),
                      reads=[SIN_b, bu_b, self.s5_b], writes=[SIN_b])
            if kd < 4:
                continue
            for tb in range(self.NTB):
                sl = slice(tb * 512, (tb + 1) * 512)
                p1, p1b, p2, p2b = self.psP[0], self.psP_b[0], self.psP[1], self.psP_b[1]
                fw.op("pe", lambda e: e.matmul(p1[:], lhsT=self.W12[:, 0, gl * 128:(gl + 1) * 128], rhs=dinb[:, sl], start=True, stop=True),
                      reads=[self.W12_b, dinb_b], writes=[p1b])
                fw.op("pe", lambda e: e.matmul(p2[:], lhsT=self.W12[:, 1, gl * 128:(gl + 1) * 128], rhs=dinb[:, sl], start=True, stop=True),
                      reads=[self.W12_b, dinb_b], writes=[p2b])
                fw.op("dve", lambda e: e.tensor_tensor(out=bu[:, sl], in0=p1[:], in1=COS[:, sl], op=ALU.mult),
                      reads=[p1b, COS_b], writes=[bu_b])
                fw.op("dve", lambda e: e.tensor_tensor(out=t5[0][:], in0=p2[:], in1=SIN[:, sl], op=ALU.mult),
                      reads=[p2b, SIN_b], writes=[t5b[0]])
                fw.op("dve", lambda e: e.tensor_tensor(out=bu[:, sl], in0=bu[:, sl], in1=t5[0][:], op=ALU.add),
                      reads=[bu_b, t5b[0]], writes=[bu_b])
            if kd < 5:
                continue
            fw.op("dve", lambda e: e.tensor_tensor_scan(out=Rr[:, 0:S], data0=self.absA[:, g:g + 1].to_broadcast([128, S]),
                                                        data1=bu[:, 0:S], initial=0.0, op0=ALU.mult, op1=ALU.add),
                  reads=[self.s5_b, bu_b, Rr_b], writes=[Rr_b])
            fw.op("dve", lambda e: e.tensor_tensor(out=Zc[:, 0:S], in0=Rr[:, 0:S], in1=COS[:, 0:S], op=ALU.mult),
                  reads=[Rr_b, COS_b], writes=[Zc_b])
            fw.op("dve", lambda e: e.tensor_tensor(out=Zs[:, 0:S], in0=Rr[:, 0:S], in1=SIN[:, 0:S], op=ALU.mult),
                  reads=[Rr_b, SIN_b], writes=[Zs_b])
            if kd < 6:
                continue
            for tb in range(self.NTB):
                sl = slice(tb * 512, (tb + 1) * 512)
                fw.op("pe", lambda e: e.matmul(yacc[tb][:], lhsT=self.LAp[:, gl * 128:(gl + 1) * 128], rhs=Zc[:, sl],
                                               start=(gl == 0), stop=False),
                      reads=[self.Lp_b, Zc_b], writes=[yacc_b[tb]], inc=False)
                fw.op("pe", lambda e: e.matmul(yacc[tb][:], lhsT=self.LBp[:, gl * 128:(gl + 1) * 128], rhs=Zs[:, sl],
                                               start=False, stop=(gl == 7)),
                      reads=[self.Lp_b, Zs_b], writes=[yacc_b[tb]])
        if kd < 7:
            return
        for tb in range(self.NTB):
            sl = slice(tb * 512, (tb + 1) * 512)
            fw.op("dve", lambda e: e.scalar_tensor_tensor(out=t5[1][:], in0=dinf[:, sl], scalar=self.dsk[:, ct:ct + 1],
                                                          in1=yacc[tb][:], op0=ALU.mult, op1=ALU.add),
                  reads=[dinf_b, self.dsk_b, yacc_b[tb]], writes=[t5b[1]])
            fw.op("act", lambda e: e.activation(out=gel[:, sl], in_=t5[1][:], func=AF.Gelu_apprx_tanh),
                  reads=[t5b[1]], writes=[gel_b])

    def mixer_D_glu(self, s):
        fw, S = self.fw, self.S
        t5, t5b = self.t5, self.t5_b
        for jo in range(4):
            for tb in range(self.NTB):
                sl = slice(tb * 512, (tb + 1) * 512)
                pp, ppb = self.psS[0], self.psS_b[0]
                for k in range(4):
                    fw.op("pe", lambda e: e.matmul(pp[:], lhsT=self.wglu[:, k, jo * 128:(jo + 1) * 128], rhs=self.gelb[k][:, sl],
                                                   start=(k == 0), stop=(k == 3)),
                          reads=[self.wglu_b, self.gelb_b[k]], writes=[ppb], inc=(k == 3))
                fw.op("act", lambda e: e.activation(out=t5[0][:], in_=pp[:], func=AF.Sigmoid, bias=self.dsk[:, 4 + jo:5 + jo]),
                      reads=[ppb, self.dsk_b], writes=[t5b[0]])
                pg, pgb = self.psP[tb % 2], self.psP_b[tb % 2]
                self.proj_fm(pg, pgb, jo // 2, jo % 2, tb)
                fw.op("act", lambda e: e.activation(out=t5[1][:], in_=pg[:], func=AF.Silu),
                      reads=[pgb], writes=[t5b[1]])
                fw.op("dve", lambda e: e.tensor_tensor(out=t5[2][:], in0=t5[0][:], in1=self.gelb[jo][:, sl], op=ALU.mult),
                      reads=[t5b[0], self.gelb_b[jo]], writes=[t5b[2]])
                yi = self.yctr % 2
                self.yctr += 1
                fw.op("dve", lambda e: e.tensor_tensor(out=self.yst[yi][:], in0=t5[2][:], in1=t5[1][:], op=ALU.mult),
                      reads=[t5b[2], t5b[1]], writes=[self.yst_b[yi]])
                self.spill_y(s, 12 + jo, tb * 512, 512, yi)

    def layer1(self, s):
        import os
        dbg = int(os.environ.get("KDBG", "9"))
        S = self.S
        xsrc = self.x1[s * S:(s + 1) * S, :]
        srcb = [self.x1_b[(s, tt)] for tt in range(self.NTT)] if self.do_l0 else None
        if dbg < 1:
            return
        self.norm_phase(xsrc, self.g16_1, self.g16_1_b, src_bufs=srcb)
        if dbg < 2:
            return
        self.gelb = [self.H[3], self.H[4], self.H[6], self.H[0]]
        self.gelb_b = [self.H_b[3], self.H_b[4], self.H_b[6], self.H_b[0]]
        if "L" not in os.environ.get("KSKIP", ""):
            self.load_w(0, self.w_in1, 0)
            self.load_w(1, self.w_in1, 1536)
        if "M" in os.environ.get("KSKIP", ""):
            return
        for n in range(6):
            if n + 1 < 6:
                self.load_w(2, self.w_in1, (n + 1) * 256) if False else None
            self.mixer_C(s, n)
            if n + 1 < 6:
                self.load_w(0, self.w_in1, (n + 1) * 256)
                self.load_w(1, self.w_in1, 1536 + (n + 1) * 256)
        if dbg < 3:
            return
        for ct in range(int(os.environ.get("KN", "4"))):
            ke = os.environ.get("KE", "")
            if ct % 2 == 0 and "n" not in ke:
                self.load_w(2, self.w_in1, 3072 + (ct // 2) * 256)
            if "a" in ke:
                continue
            self.mixer_D_tile(s, ct)
        if dbg < 4:
            return
        self.load_w(0, self.w_in1, 3584)
        self.load_w(1, self.w_in1, 3584 + 256)
        self.mixer_D_glu(s)
        if dbg < 5:
            return
        dst = self.out[s * S:(s + 1) * S, :]
        self.outproj_phase(s, self.w_out1, xsrc, dst, is_output=True, src_bufs=srcb)


def _prep_inputs_l0(inp):
    rel = np.asarray(inp["a_rel_bias"][0], np.float32)
    GA, MA = _tables_A(rel)
    TB, _ = _tables_B()
    aq = np.asarray(inp["a_q_g"][0], np.float32)
    ak = np.asarray(inp["a_k_g"][0], np.float32)
    bq = np.asarray(inp["b_q_g"][0], np.float32)
    bk = np.asarray(inp["b_k_g"][0], np.float32)
    qkg = np.stack([aq, aq, ak, ak, bq[0], bq[1], bk[0], bk[1]], axis=1)
    lamv = np.stack([inp["b_lam_q1"][0], inp["b_lam_k1"][0], inp["b_lam_q2"][0], inp["b_lam_k2"][0]], axis=1)
    subg = np.asarray(inp["b_subln_g"][0], np.float32).reshape(2, 128).T
    return {
        "w_in0": np.ascontiguousarray(inp["attn_w_in"][0], np.float32),
        "w_out0": np.ascontiguousarray(inp["attn_w_out"][0], np.float32),
        "g0": np.ascontiguousarray(np.asarray(inp["attn_norm_g"][0], np.float32).reshape(16, 128).T),
        "qkg": np.ascontiguousarray(qkg, np.float32),
        "subg": np.ascontiguousarray(subg, np.float32),
        "lamv": np.ascontiguousarray(lamv, np.float32),
        "GA": GA, "MA": MA, "TB": TB,
    }


def _prep_inputs_l1(inp):
    f = lambda k: np.asarray(inp[k][0], np.float32)
    cvw = f("lru_conv_w").reshape(4, 12, 128).transpose(2, 1, 0)
    t12 = lambda v: v.reshape(12, 128).T
    cl = np.stack([t12(f("lru_conv_b")), t12(f("lru_b_a").reshape(-1)), t12(f("lru_b_x").reshape(-1)),
                   t12(f("lru_lambda"))], axis=1)
    dup = lambda a: np.concatenate([a.T, a.T], axis=0)
    s5in = np.stack([dup(f("ssm_a_re")), dup(f("ssm_a_im")),
                     np.broadcast_to(f("ssm_log_dt")[None, :], (128, 32))], axis=1)
    sgn = np.concatenate([np.ones(64), -np.ones(64)]).astype(np.float32).reshape(128, 1)
    cre = f("ssm_c_re").transpose(2, 0, 1)
    cim = f("ssm_c_im").transpose(2, 0, 1)
    Cst = np.concatenate([cre, cim], axis=0)
    Csw = np.concatenate([cim, cre], axis=0)
    bre = f("ssm_b_re").transpose(0, 2, 1)
    bim = f("ssm_b_im").transpose(0, 2, 1)
    W1 = np.zeros((4, 128, 8, 128), np.float32)
    W2 = np.zeros((4, 128, 8, 128), np.float32)
    for g in range(32):
        ct, gl = g // 8, g % 8
        W1[ct, 16 * gl:16 * gl + 16, gl, 0:64] = bre[g]
        W1[ct, 16 * gl:16 * gl + 16, gl, 64:128] = bim[g]
        W2[ct, 16 * gl:16 * gl + 16, gl, 0:64] = bim[g]
        W2[ct, 16 * gl:16 * gl + 16, gl, 64:128] = bre[g]
    dsk = np.concatenate([f("ssm_d").reshape(4, 128).T, f("ssm_b_glu").reshape(4, 128).T], axis=1)
    c = np.ascontiguousarray
    return {
        "w_in1": c(f("rec_w_in")), "w_out1": c(f("rec_w_out")),
        "g1": c(f("rec_norm_g").reshape(16, 128).T),
        "cvw": c(cvw), "cl": c(cl), "lru_wa": c(f("lru_w_a")), "lru_wx": c(f("lru_w_x")),
        "s5in": c(s5in), "sgn": sgn, "Cst": c(Cst), "Csw": c(Csw),
        "W1": c(W1.reshape(4, 128, 1024)), "W2": c(W2.reshape(4, 128, 1024)),
        "dsk": c(dsk), "wglu": c(f("ssm_w_glu")),
    }


def run(inp, S, NSEQ, n_cores, do_l0=True, do_l1=True):
    prog = Prog(S, NSEQ, do_l0, do_l1)
    nc = prog.build()
    x = np.asarray(inp["x"], np.float32)
    shared = {"ident": np.eye(128, dtype=np.float32)}
    if do_l0:
        shared.update(_prep_inputs_l0(inp))
    if do_l1:
        shared.update(_prep_inputs_l1(inp))
    in_maps = []
    for c in range(n_cores):
        m = dict(shared)
        m["x"] = np.ascontiguousarray(x[c * NSEQ:(c + 1) * NSEQ].reshape(NSEQ * S, D))
        in_maps.append(m)
    res = run_bass_kernel_spmd(nc, in_maps, core_ids=list(range(n_cores)))
    outs = [np.asarray(r["out"]).reshape(NSEQ, S, D) for r in res.results]
    return np.concatenate(outs, axis=0).astype(np.float32)


def kernel(**inputs):
    return run(inputs, SEQ, BATCH // N_CORES, N_CORES)
```

```python
import math
import numpy as np
from contextlib import ExitStack
import concourse.bass as bass
import concourse.mybir as mybir
from concourse.bass_utils import run_bass_kernel_spmd

F32 = mybir.dt.float32
BF16 = mybir.dt.bfloat16
AF = mybir.ActivationFunctionType
ALU = mybir.AluOpType

D = 2048
KT = 16
EPS = 1e-6
NEG = -30000.0
N_CORES = 8
SEQ = 2048
BATCH = 16


class Buf:
    __slots__ = ("name", "w", "r", "excl")

    def __init__(self, name="", excl=False):
        self.name = name
        self.w = None
        self.r = {}
        self.excl = excl


class Eng:
    def __init__(self, name, eng):
        self.name = name
        self.eng = eng
        self.sem = None
        self.count = 0
        self.seen = {}
        self.pend_r = []
        self.pend_w = []


class FW:
    def __init__(self, nc, stack, n_dma_sems=12):
        self.nc = nc
        self.E = {}
        for name, eng in (("pe", nc.tensor), ("act", nc.scalar), ("dve", nc.vector),
                          ("pool", nc.gpsimd), ("sp", nc.sync)):
            e = Eng(name, eng)
            e.sem = stack.enter_context(nc.semaphore("s_" + name))
            self.E[name] = e
        self.dma_sems = {}
        for q, nq in (("sp", n_dma_sems), ("pool", 4)):
            sems = [stack.enter_context(nc.semaphore(f"d_{q}{i}")) for i in range(nq)]
            self.dma_sems[q] = {"sems": sems, "vals": [0] * nq, "next": 0}
        self.same_eng_sync = True
        self.out_events = []

    def _wait(self, e, ev, raw=True):
        if ev is None:
            return
        key, val, sem = ev
        if key == e.name and (not raw or e.name == "pe" or not self.same_eng_sync):
            return
        if e.seen.get(key, 0) >= val:
            return
        e.eng.wait_ge(sem, val)
        e.seen[key] = val

    def _deps(self, e, reads, writes, extra):
        for b in reads:
            self._wait(e, b.w)
            if b.excl:
                for ev in b.r.values():
                    if ev[0] != e.name:
                        self._wait(e, ev)
        for b in writes:
            self._wait(e, b.w, raw=False)
            for ev in b.r.values():
                self._wait(e, ev, raw=False)
        for ev in extra:
            self._wait(e, ev)

    def _record(self, ev, reads, writes):
        for b in reads:
            old = b.r.get(ev[0])
            if old is None or old[1] < ev[1]:
                b.r[ev[0]] = ev
        for b in writes:
            b.w = ev
            b.r = {}

    def op(self, engname, fn, reads=(), writes=(), extra=(), inc=True):
        e = self.E[engname]
        self._deps(e, reads, writes, extra)
        ins = fn(e.eng)
        if inc:
            e.count += 1
            ins.then_inc(e.sem, 1)
            ev = (e.name, e.count, e.sem)
            self._record(ev, list(reads) + e.pend_r, list(writes) + e.pend_w)
            e.pend_r = []
            e.pend_w = []
            return ev
        e.pend_r.extend(reads)
        e.pend_w.extend(writes)
        return None

    def dma(self, q, out, in_, reads=(), writes=(), extra=(), is_output=False, **kw):
        e = self.E[q]
        pool = self.dma_sems[q]
        i = pool["next"]
        pool["next"] = (i + 1) % len(pool["sems"])
        sem = pool["sems"][i]
        key = f"d_{q}{i}"
        if pool["vals"][i] > 0:
            self._wait(e, (key, pool["vals"][i], sem))
        self._deps(e, reads, writes, extra)
        pool["vals"][i] += 16
        e.eng.dma_start(out=out, in_=in_, **kw).then_inc(sem, 16)
        ev = (key, pool["vals"][i], sem)
        self._record(ev, reads, writes)
        if is_output:
            self.out_events.append(ev)
        return ev

    def barrier(self):
        evs = []
        for name, e in self.E.items():
            if e.count > 0:
                evs.append((name, e.count, e.sem))
        for q, pool in self.dma_sems.items():
            for i, sem in enumerate(pool["sems"]):
                if pool["vals"][i] > 0:
                    evs.append((f"d_{q}{i}", pool["vals"][i], sem))
        for name, e in self.E.items():
            for ev in evs:
                if ev[0] != name:
                    self._wait(e, ev)

    def finish(self):
        e = self.E["sp"]
        for q, pool in self.dma_sems.items():
            for i, sem in enumerate(pool["sems"]):
                if pool["vals"][i] > 0:
                    self._wait(e, (f"d_{q}{i}", pool["vals"][i], sem))


def _tables_A(rel_bias):
    ki = np.arange(128)[:, None, None]
    jp = np.arange(6)[None, :, None]
    qc = np.arange(256)[None, None, :]
    bq = qc // 128
    qi = qc % 128
    j = jp - bq
    rel = 128 * (j - 4) + ki - qi
    idx = np.clip(rel, -128, 128) + 128
    dchunk = 2 * (j - 4) + (ki >= 64).astype(np.int64) - (qi >= 64).astype(np.int64)
    ok = (j >= 0) & (j <= 4) & (dchunk >= -8) & (dchunk <= 0)
    MA = np.where(ok, 0.0, NEG).astype(np.float32)
    GA = np.ascontiguousarray(np.transpose(rel_bias[:, idx], (1, 0, 2, 3))).astype(np.float32)
    return GA, np.ascontiguousarray(MA)


def _tables_B():
    slopes = 2.0 ** (-8.0 * np.arange(1, 5) / 4.0)
    ki = np.arange(128)[:, None]
    c = np.arange(512)[None, :]
    T = np.zeros((128, 4, 2, 512), np.float32)
    for h in range(4):
        b0 = -slopes[h] * np.abs(c - ki)
        b0 = np.where((ki >= 64) & (c < 64), NEG, b0)
        b1 = -slopes[h] * (128 + c - ki)
        T[:, h, 0] = b0
        T[:, h, 1] = b1
    return T, slopes


class Prog:
    def __init__(self, S, NSEQ, do_l0=True, do_l1=True):
        self.S = S
        self.NSEQ = NSEQ
        self.do_l0 = do_l0
        self.do_l1 = do_l1
        self.NTT = S // 128
        self.NTB = S // 512

    def build(self):
        S, NSEQ = self.S, self.NSEQ
        nc = bass.Bass("TRN2", target_bir_lowering=False)
        self.nc = nc
        T = S * NSEQ

        def din(name, shape, dt=F32):
            return nc.dram_tensor(name, list(shape), dt, kind="ExternalInput").ap()

        self.x = din("x", [T, D])
        self.out = nc.dram_tensor("out", [T, D], F32, kind="ExternalOutput").ap()
        self.ident = din("ident", [128, 128])
        if self.do_l0:
            self.w_in0 = din("w_in0", [D, 8192])
            self.w_out0 = din("w_out0", [D, D])
            self.g0 = din("g0", [128, 16])
            self.qkg = din("qkg", [128, 8])
            self.subg = din("subg", [128, 2])
            self.lamv = din("lamv", [128, 4])
            self.GA = din("GA", [128, 8, 6, 256])
            self.MA = din("MA", [128, 6, 256])
            self.TB = din("TB", [128, 4, 2, 512])
        if self.do_l1:
            self.w_in1 = din("w_in1", [D, 4096])
            self.w_out1 = din("w_out1", [D, D])
            self.g1 = din("g1", [128, 16])
            self.cvw_d = din("cvw", [128, 12, 4])
            self.cl_d = din("cl", [128, 4, 12])
            self.lru_wa_d = din("lru_wa", [6, 256, 256])
            self.lru_wx_d = din("lru_wx", [6, 256, 256])
            self.s5in_d = din("s5in", [128, 3, 32])
            self.sgn_d = din("sgn", [128, 1])
            self.Cst_d = din("Cst", [128, 32, 16])
            self.Csw_d = din("Csw", [128, 32, 16])
            self.W1_d = din("W1", [4, 128, 1024])
            self.W2_d = din("W2", [4, 128, 1024])
            self.dsk_d = din("dsk", [128, 8])
            self.wglu_d = din("wglu", [512, 512])
        if self.do_l0 and self.do_l1:
            self.x1 = nc.dram_tensor("x1", [T, D], F32, kind="Internal").ap()
        elif self.do_l0:
            self.x1 = self.out
        else:
            self.x1 = self.x
        self.mixs = nc.dram_tensor("mixs", [NSEQ, KT, 128, S], BF16, kind="Internal").ap()

        with ExitStack() as st:
            self.st = st
            self.block = st.enter_context(nc.Block())
            self.fw = FW(nc, st)
            self.cur = st
            self._alloc_common()
            self._load_consts()
            if self.do_l0:
                with ExitStack() as l0st:
                    self.cur = l0st
                    self._alloc_l0()
                    self._prep_l0()
                    for s in range(NSEQ):
                        self.layer0(s)
                    self.fw.barrier()
                self.cur = st
            if self.do_l1:
                with ExitStack() as l1st:
                    self.cur = l1st
                    self._alloc_l1()
                    self._prep_l1()
                    for s in range(NSEQ):
                        self.layer1(s)
                    self.fw.barrier()
                self.cur = st
            self.fw.finish()
        return nc

    def sb(self, name, shape, dt):
        self._uid = getattr(self, "_uid", 0) + 1
        return self.cur.enter_context(self.nc.sbuf_tensor(f"sb{self._uid}_{name}", list(shape), dt))

    def ps(self, name, shape, dt):
        return self.st.enter_context(self.nc.psum_tensor("ps_" + name, list(shape), dt))

    def _alloc_common(self):
        S = self.S
        self.mixs_b = [Buf(f"mixs{i}") for i in range(KT)]
        self.x1_b = {(s_, t_): Buf(f"x1_{s_}_{t_}") for s_ in range(self.NSEQ) for t_ in range(self.NTT)}
        self.big = self.sb("big", [128, KT, S], BF16)
        self.big_b = [Buf(f"big{i}") for i in range(self.NTB)]
        self.wraw = self.sb("wraw", [128, 4 * KT * 256], BF16)
        self.w_b = [Buf(f"w{i}") for i in range(4)]
        self.epsc = self.sb("epsc", [128, 1], F32)
        self.epsc_b = Buf("epsc")
        self.onec = self.sb("onec", [128, 1], F32)
        self.onec_b = Buf("onec")
        self.col = self.sb("col", [128, 8], F32)
        self.col_b = [Buf(f"col{i}") for i in range(8)]
        self.identb = self.sb("identb", [128, 128], BF16)
        self.identb_b = Buf("identb")
        self.ones = self.sb("ones", [128, 128], BF16)
        self.ones_b = Buf("ones")
        self.onesf = self.sb("onesf", [128, 128], F32)
        self.onesf_b = Buf("onesf")
        self.psS = [self.ps(f"psS{i}", [128, 512], F32) for i in range(2)]
        self.psS_b = [Buf(f"psS{i}", excl=True) for i in range(2)]
        self.psO = [self.ps(f"psO{i}", [128, 512], F32) for i in range(3)]
        self.psO_b = [Buf(f"psO{i}", excl=True) for i in range(3)]
        self.psP = [self.ps(f"psP{i}", [128, 512], F32) for i in range(2)]
        self.psP_b = [Buf(f"psP{i}", excl=True) for i in range(2)]
        self.psX = self.ps("psX", [128, 512], F32)
        self.psX_b = Buf("psX", excl=True)
        self.psT = self.psX[:, :].bitcast(BF16).rearrange("p (j c) -> p j c", c=128)
        self.psT_b = self.psX_b

    def wslot(self, i):
        return self.wraw[:, i * 4096:(i + 1) * 4096].rearrange("p (k c) -> p k c", c=256)

    def wbig(self, j):
        return self.wraw[:, j * 8192:(j + 1) * 8192].rearrange("p (k c) -> p k c", c=512)

    def _load_consts(self):
        fw = self.fw
        fw.dma("pool", self.identb[:], self.ident, writes=[self.identb_b])
        fw.op("dve", lambda e: e.memset(self.ones[:], 1.0), writes=[self.ones_b])
        fw.op("dve", lambda e: e.memset(self.onesf[:], 1.0), writes=[self.onesf_b])
        fw.op("dve", lambda e: e.memset(self.epsc[:], EPS), writes=[self.epsc_b])
        fw.op("dve", lambda e: e.memset(self.onec[:], 1.0), writes=[self.onec_b])

    def norm_phase(self, xsrc, g16, g16_b, src_bufs=None):
        fw, S = self.fw, self.S
        for tt in range(self.NTT):
            i = tt % 2
            xs, xsb = self.xst[i], self.xst_b[i]
            rd = [src_bufs[tt]] if src_bufs is not None else []
            fw.dma("sp", xs[:, 0:D], xsrc[tt * 128:(tt + 1) * 128, :], reads=rd, writes=[xsb])
            xn, xn_b = self.xn2[tt % 2]
            c3 = 3 * (tt % 2)
            fw.op("act", lambda e: e.activation(out=xn[:, 0:D], in_=xs[:, 0:D], func=AF.Square,
                                                accum_out=self.col[:, c3:c3 + 1]),
                  reads=[xsb], writes=[xn_b, self.col_b[c3]])
            fw.op("act", lambda e: e.activation(out=self.col[:, c3 + 1:c3 + 2], in_=self.col[:, c3:c3 + 1], func=AF.Sqrt,
                                                scale=1.0 / D, bias=self.epsc[:, 0:1]),
                  reads=[self.col_b[c3], self.epsc_b], writes=[self.col_b[c3 + 1]])
            fw.op("dve", lambda e: e.reciprocal(out=self.col[:, c3 + 2:c3 + 3], in_=self.col[:, c3 + 1:c3 + 2]),
                  reads=[self.col_b[c3 + 1]], writes=[self.col_b[c3 + 2]])
            fw.op("act", lambda e: e.activation(out=xn[:, 0:D], in_=xs[:, 0:D], func=AF.Copy,
                                                scale=self.col[:, c3 + 2:c3 + 3]),
                  reads=[xsb, self.col_b[c3 + 2]], writes=[xn_b])
            tb = tt // 4
            for half in range(2):
                for j in range(8):
                    kt = half * 8 + j
                    fw.op("pe", lambda e: e.transpose(out=self.psT[:, j, :], in_=xn[:, kt * 128:(kt + 1) * 128],
                                                      identity=self.identb[:]),
                          reads=[xn_b, self.identb_b], writes=[self.psT_b], inc=(j == 7))
                dst = self.big[:, half * 8:(half + 1) * 8, tt * 128:(tt + 1) * 128]
                gsl = g16[:, half * 8:(half + 1) * 8].unsqueeze(2).to_broadcast([128, 8, 128])
                fw.op("dve", lambda e: e.tensor_tensor(out=dst, in0=self.psT[:], in1=gsl, op=ALU.mult),
                      reads=[self.psT_b, g16_b], writes=[self.big_b[tb]])

    def load_w(self, slot, wsrc, col0):
        src = wsrc[:, col0:col0 + 256].rearrange("(k p) c -> p k c", p=128)
        return self.fw.dma("pool", self.wslot(slot), src, writes=[self.w_b[slot]])

    def proj_fm(self, ps, psb, slot, sel, tb):
        w = self.wslot(slot)
        for kt in range(KT):
            self.fw.op("pe", lambda e: e.matmul(ps[:], lhsT=w[:, kt, sel * 128:(sel + 1) * 128],
                                                rhs=self.big[:, kt, tb * 512:(tb + 1) * 512],
                                                start=(kt == 0), stop=(kt == KT - 1)),
                       reads=[self.w_b[slot], self.big_b[tb]], writes=[psb], inc=(kt == KT - 1))

    def outproj_phase(self, s, wsrc, xsrc, dst, is_output, src_bufs=None):
        fw, S = self.fw, self.S
        for ft in range(KT):
            fw.dma("sp", self.big[:, ft, :], self.mixs[s, ft], reads=[self.mixs_b[ft]],
                   writes=self.big_b)
        steps = [(ch, tt) for ch in range(4) for tt in range(self.NTT)]

        def load_x(k):
            ch, tt = steps[k]
            i = k % 2
            fw.dma("sp", self.xst[i][:, 0:512], xsrc[tt * 128:(tt + 1) * 128, ch * 512:(ch + 1) * 512],
                   reads=([src_bufs[tt]] if src_bufs is not None else []), writes=[self.xst_b[i]])

        load_x(0)
        for k, (ch, tt) in enumerate(steps):
            j = ch % 2
            if tt == 0:
                src = wsrc[:, ch * 512:(ch + 1) * 512].rearrange("(k p) c -> p k c", p=128)
                fw.dma("pool", self.wbig(j), src, writes=[self.w_b[2 * j], self.w_b[2 * j + 1]])
            w = self.wbig(j)
            i = k % 2
            xs, xsb = self.xst[i], self.xst_b[i]
            if k + 1 < len(steps):
                load_x(k + 1)
            pp, ppb = self.psP[i], self.psP_b[i]
            for kt in range(KT):
                fw.op("pe", lambda e: e.matmul(pp[:], lhsT=self.big[:, kt, tt * 128:(tt + 1) * 128],
                                               rhs=w[:, kt, :], start=(kt == 0), stop=(kt == KT - 1)),
                      reads=[self.big_b[tt // 4], self.w_b[2 * j], self.w_b[2 * j + 1]], writes=[ppb],
                      inc=(kt == KT - 1))
            fw.op("dve", lambda e: e.tensor_tensor(out=xs[:, 512:1024], in0=pp[:], in1=xs[:, 0:512], op=ALU.add),
                  reads=[ppb, xsb], writes=[xsb])
            fw.dma("sp", dst[tt * 128:(tt + 1) * 128, ch * 512:(ch + 1) * 512], xs[:, 512:1024],
                   reads=[xsb], writes=([] if is_output else [self.x1_b[(s, tt)]]), is_output=is_output)

    def _alloc_l0(self):
        S = self.S
        self.cbt = self.sb("cbt", [128, 64], F32)
        self.cb_b = Buf("cbt")
        self.xst = [self.sb(f"xst{i}", [128, D], F32) for i in range(2)]
        self.xst_b = [Buf(f"xst{i}") for i in range(2)]
        self.xn = self.sb("xn", [128, D], BF16)
        self.xn_b = Buf("xn")
        self.xnB = self.sb("xnB", [128, D], BF16)
        self.xn2 = [(self.xn, self.xn_b), (self.xnB, Buf("xnB"))]
        self.g16_0 = self.sb("g16_0", [128, 16], F32)
        self.g16_0_b = Buf("g16_0")
        self.qT = self.sb("qT", [128, 2, S], BF16)
        self.kT = self.sb("kT", [128, 2, S], BF16)
        self.vv = self.sb("vv", [128, self.NTT, 256], BF16)
        self.gT = self.sb("gT", [128, 2, S], BF16)
        self.qT_b, self.kT_b, self.vv_b, self.gT_b = Buf("qT"), Buf("kT"), Buf("vv"), Buf("gT")
        self.sq = [self.sb(f"sq{i}", [128, 512], BF16) for i in range(2)]
        self.sq_b = [Buf(f"sq{i}") for i in range(2)]
        self.sd = [self.sb(f"sd{i}", [128, 512], F32) for i in range(2)]
        self.sd_b = [Buf(f"sd{i}") for i in range(2)]
        self.tmp = [self.sb(f"tmp{i}", [128, 512], F32) for i in range(4)]
        self.tmp_b = [Buf(f"tmp{i}") for i in range(4)]
        self.pt = [self.sb(f"pt{i}", [128, 512], BF16) for i in range(4)]
        self.pt_b = [Buf(f"pt{i}") for i in range(4)]
        self.biasA = self.sb("biasA", [128, 6, 256], F32)
        self.biasA_b = Buf("biasA")
        self.maskA = self.sb("maskA", [128, 6, 256], F32)
        self.maskA_b = Buf("maskA")
        self.tabB = self.sb("tabB", [128, 2, 512], F32)
        self.tabB_b = Buf("tabB")
        self.t0 = self.sb("t0", [128, 2, 512], F32)
        self.t0_b = Buf("t0")
        self.dd = self.sb("dd", [128, 2, 512], F32)
        self.dd_b = Buf("dd")
        self.rs = self.sb("rs", [128, 512], F32)
        self.rs_b = Buf("rs")
        self.yst = [self.sb(f"yst{i}", [128, 512], BF16) for i in range(2)]
        self.yst_b = [Buf(f"yst{i}") for i in range(2)]
        self.c0 = self.sb("c0", [128, 16], F32)
        self.c0_b = Buf("c0")
        self.lam4 = self.sb("lam4", [128, 4], F32)
        self.lam4_b = Buf("lam4")
        self.subgc = self.sb("subgc", [128, 2], F32)
        self.subgc_b = Buf("subgc")
        self.yctr = 0

    def _prep_l0(self):
        fw = self.fw
        fw.dma("sp", self.g16_0[:], self.g0, writes=[self.g16_0_b])
        fw.dma("sp", self.c0[:, 0:8], self.qkg, writes=[self.c0_b])
        fw.dma("sp", self.lam4[:], self.lamv, writes=[self.lam4_b])
        fw.dma("sp", self.subgc[:], self.subg, writes=[self.subgc_b])
        fw.dma("sp", self.maskA[:], self.MA, writes=[self.maskA_b])
        c0 = self.c0
        fw.op("dve", lambda e: e.tensor_tensor(out=c0[:, 9:10], in0=self.lam4[:, 0:1], in1=self.lam4[:, 1:2], op=ALU.mult),
              reads=[self.lam4_b], writes=[self.c0_b])
        fw.op("dve", lambda e: e.tensor_tensor(out=c0[:, 10:11], in0=self.lam4[:, 2:3], in1=self.lam4[:, 3:4], op=ALU.mult),
              reads=[self.lam4_b], writes=[self.c0_b])
        pp, ppb = self.psP[0], self.psP_b[0]
        fw.op("pe", lambda e: e.matmul(pp[:, 0:2], lhsT=self.onesf[:], rhs=c0[:, 9:11], start=True, stop=True),
              reads=[self.onesf_b, self.c0_b], writes=[ppb])
        fw.op("act", lambda e: e.activation(out=c0[:, 11:13], in_=pp[:, 0:2], func=AF.Exp),
              reads=[ppb], writes=[self.c0_b])
        fw.op("dve", lambda e: e.scalar_tensor_tensor(out=c0[:, 8:9], in0=c0[:, 12:13], scalar=-0.2, in1=c0[:, 11:12],
                                                      op0=ALU.add, op1=ALU.subtract),
              reads=[self.c0_b], writes=[self.c0_b])
        fw.op("dve", lambda e: e.tensor_scalar(out=self.subgc[:], in0=self.subgc[:], scalar1=0.8, scalar2=None,
                                               op0=ALU.mult),
              reads=[self.subgc_b], writes=[self.subgc_b])

    def run_deferred(self, keep=0):
        q = self.__dict__.setdefault("_defq", [])
        while len(q) > keep:
            q.pop(0)()

    def defer(self, fn):
        self.__dict__.setdefault("_defq", []).append(fn)

    def proj_bank(self):
        k = getattr(self, "_pbk", 0)
        self._pbk = k + 1
        banks = [(self.psP[0], self.psP_b[0]), (self.psP[1], self.psP_b[1]),
                 (self.psO[0], self.psO_b[0]), (self.psO[1], self.psO_b[1])]
        return banks[k % 4]

    def qk_unit(self, slot, sel, gcol, dstT, dst_b):
        fw = self.fw
        for tb in range(self.NTB):
            k = getattr(self, "_qkk", 0)
            self._qkk = k + 1
            i = k % 2
            pp, ppb = self.proj_bank()
            self.proj_fm(pp, ppb, slot, sel, tb)
            fw.op("act", lambda e: e.activation(out=self.sq[i][:], in_=pp[:], func=AF.Square),
                  reads=[ppb], writes=[self.sq_b[i]])
            self.run_deferred()

            def tail(i=i, pp=pp, ppb=ppb, tb=tb, sel=sel, gcol=gcol, dstT=dstT, dst_b=dst_b):
                pq, pqb = self.psS[i], self.psS_b[i]
                fw.op("pe", lambda e: e.matmul(pq[:], lhsT=self.ones[:], rhs=self.sq[i][:], start=True, stop=True),
                      reads=[self.ones_b, self.sq_b[i]], writes=[pqb])
                fw.op("act", lambda e: e.activation(out=self.sd[i][:], in_=pq[:], func=AF.Sqrt, scale=1.0 / 128,
                                                    bias=self.epsc[:, 0:1]),
                      reads=[pqb, self.epsc_b], writes=[self.sd_b[i]])
                fw.op("dve", lambda e: e.reciprocal(out=self.sd[i][:], in_=self.sd[i][:]),
                      reads=[self.sd_b[i]], writes=[self.sd_b[i]])
                fw.op("dve", lambda e: e.scalar_tensor_tensor(out=dstT[:, sel, tb * 512:(tb + 1) * 512], in0=pp[:],
                                                              scalar=self.c0[:, gcol:gcol + 1], in1=self.sd[i][:],
                                                              op0=ALU.mult, op1=ALU.mult),
                      reads=[ppb, self.c0_b, self.sd_b[i]], writes=[dst_b])
            self.defer(tail)

    def gate_unit(self, slot, sel):
        fw = self.fw
        for tb in range(self.NTB):
            pp, ppb = self.proj_bank()
            self.proj_fm(pp, ppb, slot, sel, tb)
            self.run_deferred()
            fw.op("act", lambda e: e.activation(out=self.gT[:, sel, tb * 512:(tb + 1) * 512], in_=pp[:], func=AF.Silu),
                  reads=[ppb], writes=[self.gT_b])

    def v_unit(self, slot):
        fw = self.fw
        w = self.wslot(slot)
        for tt in range(self.NTT):
            i = tt % 2
            pp, ppb = self.proj_bank()
            for kt in range(KT):
                fw.op("pe", lambda e: e.matmul(pp[:, 0:256], lhsT=self.big[:, kt, tt * 128:(tt + 1) * 128],
                                               rhs=w[:, kt, :], start=(kt == 0), stop=(kt == KT - 1)),
                      reads=[self.big_b[tt // 4], self.w_b[slot]], writes=[ppb], inc=(kt == KT - 1))
            self.run_deferred()
            if i == 0:
                fw.op("act", lambda e: e.activation(out=self.vv[:, tt, :], in_=pp[:, 0:256], func=AF.Copy),
                      reads=[ppb], writes=[self.vv_b])
            else:
                fw.op("dve", lambda e: e.tensor_copy(out=self.vv[:, tt, :], in_=pp[:, 0:256]),
                      reads=[ppb], writes=[self.vv_b])

    def spill_y(self, s, ft, col0, ncols, yi):
        self.fw.dma("sp", self.mixs[s, ft, :, col0:col0 + ncols], self.yst[yi][:, 0:ncols],
                    reads=[self.yst_b[yi]], writes=[self.mixs_b[ft]])

    def attn_A(self, s, pair):
        fw, S = self.fw, self.S
        scale = 128.0 ** -0.5
        for hh in range(2):
            h = 2 * pair + hh
            fw.dma("sp", self.biasA[:], self.GA[:, h], writes=[self.biasA_b])
            fw.op("dve", lambda e: e.tensor_tensor(out=self.biasA[:], in0=self.biasA[:], in1=self.maskA[:], op=ALU.add),
                  reads=[self.biasA_b, self.maskA_b], writes=[self.biasA_b])
            cnt = 0
            for Bq in range(S // 256):
                q0 = Bq * 256
                kts = [(jp, 2 * Bq - 4 + jp) for jp in range(6) if 2 * Bq - 4 + jp >= 0]
                ab = (Bq % 2)
                po, pob = (self.psO[0], self.psO_b[0]) if ab == 0 else (self.psO[2], self.psO_b[2])
                psm, psmb = (self.psO[1], self.psO_b[1]) if ab == 0 else (self.psP[0], self.psP_b[0])
                for n, (jp, kt) in enumerate(kts):
                    i = cnt % 4
                    cnt += 1
                    pS, pSb = self.sbank(i)
                    fw.op("pe", lambda e: e.matmul(pS[:, 0:256], lhsT=self.kT[:, hh, kt * 128:(kt + 1) * 128],
                                                   rhs=self.qT[:, hh, q0:q0 + 256], start=True, stop=True),
                          reads=[self.kT_b, self.qT_b], writes=[pSb])
                    fw.op("dve", lambda e: e.scalar_tensor_tensor(out=self.tmp[i][:, 0:256], in0=pS[:, 0:256], scalar=scale,
                                                                  in1=self.biasA[:, jp, :], op0=ALU.mult, op1=ALU.add),
                          reads=[pSb, self.biasA_b], writes=[self.tmp_b[i]])
                    fw.op("act", lambda e: e.activation(out=self.pt[i][:, 0:256], in_=self.tmp[i][:, 0:256], func=AF.Exp),
                          reads=[self.tmp_b[i]], writes=[self.pt_b[i]])
                    self.run_deferred(keep=2)
                    first, last = (n == 0), (n == len(kts) - 1)

                    def tail(i=i, kt=kt, first=first, last=last, po=po, pob=pob, psm=psm, psmb=psmb, q0=q0, hh=hh, h=h):
                        fw.op("pe", lambda e: e.matmul(po[:, 0:256], lhsT=self.vv[:, kt, hh * 128:(hh + 1) * 128],
                                                       rhs=self.pt[i][:, 0:256], start=first, stop=last),
                              reads=[self.vv_b, self.pt_b[i]], writes=[pob], inc=False)
                        fw.op("pe", lambda e: e.matmul(psm[:, 0:256], lhsT=self.ones[:], rhs=self.pt[i][:, 0:256],
                                                       start=first, stop=last),
                              reads=[self.ones_b, self.pt_b[i]], writes=[psmb])
                        if not last:
                            return
                        yi = self.yctr % 2
                        self.yctr += 1
                        fw.op("dve", lambda e: e.reciprocal(out=self.rs[:, 0:256], in_=psm[:, 0:256]),
                              reads=[psmb], writes=[self.rs_b])
                        fw.op("dve", lambda e: e.tensor_tensor(out=self.rs[:, 256:512], in0=po[:, 0:256], in1=self.rs[:, 0:256], op=ALU.mult),
                              reads=[pob, self.rs_b], writes=[self.rs_b])
                        fw.op("dve", lambda e: e.tensor_tensor(out=self.yst[yi][:, 0:256], in0=self.rs[:, 256:512],
                                                               in1=self.gT[:, hh, q0:q0 + 256], op=ALU.mult),
                              reads=[self.rs_b, self.gT_b], writes=[self.yst_b[yi]])
                        self.spill_y(s, h, q0, 256, yi)
                    self.defer(tail)
            self.run_deferred()

    def attn_B(self, s, h, slopes):
        fw, S = self.fw, self.S
        scale = 128.0 ** -0.5
        fw.dma("sp", self.tabB[:], self.TB[:, h], writes=[self.tabB_b])
        cnt = 0
        for Q in range(S // 512):
            q0 = Q * 512
            nkt = 4 * Q + 4
            for c in range(2):
                for kt in range(nkt):
                    i = cnt % 4
                    cnt += 1
                    pS, pSb = self.sbank(i)
                    if kt < 4 * Q:
                        lo = 0
                        strip = self.tabB[:, 1, :]
                        cb = float(-slopes[h] * 128.0 * (4 * Q - kt - 1))
                    else:
                        lo = 128 * (kt - 4 * Q)
                        strip = self.tabB[:, 0, 0:512 - lo]
                        cb = 0.0
                    fw.op("pe", lambda e: e.matmul(pS[:, lo:512], lhsT=self.kT[:, c, kt * 128:(kt + 1) * 128],
                                                   rhs=self.qT[:, c, q0 + lo:q0 + 512], start=True, stop=True),
                          reads=[self.kT_b, self.qT_b], writes=[pSb])
                    fw.op("dve", lambda e: e.scalar_tensor_tensor(out=self.tmp[i][:, lo:512], in0=pS[:, lo:512], scalar=scale,
                                                                  in1=strip, op0=ALU.mult, op1=ALU.add),
                          reads=[pSb, self.tabB_b], writes=[self.tmp_b[i]])
                    if cb != 0.0:
                        cbi = self.cbias_col(cb)
                        fw.op("act", lambda e: e.activation(out=self.pt[i][:, lo:512], in_=self.tmp[i][:, lo:512], func=AF.Exp,
                                                            bias=cbi),
                              reads=[self.tmp_b[i], self.cb_b], writes=[self.pt_b[i]])
                    else:
                        fw.op("act", lambda e: e.activation(out=self.pt[i][:, lo:512], in_=self.tmp[i][:, lo:512], func=AF.Exp),
                              reads=[self.tmp_b[i]], writes=[self.pt_b[i]])
                    self.run_deferred(keep=2)
                    first, last = (kt == 0), (kt == nkt - 1)

                    def tail(i=i, kt=kt, lo=lo, first=first, last=last, c=c, q0=q0, Q=Q):
                        for sl in range(2):
                            fw.op("pe", lambda e: e.matmul(self.psO[sl][:, lo:512], lhsT=self.vv[:, kt, sl * 128:(sl + 1) * 128],
                                                           rhs=self.pt[i][:, lo:512], start=first, stop=last),
                                  reads=[self.vv_b, self.pt_b[i]], writes=[self.psO_b[sl]], inc=False)
                        fw.op("pe", lambda e: e.matmul(self.psO[2][:, lo:512], lhsT=self.ones[:], rhs=self.pt[i][:, lo:512],
                                                       start=first, stop=last),
                              reads=[self.ones_b, self.pt_b[i]], writes=[self.psO_b[2]])
                        if last:
                            self.B_epilogue(s, h, c, q0)
                    self.defer(tail)
        self.run_deferred()

    def sbank(self, i):
        return [(self.psS[0], self.psS_b[0]), (self.psS[1], self.psS_b[1]), (self.psP[1], self.psP_b[1]),
                (self.psX, self.psX_b)][i]

    def cbias_col(self, cb):
        if not hasattr(self, "_cbcols"):
            self._cbcols = {}
        if cb not in self._cbcols:
            j = len(self._cbcols)
            assert j < self.cbt.shape[1]
            self.fw.op("pool", lambda e: e.memset(self.cbt[:, j:j + 1], cb), writes=[self.cb_b])
            self._cbcols[cb] = j
        j = self._cbcols[cb]
        return self.cbt[:, j:j + 1]

    def B_epilogue(self, s, h, c, q0):
        fw = self.fw
        fw.op("dve", lambda e: e.reciprocal(out=self.rs[:], in_=self.psO[2][:]),
              reads=[self.psO_b[2]], writes=[self.rs_b])
        for sl in range(2):
            if c == 0:
                fw.op("dve", lambda e: e.tensor_tensor(out=self.t0[:, sl, :], in0=self.psO[sl][:], in1=self.rs[:], op=ALU.mult),
                      reads=[self.psO_b[sl], self.rs_b], writes=[self.t0_b])
            else:
                fw.op("dve", lambda e: e.tensor_tensor(out=self.dd[:, sl, :], in0=self.psO[sl][:], in1=self.rs[:], op=ALU.mult),
                      reads=[self.psO_b[sl], self.rs_b], writes=[self.dd_b])
                fw.op("dve", lambda e: e.scalar_tensor_tensor(out=self.dd[:, sl, :], in0=self.dd[:, sl, :],
                                                              scalar=self.c0[:, 8:9], in1=self.t0[:, sl, :],
                                                              op0=ALU.mult, op1=ALU.add),
                      reads=[self.dd_b, self.c0_b, self.t0_b], writes=[self.dd_b])
        if c == 0:
            return
        pq, pqb = self.psP[0], self.psP_b[0]
        for sl in range(2):
            fw.op("act", lambda e: e.activation(out=self.sq[sl][:], in_=self.dd[:, sl, :], func=AF.Square),
                  reads=[self.dd_b], writes=[self.sq_b[sl]])
            fw.op("pe", lambda e: e.matmul(pq[:], lhsT=self.ones[:], rhs=self.sq[sl][:], start=(sl == 0), stop=(sl == 1)),
                  reads=[self.ones_b, self.sq_b[sl]], writes=[pqb], inc=(sl == 1))
        fw.op("act", lambda e: e.activation(out=self.sd[0][:], in_=pq[:], func=AF.Sqrt, scale=1.0 / 256,
                                            bias=self.epsc[:, 0:1]),
              reads=[pqb, self.epsc_b], writes=[self.sd_b[0]])
        fw.op("dve", lambda e: e.reciprocal(out=self.sd[0][:], in_=self.sd[0][:]),
              reads=[self.sd_b[0]], writes=[self.sd_b[0]])
        for sl in range(2):
            yi = self.yctr % 2
            self.yctr += 1
            fw.op("dve", lambda e: e.scalar_tensor_tensor(out=self.dd[:, sl, :], in0=self.dd[:, sl, :],
                                                          scalar=self.subgc[:, sl:sl + 1], in1=self.sd[0][:],
                                                          op0=ALU.mult, op1=ALU.mult),
                  reads=[self.dd_b, self.subgc_b, self.sd_b[0]], writes=[self.dd_b])
            fw.op("dve", lambda e: e.tensor_tensor(out=self.yst[yi][:], in0=self.dd[:, sl, :],
                                                   in1=self.gT[:, sl, q0:q0 + 512], op=ALU.mult),
                  reads=[self.dd_b, self.gT_b], writes=[self.yst_b[yi]])
            self.spill_y(s, 8 + 2 * h + sl, q0, 512, yi)

    def layer0(self, s):
        S = self.S
        xsrc = self.x[s * S:(s + 1) * S, :]
        self.norm_phase(xsrc, self.g16_0, self.g16_0_b)
        _, slopes = _tables_B()
        groups = [("A", i) for i in range(4)] + [("B", h) for h in range(4)]
        offs = {"A": (0, 1024, 2048, 3072), "B": (4096, 5120, 6144, 7168)}
        gcols = {"A": (0, 2), "B": (4, 6)}

        def issue_loads(g):
            kind, idx = groups[g]
            for u in range(4):
                self.load_w(u, self.w_in0, offs[kind][u] + idx * 256)

        issue_loads(0)
        for g, (kind, idx) in enumerate(groups):
            gq, gk = gcols[kind]
            for sel in range(2):
                self.qk_unit(0, sel, gq + sel, self.qT, self.qT_b)
            for sel in range(2):
                self.qk_unit(1, sel, gk + sel, self.kT, self.kT_b)
            self.v_unit(2)
            for sel in range(2):
                self.gate_unit(3, sel)
            if g + 1 < len(groups):
                issue_loads(g + 1)
            if kind == "A":
                self.attn_A(s, idx)
            else:
                self.attn_B(s, idx, slopes)
        self.run_deferred()
        dst = self.x1[s * S:(s + 1) * S, :]
        self.outproj_phase(s, self.w_out0, xsrc, dst, is_output=(not self.do_l1))


    def _alloc_l1(self):
        S = self.S
        FWID = max(S, D) + 8
        self.F = [self.sb(f"F{i}", [128, FWID], F32) for i in range(6)]
        self.F_b = [Buf(f"F{i}") for i in range(6)]
        self.H = [self.sb(f"H{i}", [128, max(S, D)], BF16) for i in range(7)]
        self.H_b = [Buf(f"H{i}") for i in range(7)]
        self.xst = [self.F[0], self.F[1]]
        self.xst_b = [self.F_b[0], self.F_b[1]]
        self.xn = self.H[0]
        self.xn_b = self.H_b[0]
        self.xn2 = [(self.H[0], self.H_b[0]), (self.H[1], self.H_b[1])]
        self.t5 = [self.sb(f"t5_{i}", [128, 512], F32) for i in range(3)]
        self.t5_b = [Buf(f"t5_{i}") for i in range(3)]
        self.yst = [self.sb(f"yst{i}", [128, 512], BF16) for i in range(2)]
        self.yst_b = [Buf(f"yst{i}") for i in range(2)]
        self.yctr = 0
        self.g16_1 = self.sb("g16_1", [128, 16], F32)
        self.g16_1_b = Buf("g16_1")
        self.cvw = self.sb("cvw", [128, 12, 4], F32)
        self.cl = self.sb("cl", [128, 8, 12], F32)
        self.cl_b = Buf("cl")
        self.wax = self.sb("wax", [128, 2, 2, 256], BF16)
        self.wax_b = Buf("wax")
        self.s5in = self.sb("s5in", [128, 3, 32], F32)
        self.s5w = self.F[4][:, 0:768].rearrange("p (r g) -> p r g", g=32)
        self.s5_b = Buf("s5")
        self.sgn = self.sb("sgn", [128, 1], F32)
        self.csK = self.sb("csK", [128, 12, 32], F32)
        self.snK = self.sb("snK", [128, 12, 32], F32)
        self.Cst = self.F[2][:, 0:512].rearrange("p (g h) -> p g h", h=16)
        self.Csw = self.F[3][:, 0:512].rearrange("p (g h) -> p g h", h=16)
        self.absA = self.sb("absA", [128, 32], F32)
        self.LA = self.sb("LA", [128, 32, 16], F32)
        self.LB = self.sb("LB", [128, 32, 16], F32)
        self.LT = self.t5[2][:, :].rearrange("p (g h) -> p g h", h=16)
        self.LAp = self.sb("LAp", [128, 1152], BF16)
        self.LBp = self.sb("LBp", [128, 1152], BF16)
        self.Lp_b = Buf("Lp")
        self.W12 = self.sb("W12", [128, 2, 1024], BF16)
        self.W12_b = Buf("W12")
        self.dsk = self.sb("dsk", [128, 8], F32)
        self.dsk_b = Buf("dsk")
        self.wglu = self.sb("wglu", [128, 4, 512], BF16)
        self.wglu_b = Buf("wglu")

    def _prep_l1(self):
        fw = self.fw
        fw.dma("sp", self.g16_1[:], self.g1, writes=[self.g16_1_b])
        fw.dma("sp", self.cvw[:], self.cvw_d, writes=[self.cl_b])
        fw.dma("sp", self.cl[:, 0:4, :], self.cl_d, writes=[self.cl_b])
        fw.dma("sp", self.s5in[:], self.s5in_d, writes=[self.s5_b])
        fw.dma("sp", self.sgn[:], self.sgn_d, writes=[self.s5_b])
        fw.dma("sp", self.Cst, self.Cst_d, writes=[self.s5_b])
        fw.dma("sp", self.Csw, self.Csw_d, writes=[self.s5_b])
        fw.dma("sp", self.dsk[:], self.dsk_d, writes=[self.dsk_b])
        fw.dma("pool", self.wglu[:], self.wglu_d.rearrange("(k p) c -> p k c", p=128), writes=[self.wglu_b])
        fw.op("dve", lambda e: e.memset(self.LAp[:], 0.0), writes=[self.Lp_b])
        fw.op("dve", lambda e: e.memset(self.LBp[:], 0.0), writes=[self.Lp_b])
        cl, clb = self.cl, [self.cl_b]
        PI = math.pi

        def act(out, in_, func, b, scale=1.0, bias=None):
            kw = {}
            rd = list(b)
            if bias is not None:
                kw["bias"] = bias
                rd.append(self.onec_b)
            fw.op("act", lambda e: e.activation(out=out, in_=in_, func=func, scale=scale, **kw), reads=rd, writes=b)

        def ts(out, in0, s1, op0, b, s2=None, op1=None):
            if op1 is None:
                fw.op("dve", lambda e: e.tensor_scalar(out=out, in0=in0, scalar1=s1, scalar2=None, op0=op0), reads=b, writes=b)
            else:
                fw.op("dve", lambda e: e.tensor_scalar(out=out, in0=in0, scalar1=s1, scalar2=s2, op0=op0, op1=op1), reads=b, writes=b)

        def tt(out, in0, in1, op, b):
            fw.op("dve", lambda e: e.tensor_tensor(out=out, in0=in0, in1=in1, op=op), reads=b, writes=b)

        lam, z, w, cc, cc2, t6, t7 = cl[:, 3, :], cl[:, 4, :], cl[:, 5, :], cl[:, 4, :], cl[:, 5, :], cl[:, 6, :], cl[:, 7, :]
        act(t6, lam, AF.Exp, clb, scale=-1.0)
        ts(t7, t6, 1.0, ALU.add, clb)
        act(w, t7, AF.Ln, clb)
        ts(t7, t7, -1.0, ALU.add, clb, 1e-30, ALU.max)
        fw.op("dve", lambda e: e.reciprocal(out=t7, in_=t7), reads=clb, writes=clb)
        tt(t7, t7, t6, ALU.mult, clb)
        tt(t7, t7, w, ALU.mult, clb)
        ts(cc, t7, -8.0, ALU.mult, clb)
        ts(cc2, t7, -16.0, ALU.mult, clb)

        sw, sb_ = self.s5w, [self.s5_b]
        are, aim, ldt = self.s5in[:, 0, :], self.s5in[:, 1, :], self.s5in[:, 2, :]
        R = lambda i: sw[:, i, :]
        dt, th, ea, y1, y2, tq, sn, cs, nr, ni, den, cr, ci, kA1, nci, kB2, nrm = [R(i) for i in range(17)]
        act(dt, ldt, AF.Exp, sb_)
        tt(th, aim, dt, ALU.mult, sb_)
        tt(tq, are, dt, ALU.mult, sb_)
        act(ea, tq, AF.Exp, sb_)

        def reduce_to(dst, add):
            ts(dst, th, add, ALU.add, sb_)
            ts(R(17), dst, 0.0, ALU.add, sb_)
            for m in range(1, 12):
                ts(tq, R(17), (2 * m - 1) * PI, ALU.is_gt, sb_, -2.0 * PI, ALU.mult)
                tt(dst, dst, tq, ALU.add, sb_)
            ts(dst, dst, -3.141592, ALU.max, sb_, 3.141592, ALU.min)

        reduce_to(y1, 0.0)
        reduce_to(y2, PI / 2)
        act(sn, y1, AF.Sin, sb_)
        act(cs, y2, AF.Sin, sb_)
        tt(nr, ea, cs, ALU.mult, sb_)
        ts(nr, nr, -1.0, ALU.add, sb_)
        tt(ni, ea, sn, ALU.mult, sb_)
        tt(den, are, are, ALU.mult, sb_)
        tt(tq, aim, aim, ALU.mult, sb_)
        tt(den, den, tq, ALU.add, sb_)
        fw.op("dve", lambda e: e.reciprocal(out=den, in_=den), reads=sb_, writes=sb_)
        tt(cr, nr, are, ALU.mult, sb_)
        tt(tq, ni, aim, ALU.mult, sb_)
        tt(cr, cr, tq, ALU.add, sb_)
        tt(cr, cr, den, ALU.mult, sb_)
        tt(ci, ni, are, ALU.mult, sb_)
        tt(tq, nr, aim, ALU.mult, sb_)
        tt(ci, ci, tq, ALU.subtract, sb_)
        tt(ci, ci, den, ALU.mult, sb_)
        fw.op("dve", lambda e: e.tensor_scalar(out=kA1, in0=cr, scalar1=self.sgn[:, 0:1], scalar2=None, op0=ALU.mult), reads=sb_, writes=sb_)
        ts(nci, ci, -1.0, ALU.mult, sb_)
        ts(kB2, kA1, -1.0, ALU.mult, sb_)
        bc = lambda a: a.unsqueeze(2).to_broadcast([128, 32, 16])
        tt(self.LA[:], self.Cst, bc(kA1), ALU.mult, sb_)
        tt(self.LT, self.Csw, bc(nci), ALU.mult, sb_)
        tt(self.LA[:], self.LA[:], self.LT, ALU.add, sb_)
        tt(self.LB[:], self.Cst, bc(nci), ALU.mult, sb_)
        tt(self.LT, self.Csw, bc(kB2), ALU.mult, sb_)
        tt(self.LB[:], self.LB[:], self.LT, ALU.add, sb_)
        c0, s0 = self.csK[:, 0, :], self.snK[:, 0, :]
        fw.op("dve", lambda e: e.tensor_scalar(out=s0, in0=sn, scalar1=self.sgn[:, 0:1], scalar2=None, op0=ALU.mult), reads=sb_, writes=sb_)
        tt(nrm, cs, cs, ALU.mult, sb_)
        tt(tq, sn, sn, ALU.mult, sb_)
        tt(nrm, nrm, tq, ALU.add, sb_)
        act(nrm, nrm, AF.Sqrt, sb_)
        fw.op("dve", lambda e: e.reciprocal(out=nrm, in_=nrm), reads=sb_, writes=sb_)
        tt(c0, cs, nrm, ALU.mult, sb_)
        tt(s0, s0, nrm, ALU.mult, sb_)
        self.NLEV = int(round(math.log2(self.S)))
        for k in range(self.NLEV - 1):
            ck, sk = self.csK[:, k, :], self.snK[:, k, :]
            cn, sn_ = self.csK[:, k + 1, :], self.snK[:, k + 1, :]
            tt(cn, ck, ck, ALU.mult, sb_)
            tt(tq, sk, sk, ALU.mult, sb_)
            tt(cn, cn, tq, ALU.subtract, sb_)
            tt(sn_, ck, sk, ALU.mult, sb_)
            ts(sn_, sn_, 2.0, ALU.mult, sb_)
        ts(self.absA[:], ea, 0.0, ALU.add, sb_)
        fw.barrier()

    def mixer_C(self, s, n):
        fw, S = self.fw, self.S
        F, Fb, H, Hb = self.F, self.F_b, self.H, self.H_b
        cinp, cinp_b = F[0], Fb[0]
        U, U_b = [F[1], F[2]], [Fb[1], Fb[2]]
        r, r_b, ii, ii_b, a, a_b = F[3], Fb[3], F[4], Fb[4], F[5], Fb[5]
        UB, UB_b = [H[1], H[2]], [Hb[1], Hb[2]]
        yb, yb_b = H[3], Hb[3]
        cl, clb = self.cl, self.cl_b
        import os
        kc = int(os.environ.get("KC", "99"))
        if kc not in (-1, -4):
          fw.dma("pool", self.wax[:, 0], self.lru_wa_d[n].rearrange("(k p) j -> p k j", p=128), writes=[self.wax_b])
          fw.dma("pool", self.wax[:, 1], self.lru_wx_d[n].rearrange("(k p) j -> p k j", p=128), writes=[self.wax_b])
        if kc not in (-2, -4):
            fw.op("dve", lambda e: e.memset(cinp[:, 0:3], 0.0), writes=[cinp_b])
        if kc in (-3, -4):
            return
        for j in range(2):
            ct = 2 * n + j
            for tb in range(self.NTB):
                i = tb % 2
                pp, ppb = self.psP[i], self.psP_b[i]
                self.proj_fm(pp, ppb, 0, j, tb)
                fw.op("act", lambda e: e.activation(out=cinp[:, 3 + tb * 512:3 + (tb + 1) * 512], in_=pp[:], func=AF.Copy),
                      reads=[ppb], writes=[cinp_b])
            u = U[j]
            if kc < 1:
                continue
            fw.op("dve", lambda e: e.tensor_scalar(out=u[:, 0:S], in0=cinp[:, 0:S], scalar1=self.cvw[:, ct, 0:1],
                                                   scalar2=cl[:, 0, ct:ct + 1], op0=ALU.mult, op1=ALU.add),
                  reads=[cinp_b, clb], writes=[U_b[j]])
            if kc < 2:
                continue
            for tap in range(1, 4):
                fw.op("dve", lambda e: e.scalar_tensor_tensor(out=u[:, 0:S], in0=cinp[:, tap:tap + S],
                                                              scalar=self.cvw[:, ct, tap:tap + 1], in1=u[:, 0:S],
                                                              op0=ALU.mult, op1=ALU.add),
                      reads=[cinp_b, clb, U_b[j]], writes=[U_b[j]])
            fw.op("act", lambda e: e.activation(out=UB[j][:, 0:S], in_=u[:, 0:S], func=AF.Copy),
                  reads=[U_b[j]], writes=[UB_b[j]])
        if kc < 3:
            return
        for jo in range(2):
            ct = 2 * n + jo
            for gi, (dst, dst_b, brow) in enumerate(((r, r_b, 1), (ii, ii_b, 2))):
                for tb in range(self.NTB):
                    i = tb % 2
                    pp, ppb = self.psS[i], self.psS_b[i]
                    for k in range(2):
                        fw.op("pe", lambda e: e.matmul(pp[:], lhsT=self.wax[:, gi, k, jo * 128:(jo + 1) * 128],
                                                       rhs=UB[k][:, tb * 512:(tb + 1) * 512], start=(k == 0), stop=(k == 1)),
                              reads=[self.wax_b, UB_b[k]], writes=[ppb], inc=(k == 1))
                    fw.op("act", lambda e: e.activation(out=dst[:, tb * 512:(tb + 1) * 512], in_=pp[:], func=AF.Sigmoid,
                                                        bias=cl[:, brow, ct:ct + 1]),
                          reads=[ppb, clb], writes=[dst_b])
            if kc < 4:
                continue
            fw.op("act", lambda e: e.activation(out=a[:, 0:S], in_=r[:, 0:S], func=AF.Exp, scale=cl[:, 4, ct:ct + 1]),
                  reads=[r_b, clb], writes=[a_b])
            fw.op("act", lambda e: e.activation(out=r[:, 0:S], in_=r[:, 0:S], func=AF.Exp, scale=cl[:, 5, ct:ct + 1]),
                  reads=[r_b, clb], writes=[r_b])
            fw.op("act", lambda e: e.activation(out=r[:, 0:S], in_=r[:, 0:S], func=AF.Sqrt, scale=-1.0, bias=self.onec[:, 0:1]),
                  reads=[r_b, self.onec_b], writes=[r_b])
            if kc < 5:
                continue
            fw.op("dve", lambda e: e.tensor_tensor(out=ii[:, 0:S], in0=ii[:, 0:S], in1=U[jo][:, 0:S], op=ALU.mult),
                  reads=[ii_b, U_b[jo]], writes=[ii_b])
            fw.op("dve", lambda e: e.tensor_tensor(out=ii[:, 0:S], in0=ii[:, 0:S], in1=r[:, 0:S], op=ALU.mult),
                  reads=[ii_b, r_b], writes=[ii_b])
            fw.op("dve", lambda e: e.tensor_tensor_scan(out=r[:, 0:S], data0=a[:, 0:S], data1=ii[:, 0:S], initial=0.0,
                                                        op0=ALU.mult, op1=ALU.add),
                  reads=[a_b, ii_b, r_b], writes=[r_b])
            for tb in range(self.NTB):
                i = tb % 2
                pp, ppb = self.psP[i], self.psP_b[i]
                self.proj_fm(pp, ppb, 1, jo, tb)
                fw.op("act", lambda e: e.activation(out=a[:, tb * 512:(tb + 1) * 512], in_=pp[:], func=AF.Silu),
                      reads=[ppb], writes=[a_b])
            fw.op("dve", lambda e: e.tensor_tensor(out=yb[:, 0:S], in0=r[:, 0:S], in1=a[:, 0:S], op=ALU.mult),
                  reads=[r_b, a_b], writes=[yb_b])
            fw.dma("sp", self.mixs[s, ct], yb[:, 0:S], reads=[yb_b], writes=[self.mixs_b[ct]])

    def mixer_D_tile(self, s, ct):
        fw, S = self.fw, self.S
        F, Fb, H, Hb = self.F, self.F_b, self.H, self.H_b
        dinf, dinf_b = F[0], Fb[0]
        COS, COS_b, SIN, SIN_b = F[1], Fb[1], F[2], Fb[2]
        bu, bu_b, Rr, Rr_b = F[3], Fb[3], F[4], Fb[4]
        dinb, dinb_b = H[1], Hb[1]
        Zc, Zc_b, Zs, Zs_b = H[2], Hb[2], H[5], Hb[5]
        gel, gel_b = self.gelb[ct], self.gelb_b[ct]
        t5, t5b = self.t5, self.t5_b
        yacc = [self.psO[0], self.psO[1], self.psO[2], self.psS[1]]
        yacc_b = [self.psO_b[0], self.psO_b[1], self.psO_b[2], self.psS_b[1]]
        import os
        for tb in range(self.NTB):
            i = tb % 2
            pp, ppb = self.psP[i], self.psP_b[i]
            ke = os.environ.get("KE", "")
            self.proj_fm(pp, ppb, 0 if "s" in ke else 2, ct % 2, tb)
            if "c" not in ke:
                fw.op("act", lambda e: e.activation(out=dinf[:, tb * 512:(tb + 1) * 512], in_=pp[:], func=AF.Copy),
                      reads=[ppb], writes=[dinf_b])
            if "d" not in ke:
                fw.op("dve", lambda e: e.tensor_copy(out=dinb[:, tb * 512:(tb + 1) * 512], in_=pp[:]),
                      reads=[ppb], writes=[dinb_b])
        import os
        kd = int(os.environ.get("KD", "99"))
        if kd < 1:
            return
        fw.dma("pool", self.W12[:, 0, :], self.W1_d[ct], writes=[self.W12_b])
        fw.dma("pool", self.W12[:, 1, :], self.W2_d[ct], writes=[self.W12_b])
        if kd < 2:
            return
        diagA = self.LAp[:, 0:8 * 144].rearrange("p (g c) -> p g c", c=144)[:, :, 0:16]
        diagB = self.LBp[:, 0:8 * 144].rearrange("p (g c) -> p g c", c=144)[:, :, 0:16]
        fw.op("dve", lambda e: e.tensor_copy(out=diagA, in_=self.LA[:, 8 * ct:8 * ct + 8, :]),
              reads=[self.s5_b], writes=[self.Lp_b])
        fw.op("dve", lambda e: e.tensor_copy(out=diagB, in_=self.LB[:, 8 * ct:8 * ct + 8, :]),
              reads=[self.s5_b], writes=[self.Lp_b])
        if kd < 3:
            return
        for gl in range(8):
            g = 8 * ct + gl
            fw.op("dve", lambda e: e.memset(COS[:, 0:1], 1.0), writes=[COS_b])
            fw.op("dve", lambda e: e.memset(SIN[:, 0:1], 0.0), writes=[SIN_b])
            for k in range(self.NLEV):
                nn = 1 << k
                ck = self.csK[:, k, g:g + 1]
                sk = self.snK[:, k, g:g + 1]
                fw.op("dve", lambda e: e.tensor_scalar(out=Rr[:, 0:nn], in0=SIN[:, 0:nn], scalar1=sk, scalar2=None, op0=ALU.mult),
                      reads=[SIN_b, self.s5_b], writes=[Rr_b])
                fw.op("dve", lambda e: e.tensor_scalar(out=bu[:, 0:nn], in0=COS[:, 0:nn], scalar1=sk, scalar2=None, op0=ALU.mult),
                      reads=[COS_b, self.s5_b], writes=[bu_b])
                fw.op("dve", lambda e: e.scalar_tensor_tensor(out=COS[:, nn:2 * nn], in0=COS[:, 0:nn], scalar=ck, in1=Rr[:, 0:nn],
                                                              op0=ALU.mult, op1=ALU.subtract),
                      reads=[COS_b, Rr_b, self.s5_b], writes=[COS_b])
                fw.op("dve", lambda e: e.scalar_tensor_tensor(out=SIN[:, nn:2 * nn], in0=SIN[:, 0:nn], scalar=ck, in1=bu[:, 0:nn],
                                                              op0=ALU.mult, op1=ALU.add),
                      reads=[SIN_b, bu_b, self.s5_b], writes=[SIN_b])
            if kd < 4:
                continue
            for tb in range(self.NTB):
                sl = slice(tb * 512, (tb + 1) * 512)
                p1, p1b, p2, p2b = self.psP[0], self.psP_b[0], self.psP[1], self.psP_b[1]
                fw.op("pe", lambda e: e.matmul(p1[:], lhsT=self.W12[:, 0, gl * 128:(gl + 1) * 128], rhs=dinb[:, sl], start=True, stop=True),
                      reads=[self.W12_b, dinb_b], writes=[p1b])
                fw.op("pe", lambda e: e.matmul(p2[:], lhsT=self.W12[:, 1, gl * 128:(gl + 1) * 128], rhs=dinb[:, sl], start=True, stop=True),
                      reads=[self.W12_b, dinb_b], writes=[p2b])
                fw.op("dve", lambda e: e.tensor_tensor(out=bu[:, sl], in0=p1[:], in1=COS[:, sl], op=ALU.mult),
                      reads=[p1b, COS_b], writes=[bu_b])
                fw.op("dve", lambda e: e.tensor_tensor(out=t5[0][:], in0=p2[:], in1=SIN[:, sl], op=ALU.mult),
                      reads=[p2b, SIN_b], writes=[t5b[0]])
                fw.op("dve", lambda e: e.tensor_tensor(out=bu[:, sl], in0=bu[:, sl], in1=t5[0][:], op=ALU.add),
                      reads=[bu_b, t5b[0]], writes=[bu_b])
            if kd < 5:
                continue
            fw.op("dve", lambda e: e.tensor_tensor_scan(out=Rr[:, 0:S], data0=self.absA[:, g:g + 1].to_broadcast([128, S]),
                                                        data1=bu[:, 0:S], initial=0.0, op0=ALU.mult, op1=ALU.add),
                  reads=[self.s5_b, bu_b, Rr_b], writes=[Rr_b])
            fw.op("dve", lambda e: e.tensor_tensor(out=Zc[:, 0:S], in0=Rr[:, 0:S], in1=COS[:, 0:S], op=ALU.mult),
                  reads=[Rr_b, COS_b], writes=[Zc_b])
            fw.op("dve", lambda e: e.tensor_tensor(out=Zs[:, 0:S], in0=Rr[:, 0:S], in1=SIN[:, 0:S], op=ALU.mult),
                  reads=[Rr_b, SIN_b], writes=[Zs_b])
            if kd < 6:
                continue
            for tb in range(self.NTB):
                sl = slice(tb * 512, (tb + 1) * 512)
                fw.op("pe", lambda e: e.matmul(yacc[tb][:], lhsT=self.LAp[:, gl * 128:(gl + 1) * 128], rhs=Zc[:, sl],
                                               start=(gl == 0), stop=False),
                      reads=[self.Lp_b, Zc_b], writes=[yacc_b[tb]], inc=False)
                fw.op("pe", lambda e: e.matmul(yacc[tb][:], lhsT=self.LBp[:, gl * 128:(gl + 1) * 128], rhs=Zs[:, sl],
                                               start=False, stop=(gl == 7)),
                      reads=[self.Lp_b, Zs_b], writes=[yacc_b[tb]])
        if kd < 7:
            return
        for tb in range(self.NTB):
            sl = slice(tb * 512, (tb + 1) * 512)
            fw.op("dve", lambda e: e.scalar_tensor_tensor(out=t5[1][:], in0=dinf[:, sl], scalar=self.dsk[:, ct:ct + 1],
                                                          in1=yacc[tb][:], op0=ALU.mult, op1=ALU.add),
                  reads=[dinf_b, self.dsk_b, yacc_b[tb]], writes=[t5b[1]])
            fw.op("act", lambda e: e.activation(out=gel[:, sl], in_=t5[1][:], func=AF.Gelu_apprx_tanh),
                  reads=[t5b[1]], writes=[gel_b])

    def mixer_D_glu(self, s):
        fw, S = self.fw, self.S
        t5, t5b = self.t5, self.t5_b
        for jo in range(4):
            for tb in range(self.NTB):
                sl = slice(tb * 512, (tb + 1) * 512)
                pp, ppb = self.psS[0], self.psS_b[0]
                for k in range(4):
                    fw.op("pe", lambda e: e.matmul(pp[:], lhsT=self.wglu[:, k, jo * 128:(jo + 1) * 128], rhs=self.gelb[k][:, sl],
                                                   start=(k == 0), stop=(k == 3)),
                          reads=[self.wglu_b, self.gelb_b[k]], writes=[ppb], inc=(k == 3))
                fw.op("act", lambda e: e.activation(out=t5[0][:], in_=pp[:], func=AF.Sigmoid, bias=self.dsk[:, 4 + jo:5 + jo]),
                      reads=[ppb, self.dsk_b], writes=[t5b[0]])
                pg, pgb = self.psP[tb % 2], self.psP_b[tb % 2]
                self.proj_fm(pg, pgb, jo // 2, jo % 2, tb)
                fw.op("act", lambda e: e.activation(out=t5[1][:], in_=pg[:], func=AF.Silu),
                      reads=[pgb], writes=[t5b[1]])
                fw.op("dve", lambda e: e.tensor_tensor(out=t5[2][:], in0=t5[0][:], in1=self.gelb[jo][:, sl], op=ALU.mult),
                      reads=[t5b[0], self.gelb_b[jo]], writes=[t5b[2]])
                yi = self.yctr % 2
                self.yctr += 1
                fw.op("dve", lambda e: e.tensor_tensor(out=self.yst[yi][:], in0=t5[2][:], in1=t5[1][:], op=ALU.mult),
                      reads=[t5b[2], t5b[1]], writes=[self.yst_b[yi]])
                self.spill_y(s, 12 + jo, tb * 512, 512, yi)

    def layer1(self, s):
        import os
        dbg = int(os.environ.get("KDBG", "9"))
        S = self.S
        xsrc = self.x1[s * S:(s + 1) * S, :]
        srcb = [self.x1_b[(s, tt)] for tt in range(self.NTT)] if self.do_l0 else None
        if dbg < 1:
            return
        self.norm_phase(xsrc, self.g16_1, self.g16_1_b, src_bufs=srcb)
        if dbg < 2:
            return
        self.gelb = [self.H[3], self.H[4], self.H[6], self.H[0]]
        self.gelb_b = [self.H_b[3], self.H_b[4], self.H_b[6], self.H_b[0]]
        if "L" not in os.environ.get("KSKIP", ""):
            self.load_w(0, self.w_in1, 0)
            self.load_w(1, self.w_in1, 1536)
        if "M" in os.environ.get("KSKIP", ""):
            return
        for n in range(6):
            if n + 1 < 6:
                self.load_w(2, self.w_in1, (n + 1) * 256) if False else None
            self.mixer_C(s, n)
            if n + 1 < 6:
                self.load_w(0, self.w_in1, (n + 1) * 256)
                self.load_w(1, self.w_in1, 1536 + (n + 1) * 256)
        if dbg < 3:
            return
        for ct in range(int(os.environ.get("KN", "4"))):
            ke = os.environ.get("KE", "")
            if ct % 2 == 0 and "n" not in ke:
                self.load_w(2, self.w_in1, 3072 + (ct // 2) * 256)
            if "a" in ke:
                continue
            self.mixer_D_tile(s, ct)
        if dbg < 4:
            return
        self.load_w(0, self.w_in1, 3584)
        self.load_w(1, self.w_in1, 3584 + 256)
        self.mixer_D_glu(s)
        if dbg < 5:
            return
        dst = self.out[s * S:(s + 1) * S, :]
        self.outproj_phase(s, self.w_out1, xsrc, dst, is_output=True, src_bufs=srcb)


def _prep_inputs_l0(inp):
    rel = np.asarray(inp["a_rel_bias"][0], np.float32)
    GA, MA = _tables_A(rel)
    TB, _ = _tables_B()
    aq = np.asarray(inp["a_q_g"][0], np.float32)
    ak = np.asarray(inp["a_k_g"][0], np.float32)
    bq = np.asarray(inp["b_q_g"][0], np.float32)
    bk = np.asarray(inp["b_k_g"][0], np.float32)
    qkg = np.stack([aq, aq, ak, ak, bq[0], bq[1], bk[0], bk[1]], axis=1)
    lamv = np.stack([inp["b_lam_q1"][0], inp["b_lam_k1"][0], inp["b_lam_q2"][0], inp["b_lam_k2"][0]], axis=1)
    subg = np.asarray(inp["b_subln_g"][0], np.float32).reshape(2, 128).T
    return {
        "w_in0": np.ascontiguousarray(inp["attn_w_in"][0], np.float32),
        "w_out0": np.ascontiguousarray(inp["attn_w_out"][0], np.float32),
        "g0": np.ascontiguousarray(np.asarray(inp["attn_norm_g"][0], np.float32).reshape(16, 128).T),
        "qkg": np.ascontiguousarray(qkg, np.float32),
        "subg": np.ascontiguousarray(subg, np.float32),
        "lamv": np.ascontiguousarray(lamv, np.float32),
        "GA": GA, "MA": MA, "TB": TB,
    }


def _prep_inputs_l1(inp):
    f = lambda k: np.asarray(inp[k][0], np.float32)
    cvw = f("lru_conv_w").reshape(4, 12, 128).transpose(2, 1, 0)
    t12 = lambda v: v.reshape(12, 128).T
    cl = np.stack([t12(f("lru_conv_b")), t12(f("lru_b_a").reshape(-1)), t12(f("lru_b_x").reshape(-1)),
                   t12(f("lru_lambda"))], axis=1)
    dup = lambda a: np.concatenate([a.T, a.T], axis=0)
    s5in = np.stack([dup(f("ssm_a_re")), dup(f("ssm_a_im")),
                     np.broadcast_to(f("ssm_log_dt")[None, :], (128, 32))], axis=1)
    sgn = np.concatenate([np.ones(64), -np.ones(64)]).astype(np.float32).reshape(128, 1)
    cre = f("ssm_c_re").transpose(2, 0, 1)
    cim = f("ssm_c_im").transpose(2, 0, 1)
    Cst = np.concatenate([cre, cim], axis=0)
    Csw = np.concatenate([cim, cre], axis=0)
    bre = f("ssm_b_re").transpose(0, 2, 1)
    bim = f("ssm_b_im").transpose(0, 2, 1)
    W1 = np.zeros((4, 128, 8, 128), np.float32)
    W2 = np.zeros((4, 128, 8, 128), np.float32)
    for g in range(32):
        ct, gl = g // 8, g % 8
        W1[ct, 16 * gl:16 * gl + 16, gl, 0:64] = bre[g]
        W1[ct, 16 * gl:16 * gl + 16, gl, 64:128] = bim[g]
        W2[ct, 16 * gl:16 * gl + 16, gl, 0:64] = bim[g]
        W2[ct, 16 * gl:16 * gl + 16, gl, 64:128] = bre[g]
    dsk = np.concatenate([f("ssm_d").reshape(4, 128).T, f("ssm_b_glu").reshape(4, 128).T], axis=1)
    c = np.ascontiguousarray
    return {
        "w_in1": c(f("rec_w_in")), "w_out1": c(f("rec_w_out")),
        "g1": c(f("rec_norm_g").reshape(16, 128).T),
        "cvw": c(cvw), "cl": c(cl), "lru_wa": c(f("lru_w_a")), "lru_wx": c(f("lru_w_x")),
        "s5in": c(s5in), "sgn": sgn, "Cst": c(Cst), "Csw": c(Csw),
        "W1": c(W1.reshape(4, 128, 1024)), "W2": c(W2.reshape(4, 128, 1024)),
        "dsk": c(dsk), "wglu": c(f("ssm_w_glu")),
    }


def run(inp, S, NSEQ, n_cores, do_l0=True, do_l1=True):
    prog = Prog(S, NSEQ, do_l0, do_l1)
    nc = prog.build()
    x = np.asarray(inp["x"], np.float32)
    shared = {"ident": np.eye(128, dtype=np.float32)}
    if do_l0:
        shared.update(_prep_inputs_l0(inp))
    if do_l1:
        shared.update(_prep_inputs_l1(inp))
    in_maps = []
    for c in range(n_cores):
        m = dict(shared)
        m["x"] = np.ascontiguousarray(x[c * NSEQ:(c + 1) * NSEQ].reshape(NSEQ * S, D))
        in_maps.append(m)
    res = run_bass_kernel_spmd(nc, in_maps, core_ids=list(range(n_cores)))
    outs = [np.asarray(r["out"]).reshape(NSEQ, S, D) for r in res.results]
    return np.concatenate(outs, axis=0).astype(np.float32)


def kernel(**inputs):
    return run(inputs, SEQ, BATCH // N_CORES, N_CORES)
```

```python
import math
import numpy as np
from contextlib import ExitStack
import concourse.bass as bass
import concourse.mybir as mybir
from concourse.bass_utils import run_bass_kernel_spmd

F32 = mybir.dt.float32
BF16 = mybir.dt.bfloat16
AF = mybir.ActivationFunctionType
ALU = mybir.AluOpType

D = 2048
KT = 16
EPS = 1e-6
NEG = -30000.0
N_CORES = 8
SEQ = 2048
BATCH = 16


class Buf:
    __slots__ = ("name", "w", "r", "excl")

    def __init__(self, name="", excl=False):
        self.name = name
        self.w = None
        self.r = {}
        self.excl = excl


class Eng:
    def __init__(self, name, eng):
        self.name = name
        self.eng = eng
        self.sem = None
        self.count = 0
        self.seen = {}
        self.pend_r = []
        self.pend_w = []


class FW:
    def __init__(self, nc, stack, n_dma_sems=12):
        self.nc = nc
        self.E = {}
        for name, eng in (("pe", nc.tensor), ("act", nc.scalar), ("dve", nc.vector),
                          ("pool", nc.gpsimd), ("sp", nc.sync)):
            e = Eng(name, eng)
            e.sem = stack.enter_context(nc.semaphore("s_" + name))
            self.E[name] = e
        self.dma_sems = {}
        for q, nq in (("sp", n_dma_sems), ("pool", 4)):
            sems = [stack.enter_context(nc.semaphore(f"d_{q}{i}")) for i in range(nq)]
            self.dma_sems[q] = {"sems": sems, "vals": [0] * nq, "next": 0}
        self.same_eng_sync = True
        self.out_events = []

    def _wait(self, e, ev, raw=True):
        if ev is None:
            return
        key, val, sem = ev
        if key == e.name and (not raw or e.name == "pe" or not self.same_eng_sync):
            return
        if e.seen.get(key, 0) >= val:
            return
        e.eng.wait_ge(sem, val)
        e.seen[key] = val

    def _deps(self, e, reads, writes, extra):
        for b in reads:
            self._wait(e, b.w)
            if b.excl:
                for ev in b.r.values():
                    if ev[0] != e.name:
                        self._wait(e, ev)
        for b in writes:
            self._wait(e, b.w, raw=False)
            for ev in b.r.values():
                self._wait(e, ev, raw=False)
        for ev in extra:
            self._wait(e, ev)

    def _record(self, ev, reads, writes):
        for b in reads:
            old = b.r.get(ev[0])
            if old is None or old[1] < ev[1]:
                b.r[ev[0]] = ev
        for b in writes:
            b.w = ev
            b.r = {}

    def op(self, engname, fn, reads=(), writes=(), extra=(), inc=True):
        e = self.E[engname]
        self._deps(e, reads, writes, extra)
        ins = fn(e.eng)
        if inc:
            e.count += 1
            ins.then_inc(e.sem, 1)
            ev = (e.name, e.count, e.sem)
            self._record(ev, list(reads) + e.pend_r, list(writes) + e.pend_w)
            e.pend_r = []
            e.pend_w = []
            return ev
        e.pend_r.extend(reads)
        e.pend_w.extend(writes)
        return None

    def dma(self, q, out, in_, reads=(), writes=(), extra=(), is_output=False, **kw):
        e = self.E[q]
        pool = self.dma_sems[q]
        i = pool["next"]
        pool["next"] = (i + 1) % len(pool["sems"])
        sem = pool["sems"][i]
        key = f"d_{q}{i}"
        if pool["vals"][i] > 0:
            self._wait(e, (key, pool["vals"][i], sem))
        self._deps(e, reads, writes, extra)
        pool["vals"][i] += 16
        e.eng.dma_start(out=out, in_=in_, **kw).then_inc(sem, 16)
        ev = (key, pool["vals"][i], sem)
        self._record(ev, reads, writes)
        if is_output:
            self.out_events.append(ev)
        return ev

    def barrier(self):
        evs = []
        for name, e in self.E.items():
            if e.count > 0:
                evs.append((name, e.count, e.sem))
        for q, pool in self.dma_sems.items():
            for i, sem in enumerate(pool["sems"]):
                if pool["vals"][i] > 0:
                    evs.append((f"d_{q}{i}", pool["vals"][i], sem))
        for name, e in self.E.items():
            for ev in evs:
                if ev[0] != name:
                    self._wait(e, ev)

    def finish(self):
        e = self.E["sp"]
        for q, pool in self.dma_sems.items():
            for i, sem in enumerate(pool["sems"]):
                if pool["vals"][i] > 0:
                    self._wait(e, (f"d_{q}{i}", pool["vals"][i], sem))


def _tables_A(rel_bias):
    ki = np.arange(128)[:, None, None]
    jp = np.arange(6)[None, :, None]
    qc = np.arange(256)[None, None, :]
    bq = qc // 128
    qi = qc % 128
    j = jp - bq
    rel = 128 * (j - 4) + ki - qi
    idx = np.clip(rel, -128, 128) + 128
    dchunk = 2 * (j - 4) + (ki >= 64).astype(np.int64) - (qi >= 64).astype(np.int64)
    ok = (j >= 0) & (j <= 4) & (dchunk >= -8) & (dchunk <= 0)
    MA = np.where(ok, 0.0, NEG).astype(np.float32)
    GA = np.ascontiguousarray(np.transpose(rel_bias[:, idx], (1, 0, 2, 3))).astype(np.float32)
    return GA, np.ascontiguousarray(MA)


def _tables_B():
    slopes = 2.0 ** (-8.0 * np.arange(1, 5) / 4.0)
    ki = np.arange(128)[:, None]
    c = np.arange(512)[None, :]
    T = np.zeros((128, 4, 2, 512), np.float32)
    for h in range(4):
        b0 = -slopes[h] * np.abs(c - ki)
        b0 = np.where((ki >= 64) & (c < 64), NEG, b0)
        b1 = -slopes[h] * (128 + c - ki)
        T[:, h, 0] = b0
        T[:, h, 1] = b1
    return T, slopes


class Prog:
    def __init__(self, S, NSEQ, do_l0=True, do_l1=True):
        self.S = S
        self.NSEQ = NSEQ
        self.do_l0 = do_l0
        self.do_l1 = do_l1
        self.NTT = S // 128
        self.NTB = S // 512

    def build(self):
        S, NSEQ = self.S, self.NSEQ
        nc = bass.Bass("TRN2", target_bir_lowering=False)
        self.nc = nc
        T = S * NSEQ

        def din(name, shape, dt=F32):
            return nc.dram_tensor(name, list(shape), dt, kind="ExternalInput").ap()

        self.x = din("x", [T, D])
        self.out = nc.dram_tensor("out", [T, D], F32, kind="ExternalOutput").ap()
        self.ident = din("ident", [128, 128])
        if self.do_l0:
            self.w_in0 = din("w_in0", [D, 8192])
            self.w_out0 = din("w_out0", [D, D])
            self.g0 = din("g0", [128, 16])
            self.qkg = din("qkg", [128, 8])
            self.subg = din("subg", [128, 2])
            self.lamv = din("lamv", [128, 4])
            self.GA = din("GA", [128, 8, 6, 256])
            self.MA = din("MA", [128, 6, 256])
            self.TB = din("TB", [128, 4, 2, 512])
        if self.do_l1:
            self.w_in1 = din("w_in1", [D, 4096])
            self.w_out1 = din("w_out1", [D, D])
            self.g1 = din("g1", [128, 16])
            self.cvw_d = din("cvw", [128, 12, 4])
            self.cl_d = din("cl", [128, 4, 12])
            self.lru_wa_d = din("lru_wa", [6, 256, 256])
            self.lru_wx_d = din("lru_wx", [6, 256, 256])
            self.s5in_d = din("s5in", [128, 3, 32])
            self.sgn_d = din("sgn", [128, 1])
            self.Cst_d = din("Cst", [128, 32, 16])
            self.Csw_d = din("Csw", [128, 32, 16])
            self.W1_d = din("W1", [4, 128, 1024])
            self.W2_d = din("W2", [4, 128, 1024])
            self.dsk_d = din("dsk", [128, 8])
            self.wglu_d = din("wglu", [512, 512])
        if self.do_l0 and self.do_l1:
            self.x1 = nc.dram_tensor("x1", [T, D], F32, kind="Internal").ap()
        elif self.do_l0:
            self.x1 = self.out
        else:
            self.x1 = self.x
        self.mixs = nc.dram_tensor("mixs", [NSEQ, KT, 128, S], BF16, kind="Internal").ap()

        with ExitStack() as st:
            self.st = st
            self.block = st.enter_context(nc.Block())
            self.fw = FW(nc, st)
            self.cur = st
            self._alloc_common()
            self._load_consts()
            if self.do_l0:
                with ExitStack() as l0st:
                    self.cur = l0st
                    self._alloc_l0()
                    self._prep_l0()
                    for s in range(NSEQ):
                        self.layer0(s)
                    self.fw.barrier()
                self.cur = st
            if self.do_l1:
                with ExitStack() as l1st:
                    self.cur = l1st
                    self._alloc_l1()
                    self._prep_l1()
                    for s in range(NSEQ):
                        self.layer1(s)
                    self.fw.barrier()
                self.cur = st
            self.fw.finish()
        return nc

    def sb(self, name, shape, dt):
        self._uid = getattr(self, "_uid", 0) + 1
        return self.cur.enter_context(self.nc.sbuf_tensor(f"sb{self._uid}_{name}", list(shape), dt))

    def ps(self, name, shape, dt):
        return self.st.enter_context(self.nc.psum_tensor("ps_" + name, list(shape), dt))

    def _alloc_common(self):
        S = self.S
        self.mixs_b = [Buf(f"mixs{i}") for i in range(KT)]
        self.x1_b = {(s_, t_): Buf(f"x1_{s_}_{t_}") for s_ in range(self.NSEQ) for t_ in range(self.NTT)}
        self.big = self.sb("big", [128, KT, S], BF16)
        self.big_b = [Buf(f"big{i}") for i in range(self.NTB)]
        self.wraw = self.sb("wraw", [128, 4 * KT * 256], BF16)
        self.w_b = [Buf(f"w{i}") for i in range(4)]
        self.epsc = self.sb("epsc", [128, 1], F32)
        self.epsc_b = Buf("epsc")
        self.onec = self.sb("onec", [128, 1], F32)
        self.onec_b = Buf("onec")
        self.col = self.sb("col", [128, 8], F32)
        self.col_b = [Buf(f"col{i}") for i in range(8)]
        self.identb = self.sb("identb", [128, 128], BF16)
        self.identb_b = Buf("identb")
        self.ones = self.sb("ones", [128, 128], BF16)
        self.ones_b = Buf("ones")
        self.onesf = self.sb("onesf", [128, 128], F32)
        self.onesf_b = Buf("onesf")
        self.psS = [self.ps(f"psS{i}", [128, 512], F32) for i in range(2)]
        self.psS_b = [Buf(f"psS{i}", excl=True) for i in range(2)]
        self.psO = [self.ps(f"psO{i}", [128, 512], F32) for i in range(3)]
        self.psO_b = [Buf(f"psO{i}", excl=True) for i in range(3)]
        self.psP = [self.ps(f"psP{i}", [128, 512], F32) for i in range(2)]
        self.psP_b = [Buf(f"psP{i}", excl=True) for i in range(2)]
        self.psT = self.ps("psT", [128, 8, 128], BF16)
        self.psT_b = Buf("psT", excl=True)

    def wslot(self, i):
        return self.wraw[:, i * 4096:(i + 1) * 4096].rearrange("p (k c) -> p k c", c=256)

    def wbig(self, j):
        return self.wraw[:, j * 8192:(j + 1) * 8192].rearrange("p (k c) -> p k c", c=512)

    def _load_consts(self):
        fw = self.fw
        fw.dma("pool", self.identb[:], self.ident, writes=[self.identb_b])
        fw.op("dve", lambda e: e.memset(self.ones[:], 1.0), writes=[self.ones_b])
        fw.op("dve", lambda e: e.memset(self.onesf[:], 1.0), writes=[self.onesf_b])
        fw.op("dve", lambda e: e.memset(self.epsc[:], EPS), writes=[self.epsc_b])
        fw.op("dve", lambda e: e.memset(self.onec[:], 1.0), writes=[self.onec_b])

    def norm_phase(self, xsrc, g16, g16_b, src_bufs=None):
        fw, S = self.fw, self.S
        for tt in range(self.NTT):
            i = tt % 2
            xs, xsb = self.xst[i], self.xst_b[i]
            rd = [src_bufs[tt]] if src_bufs is not None else []
            fw.dma("sp", xs[:, 0:D], xsrc[tt * 128:(tt + 1) * 128, :], reads=rd, writes=[xsb])
            xn, xn_b = self.xn2[tt % 2]
            c3 = 3 * (tt % 2)
            fw.op("act", lambda e: e.activation(out=xn[:, 0:D], in_=xs[:, 0:D], func=AF.Square,
                                                accum_out=self.col[:, c3:c3 + 1]),
                  reads=[xsb], writes=[xn_b, self.col_b[c3]])
            fw.op("act", lambda e: e.activation(out=self.col[:, c3 + 1:c3 + 2], in_=self.col[:, c3:c3 + 1], func=AF.Sqrt,
                                                scale=1.0 / D, bias=self.epsc[:, 0:1]),
                  reads=[self.col_b[c3], self.epsc_b], writes=[self.col_b[c3 + 1]])
            fw.op("dve", lambda e: e.reciprocal(out=self.col[:, c3 + 2:c3 + 3], in_=self.col[:, c3 + 1:c3 + 2]),
                  reads=[self.col_b[c3 + 1]], writes=[self.col_b[c3 + 2]])
            fw.op("act", lambda e: e.activation(out=xn[:, 0:D], in_=xs[:, 0:D], func=AF.Copy,
                                                scale=self.col[:, c3 + 2:c3 + 3]),
                  reads=[xsb, self.col_b[c3 + 2]], writes=[xn_b])
            tb = tt // 4
            for half in range(2):
                for j in range(8):
                    kt = half * 8 + j
                    fw.op("pe", lambda e: e.transpose(out=self.psT[:, j, :], in_=xn[:, kt * 128:(kt + 1) * 128],
                                                      identity=self.identb[:]),
                          reads=[xn_b, self.identb_b], writes=[self.psT_b], inc=(j == 7))
                dst = self.big[:, half * 8:(half + 1) * 8, tt * 128:(tt + 1) * 128]
                gsl = g16[:, half * 8:(half + 1) * 8].unsqueeze(2).to_broadcast([128, 8, 128])
                fw.op("dve", lambda e: e.tensor_tensor(out=dst, in0=self.psT[:], in1=gsl, op=ALU.mult),
                      reads=[self.psT_b, g16_b], writes=[self.big_b[tb]])

    def load_w(self, slot, wsrc, col0):
        src = wsrc[:, col0:col0 + 256].rearrange("(k p) c -> p k c", p=128)
        return self.fw.dma("pool", self.wslot(slot), src, writes=[self.w_b[slot]])

    def proj_fm(self, ps, psb, slot, sel, tb):
        w = self.wslot(slot)
        for kt in range(KT):
            self.fw.op("pe", lambda e: e.matmul(ps[:], lhsT=w[:, kt, sel * 128:(sel + 1) * 128],
                                                rhs=self.big[:, kt, tb * 512:(tb + 1) * 512],
                                                start=(kt == 0), stop=(kt == KT - 1)),
                       reads=[self.w_b[slot], self.big_b[tb]], writes=[psb], inc=(kt == KT - 1))

    def outproj_phase(self, s, wsrc, xsrc, dst, is_output, src_bufs=None):
        fw, S = self.fw, self.S
        for ft in range(KT):
            fw.dma("sp", self.big[:, ft, :], self.mixs[s, ft], reads=[self.mixs_b[ft]],
                   writes=self.big_b)
        steps = [(ch, tt) for ch in range(4) for tt in range(self.NTT)]

        def load_x(k):
            ch, tt = steps[k]
            i = k % 2
            fw.dma("sp", self.xst[i][:, 0:512], xsrc[tt * 128:(tt + 1) * 128, ch * 512:(ch + 1) * 512],
                   reads=([src_bufs[tt]] if src_bufs is not None else []), writes=[self.xst_b[i]])

        load_x(0)
        for k, (ch, tt) in enumerate(steps):
            j = ch % 2
            if tt == 0:
                src = wsrc[:, ch * 512:(ch + 1) * 512].rearrange("(k p) c -> p k c", p=128)
                fw.dma("pool", self.wbig(j), src, writes=[self.w_b[2 * j], self.w_b[2 * j + 1]])
            w = self.wbig(j)
            i = k % 2
            xs, xsb = self.xst[i], self.xst_b[i]
            if k + 1 < len(steps):
                load_x(k + 1)
            pp, ppb = self.psP[i], self.psP_b[i]
            for kt in range(KT):
                fw.op("pe", lambda e: e.matmul(pp[:], lhsT=self.big[:, kt, tt * 128:(tt + 1) * 128],
                                               rhs=w[:, kt, :], start=(kt == 0), stop=(kt == KT - 1)),
                      reads=[self.big_b[tt // 4], self.w_b[2 * j], self.w_b[2 * j + 1]], writes=[ppb],
                      inc=(kt == KT - 1))
            fw.op("dve", lambda e: e.tensor_tensor(out=xs[:, 512:1024], in0=pp[:], in1=xs[:, 0:512], op=ALU.add),
                  reads=[ppb, xsb], writes=[xsb])
            fw.dma("sp", dst[tt * 128:(tt + 1) * 128, ch * 512:(ch + 1) * 512], xs[:, 512:1024],
                   reads=[xsb], writes=([] if is_output else [self.x1_b[(s, tt)]]), is_output=is_output)

    def _alloc_l0(self):
        S = self.S
        self.cbt = self.sb("cbt", [128, 64], F32)
        self.cb_b = Buf("cbt")
        self.xst = [self.sb(f"xst{i}", [128, D], F32) for i in range(2)]
        self.xst_b = [Buf(f"xst{i}") for i in range(2)]
        self.xn = self.sb("xn", [128, D], BF16)
        self.xn_b = Buf("xn")
        self.xnB = self.sb("xnB", [128, D], BF16)
        self.xn2 = [(self.xn, self.xn_b), (self.xnB, Buf("xnB"))]
        self.g16_0 = self.sb("g16_0", [128, 16], F32)
        self.g16_0_b = Buf("g16_0")
        self.qT = self.sb("qT", [128, 2, S], BF16)
        self.kT = self.sb("kT", [128, 2, S], BF16)
        self.vv = self.sb("vv", [128, self.NTT, 256], BF16)
        self.gT = self.sb("gT", [128, 2, S], BF16)
        self.qT_b, self.kT_b, self.vv_b, self.gT_b = Buf("qT"), Buf("kT"), Buf("vv"), Buf("gT")
        self.sq = [self.sb(f"sq{i}", [128, 512], BF16) for i in range(2)]
        self.sq_b = [Buf(f"sq{i}") for i in range(2)]
        self.sd = [self.sb(f"sd{i}", [128, 512], F32) for i in range(2)]
        self.sd_b = [Buf(f"sd{i}") for i in range(2)]
        self.tmp = [self.sb(f"tmp{i}", [128, 512], F32) for i in range(3)]
        self.tmp_b = [Buf(f"tmp{i}") for i in range(3)]
        self.pt = [self.sb(f"pt{i}", [128, 512], BF16) for i in range(4)]
        self.pt_b = [Buf(f"pt{i}") for i in range(4)]
        self.biasA = self.sb("biasA", [128, 6, 256], F32)
        self.biasA_b = Buf("biasA")
        self.maskA = self.sb("maskA", [128, 6, 256], F32)
        self.maskA_b = Buf("maskA")
        self.tabB = self.sb("tabB", [128, 2, 512], F32)
        self.tabB_b = Buf("tabB")
        self.t0 = self.sb("t0", [128, 2, 512], F32)
        self.t0_b = Buf("t0")
        self.dd = self.sb("dd", [128, 2, 512], F32)
        self.dd_b = Buf("dd")
        self.rs = self.sb("rs", [128, 512], F32)
        self.rs_b = Buf("rs")
        self.yst = [self.sb(f"yst{i}", [128, 512], BF16) for i in range(2)]
        self.yst_b = [Buf(f"yst{i}") for i in range(2)]
        self.c0 = self.sb("c0", [128, 16], F32)
        self.c0_b = Buf("c0")
        self.lam4 = self.sb("lam4", [128, 4], F32)
        self.lam4_b = Buf("lam4")
        self.subgc = self.sb("subgc", [128, 2], F32)
        self.subgc_b = Buf("subgc")
        self.yctr = 0

    def _prep_l0(self):
        fw = self.fw
        fw.dma("sp", self.g16_0[:], self.g0, writes=[self.g16_0_b])
        fw.dma("sp", self.c0[:, 0:8], self.qkg, writes=[self.c0_b])
        fw.dma("sp", self.lam4[:], self.lamv, writes=[self.lam4_b])
        fw.dma("sp", self.subgc[:], self.subg, writes=[self.subgc_b])
        fw.dma("sp", self.maskA[:], self.MA, writes=[self.maskA_b])
        c0 = self.c0
        fw.op("dve", lambda e: e.tensor_tensor(out=c0[:, 9:10], in0=self.lam4[:, 0:1], in1=self.lam4[:, 1:2], op=ALU.mult),
              reads=[self.lam4_b], writes=[self.c0_b])
        fw.op("dve", lambda e: e.tensor_tensor(out=c0[:, 10:11], in0=self.lam4[:, 2:3], in1=self.lam4[:, 3:4], op=ALU.mult),
              reads=[self.lam4_b], writes=[self.c0_b])
        pp, ppb = self.psP[0], self.psP_b[0]
        fw.op("pe", lambda e: e.matmul(pp[:, 0:2], lhsT=self.onesf[:], rhs=c0[:, 9:11], start=True, stop=True),
              reads=[self.onesf_b, self.c0_b], writes=[ppb])
        fw.op("act", lambda e: e.activation(out=c0[:, 11:13], in_=pp[:, 0:2], func=AF.Exp),
              reads=[ppb], writes=[self.c0_b])
        fw.op("dve", lambda e: e.scalar_tensor_tensor(out=c0[:, 8:9], in0=c0[:, 12:13], scalar=-0.2, in1=c0[:, 11:12],
                                                      op0=ALU.add, op1=ALU.subtract),
              reads=[self.c0_b], writes=[self.c0_b])
        fw.op("dve", lambda e: e.tensor_scalar(out=self.subgc[:], in0=self.subgc[:], scalar1=0.8, scalar2=None,
                                               op0=ALU.mult),
              reads=[self.subgc_b], writes=[self.subgc_b])

    def run_deferred(self, keep=0):
        q = self.__dict__.setdefault("_defq", [])
        while len(q) > keep:
            q.pop(0)()

    def defer(self, fn):
        self.__dict__.setdefault("_defq", []).append(fn)

    def proj_bank(self):
        k = getattr(self, "_pbk", 0)
        self._pbk = k + 1
        banks = [(self.psP[0], self.psP_b[0]), (self.psP[1], self.psP_b[1]),
                 (self.psO[0], self.psO_b[0]), (self.psO[1], self.psO_b[1])]
        return banks[k % 4]

    def qk_unit(self, slot, sel, gcol, dstT, dst_b):
        fw = self.fw
        for tb in range(self.NTB):
            k = getattr(self, "_qkk", 0)
            self._qkk = k + 1
            i = k % 2
            pp, ppb = self.proj_bank()
            self.proj_fm(pp, ppb, slot, sel, tb)
            fw.op("act", lambda e: e.activation(out=self.sq[i][:], in_=pp[:], func=AF.Square),
                  reads=[ppb], writes=[self.sq_b[i]])
            self.run_deferred()

            def tail(i=i, pp=pp, ppb=ppb, tb=tb, sel=sel, gcol=gcol, dstT=dstT, dst_b=dst_b):
                pq, pqb = self.psS[i], self.psS_b[i]
                fw.op("pe", lambda e: e.matmul(pq[:], lhsT=self.ones[:], rhs=self.sq[i][:], start=True, stop=True),
                      reads=[self.ones_b, self.sq_b[i]], writes=[pqb])
                fw.op("act", lambda e: e.activation(out=self.sd[i][:], in_=pq[:], func=AF.Sqrt, scale=1.0 / 128,
                                                    bias=self.epsc[:, 0:1]),
                      reads=[pqb, self.epsc_b], writes=[self.sd_b[i]])
                fw.op("dve", lambda e: e.reciprocal(out=self.sd[i][:], in_=self.sd[i][:]),
                      reads=[self.sd_b[i]], writes=[self.sd_b[i]])
                fw.op("dve", lambda e: e.scalar_tensor_tensor(out=dstT[:, sel, tb * 512:(tb + 1) * 512], in0=pp[:],
                                                              scalar=self.c0[:, gcol:gcol + 1], in1=self.sd[i][:],
                                                              op0=ALU.mult, op1=ALU.mult),
                      reads=[ppb, self.c0_b, self.sd_b[i]], writes=[dst_b])
            self.defer(tail)

    def gate_unit(self, slot, sel):
        fw = self.fw
        for tb in range(self.NTB):
            pp, ppb = self.proj_bank()
            self.proj_fm(pp, ppb, slot, sel, tb)
            self.run_deferred()
            fw.op("act", lambda e: e.activation(out=self.gT[:, sel, tb * 512:(tb + 1) * 512], in_=pp[:], func=AF.Silu),
                  reads=[ppb], writes=[self.gT_b])

    def v_unit(self, slot):
        fw = self.fw
        w = self.wslot(slot)
        for tt in range(self.NTT):
            i = tt % 2
            pp, ppb = self.proj_bank()
            for kt in range(KT):
                fw.op("pe", lambda e: e.matmul(pp[:, 0:256], lhsT=self.big[:, kt, tt * 128:(tt + 1) * 128],
                                               rhs=w[:, kt, :], start=(kt == 0), stop=(kt == KT - 1)),
                      reads=[self.big_b[tt // 4], self.w_b[slot]], writes=[ppb], inc=(kt == KT - 1))
            self.run_deferred()
            if i == 0:
                fw.op("act", lambda e: e.activation(out=self.vv[:, tt, :], in_=pp[:, 0:256], func=AF.Copy),
                      reads=[ppb], writes=[self.vv_b])
            else:
                fw.op("dve", lambda e: e.tensor_copy(out=self.vv[:, tt, :], in_=pp[:, 0:256]),
                      reads=[ppb], writes=[self.vv_b])

    def spill_y(self, s, ft, col0, ncols, yi):
        self.fw.dma("sp", self.mixs[s, ft, :, col0:col0 + ncols], self.yst[yi][:, 0:ncols],
                    reads=[self.yst_b[yi]], writes=[self.mixs_b[ft]])

    def attn_A(self, s, pair):
        fw, S = self.fw, self.S
        scale = 128.0 ** -0.5
        for hh in range(2):
            h = 2 * pair + hh
            fw.dma("sp", self.biasA[:], self.GA[:, h], writes=[self.biasA_b])
            fw.op("dve", lambda e: e.tensor_tensor(out=self.biasA[:], in0=self.biasA[:], in1=self.maskA[:], op=ALU.add),
                  reads=[self.biasA_b, self.maskA_b], writes=[self.biasA_b])
            cnt = 0
            for Bq in range(S // 256):
                q0 = Bq * 256
                kts = [(jp, 2 * Bq - 4 + jp) for jp in range(6) if 2 * Bq - 4 + jp >= 0]
                ab = (Bq % 2)
                po, pob = (self.psO[0], self.psO_b[0]) if ab == 0 else (self.psO[2], self.psO_b[2])
                psm, psmb = (self.psO[1], self.psO_b[1]) if ab == 0 else (self.psP[0], self.psP_b[0])
                for n, (jp, kt) in enumerate(kts):
                    i = cnt % 4
                    i3 = cnt % 3
                    cnt += 1
                    pS, pSb = self.sbank(i3)
                    fw.op("pe", lambda e: e.matmul(pS[:, 0:256], lhsT=self.kT[:, hh, kt * 128:(kt + 1) * 128],
                                                   rhs=self.qT[:, hh, q0:q0 + 256], start=True, stop=True),
                          reads=[self.kT_b, self.qT_b], writes=[pSb])
                    fw.op("dve", lambda e: e.scalar_tensor_tensor(out=self.tmp[i3][:, 0:256], in0=pS[:, 0:256], scalar=scale,
                                                                  in1=self.biasA[:, jp, :], op0=ALU.mult, op1=ALU.add),
                          reads=[pSb, self.biasA_b], writes=[self.tmp_b[i3]])
                    fw.op("act", lambda e: e.activation(out=self.pt[i][:, 0:256], in_=self.tmp[i3][:, 0:256], func=AF.Exp),
                          reads=[self.tmp_b[i3]], writes=[self.pt_b[i]])
                    self.run_deferred(keep=2)
                    first, last = (n == 0), (n == len(kts) - 1)

                    def tail(i=i, kt=kt, first=first, last=last, po=po, pob=pob, psm=psm, psmb=psmb, q0=q0, hh=hh, h=h):
                        fw.op("pe", lambda e: e.matmul(po[:, 0:256], lhsT=self.vv[:, kt, hh * 128:(hh + 1) * 128],
                                                       rhs=self.pt[i][:, 0:256], start=first, stop=last),
                              reads=[self.vv_b, self.pt_b[i]], writes=[pob], inc=False)
                        fw.op("pe", lambda e: e.matmul(psm[:, 0:256], lhsT=self.ones[:], rhs=self.pt[i][:, 0:256],
                                                       start=first, stop=last),
                              reads=[self.ones_b, self.pt_b[i]], writes=[psmb])
                        if not last:
                            return
                        yi = self.yctr % 2
                        self.yctr += 1
                        fw.op("dve", lambda e: e.reciprocal(out=self.rs[:, 0:256], in_=psm[:, 0:256]),
                              reads=[psmb], writes=[self.rs_b])
                        fw.op("dve", lambda e: e.tensor_tensor(out=self.rs[:, 256:512], in0=po[:, 0:256], in1=self.rs[:, 0:256], op=ALU.mult),
                              reads=[pob, self.rs_b], writes=[self.rs_b])
                        fw.op("dve", lambda e: e.tensor_tensor(out=self.yst[yi][:, 0:256], in0=self.rs[:, 256:512],
                                                               in1=self.gT[:, hh, q0:q0 + 256], op=ALU.mult),
                              reads=[self.rs_b, self.gT_b], writes=[self.yst_b[yi]])
                        self.spill_y(s, h, q0, 256, yi)
                    self.defer(tail)
            self.run_deferred()

    def attn_B(self, s, h, slopes):
        fw, S = self.fw, self.S
        scale = 128.0 ** -0.5
        fw.dma("sp", self.tabB[:], self.TB[:, h], writes=[self.tabB_b])
        cnt = 0
        for Q in range(S // 512):
            q0 = Q * 512
            nkt = 4 * Q + 4
            for c in range(2):
                for kt in range(nkt):
                    i = cnt % 4
                    i3 = cnt % 3
                    cnt += 1
                    pS, pSb = self.sbank(i3)
                    if kt < 4 * Q:
                        lo = 0
                        strip = self.tabB[:, 1, :]
                        cb = float(-slopes[h] * 128.0 * (4 * Q - kt - 1))
                    else:
                        lo = 128 * (kt - 4 * Q)
                        strip = self.tabB[:, 0, 0:512 - lo]
                        cb = 0.0
                    fw.op("pe", lambda e: e.matmul(pS[:, lo:512], lhsT=self.kT[:, c, kt * 128:(kt + 1) * 128],
                                                   rhs=self.qT[:, c, q0 + lo:q0 + 512], start=True, stop=True),
                          reads=[self.kT_b, self.qT_b], writes=[pSb])
                    fw.op("dve", lambda e: e.scalar_tensor_tensor(out=self.tmp[i3][:, lo:512], in0=pS[:, lo:512], scalar=scale,
                                                                  in1=strip, op0=ALU.mult, op1=ALU.add),
                          reads=[pSb, self.tabB_b], writes=[self.tmp_b[i3]])
                    if cb != 0.0:
                        cbi = self.cbias_col(cb)
                        fw.op("act", lambda e: e.activation(out=self.pt[i][:, lo:512], in_=self.tmp[i3][:, lo:512], func=AF.Exp,
                                                            bias=cbi),
                              reads=[self.tmp_b[i3], self.cb_b], writes=[self.pt_b[i]])
                    else:
                        fw.op("act", lambda e: e.activation(out=self.pt[i][:, lo:512], in_=self.tmp[i3][:, lo:512], func=AF.Exp),
                              reads=[self.tmp_b[i3]], writes=[self.pt_b[i]])
                    self.run_deferred(keep=2)
                    first, last = (kt == 0), (kt == nkt - 1)

                    def tail(i=i, kt=kt, lo=lo, first=first, last=last, c=c, q0=q0, Q=Q):
                        for sl in range(2):
                            fw.op("pe", lambda e: e.matmul(self.psO[sl][:, lo:512], lhsT=self.vv[:, kt, sl * 128:(sl + 1) * 128],
                                                           rhs=self.pt[i][:, lo:512], start=first, stop=last),
                                  reads=[self.vv_b, self.pt_b[i]], writes=[self.psO_b[sl]], inc=False)
                        fw.op("pe", lambda e: e.matmul(self.psO[2][:, lo:512], lhsT=self.ones[:], rhs=self.pt[i][:, lo:512],
                                                       start=first, stop=last),
                              reads=[self.ones_b, self.pt_b[i]], writes=[self.psO_b[2]])
                        if last:
                            self.B_epilogue(s, h, c, q0)
                    self.defer(tail)
        self.run_deferred()

    def sbank(self, i):
        return [(self.psS[0], self.psS_b[0]), (self.psS[1], self.psS_b[1]), (self.psP[1], self.psP_b[1])][i]

    def cbias_col(self, cb):
        if not hasattr(self, "_cbcols"):
            self._cbcols = {}
        if cb not in self._cbcols:
            j = len(self._cbcols)
            assert j < self.cbt.shape[1]
            self.fw.op("pool", lambda e: e.memset(self.cbt[:, j:j + 1], cb), writes=[self.cb_b])
            self._cbcols[cb] = j
        j = self._cbcols[cb]
        return self.cbt[:, j:j + 1]

    def B_epilogue(self, s, h, c, q0):
        fw = self.fw
        fw.op("dve", lambda e: e.reciprocal(out=self.rs[:], in_=self.psO[2][:]),
              reads=[self.psO_b[2]], writes=[self.rs_b])
        for sl in range(2):
            if c == 0:
                fw.op("dve", lambda e: e.tensor_tensor(out=self.t0[:, sl, :], in0=self.psO[sl][:], in1=self.rs[:], op=ALU.mult),
                      reads=[self.psO_b[sl], self.rs_b], writes=[self.t0_b])
            else:
                fw.op("dve", lambda e: e.tensor_tensor(out=self.dd[:, sl, :], in0=self.psO[sl][:], in1=self.rs[:], op=ALU.mult),
                      reads=[self.psO_b[sl], self.rs_b], writes=[self.dd_b])
                fw.op("dve", lambda e: e.scalar_tensor_tensor(out=self.dd[:, sl, :], in0=self.dd[:, sl, :],
                                                              scalar=self.c0[:, 8:9], in1=self.t0[:, sl, :],
                                                              op0=ALU.mult, op1=ALU.add),
                      reads=[self.dd_b, self.c0_b, self.t0_b], writes=[self.dd_b])
        if c == 0:
            return
        pq, pqb = self.psP[0], self.psP_b[0]
        for sl in range(2):
            fw.op("act", lambda e: e.activation(out=self.sq[sl][:], in_=self.dd[:, sl, :], func=AF.Square),
                  reads=[self.dd_b], writes=[self.sq_b[sl]])
            fw.op("pe", lambda e: e.matmul(pq[:], lhsT=self.ones[:], rhs=self.sq[sl][:], start=(sl == 0), stop=(sl == 1)),
                  reads=[self.ones_b, self.sq_b[sl]], writes=[pqb], inc=(sl == 1))
        fw.op("act", lambda e: e.activation(out=self.sd[0][:], in_=pq[:], func=AF.Sqrt, scale=1.0 / 256,
                                            bias=self.epsc[:, 0:1]),
              reads=[pqb, self.epsc_b], writes=[self.sd_b[0]])
        fw.op("dve", lambda e: e.reciprocal(out=self.sd[0][:], in_=self.sd[0][:]),
              reads=[self.sd_b[0]], writes=[self.sd_b[0]])
        for sl in range(2):
            yi = self.yctr % 2
            self.yctr += 1
            fw.op("dve", lambda e: e.scalar_tensor_tensor(out=self.dd[:, sl, :], in0=self.dd[:, sl, :],
                                                          scalar=self.subgc[:, sl:sl + 1], in1=self.sd[0][:],
                                                          op0=ALU.mult, op1=ALU.mult),
                  reads=[self.dd_b, self.subgc_b, self.sd_b[0]], writes=[self.dd_b])
            fw.op("dve", lambda e: e.tensor_tensor(out=self.yst[yi][:], in0=self.dd[:, sl, :],
                                                   in1=self.gT[:, sl, q0:q0 + 512], op=ALU.mult),
                  reads=[self.dd_b, self.gT_b], writes=[self.yst_b[yi]])
            self.spill_y(s, 8 + 2 * h + sl, q0, 512, yi)

    def layer0(self, s):
        S = self.S
        xsrc = self.x[s * S:(s + 1) * S, :]
        self.norm_phase(xsrc, self.g16_0, self.g16_0_b)
        _, slopes = _tables_B()
        groups = [("A", i) for i in range(4)] + [("B", h) for h in range(4)]
        offs = {"A": (0, 1024, 2048, 3072), "B": (4096, 5120, 6144, 7168)}
        gcols = {"A": (0, 2), "B": (4, 6)}

        def issue_loads(g):
            kind, idx = groups[g]
            for u in range(4):
                self.load_w(u, self.w_in0, offs[kind][u] + idx * 256)

        issue_loads(0)
        for g, (kind, idx) in enumerate(groups):
            gq, gk = gcols[kind]
            for sel in range(2):
                self.qk_unit(0, sel, gq + sel, self.qT, self.qT_b)
            for sel in range(2):
                self.qk_unit(1, sel, gk + sel, self.kT, self.kT_b)
            self.v_unit(2)
            for sel in range(2):
                self.gate_unit(3, sel)
            if g + 1 < len(groups):
                issue_loads(g + 1)
            if kind == "A":
                self.attn_A(s, idx)
            else:
                self.attn_B(s, idx, slopes)
        self.run_deferred()
        dst = self.x1[s * S:(s + 1) * S, :]
        self.outproj_phase(s, self.w_out0, xsrc, dst, is_output=(not self.do_l1))


    def _alloc_l1(self):
        S = self.S
        FWID = max(S, D) + 8
        self.F = [self.sb(f"F{i}", [128, FWID], F32) for i in range(6)]
        self.F_b = [Buf(f"F{i}") for i in range(6)]
        self.H = [self.sb(f"H{i}", [128, max(S, D)], BF16) for i in range(7)]
        self.H_b = [Buf(f"H{i}") for i in range(7)]
        self.xst = [self.F[0], self.F[1]]
        self.xst_b = [self.F_b[0], self.F_b[1]]
        self.xn = self.H[0]
        self.xn_b = self.H_b[0]
        self.xn2 = [(self.H[0], self.H_b[0]), (self.H[1], self.H_b[1])]
        self.t5 = [self.sb(f"t5_{i}", [128, 512], F32) for i in range(3)]
        self.t5_b = [Buf(f"t5_{i}") for i in range(3)]
        self.yst = [self.sb(f"yst{i}", [128, 512], BF16) for i in range(2)]
        self.yst_b = [Buf(f"yst{i}") for i in range(2)]
        self.yctr = 0
        self.g16_1 = self.sb("g16_1", [128, 16], F32)
        self.g16_1_b = Buf("g16_1")
        self.cvw = self.sb("cvw", [128, 12, 4], F32)
        self.cl = self.sb("cl", [128, 8, 12], F32)
        self.cl_b = Buf("cl")
        self.wax = self.sb("wax", [128, 2, 2, 256], BF16)
        self.wax_b = Buf("wax")
        self.s5in = self.sb("s5in", [128, 3, 32], F32)
        self.s5w = self.F[4][:, 0:768].rearrange("p (r g) -> p r g", g=32)
        self.s5_b = Buf("s5")
        self.sgn = self.sb("sgn", [128, 1], F32)
        self.csK = self.sb("csK", [128, 12, 32], F32)
        self.snK = self.sb("snK", [128, 12, 32], F32)
        self.Cst = self.F[2][:, 0:512].rearrange("p (g h) -> p g h", h=16)
        self.Csw = self.F[3][:, 0:512].rearrange("p (g h) -> p g h", h=16)
        self.absA = self.sb("absA", [128, 32], F32)
        self.LA = self.sb("LA", [128, 32, 16], F32)
        self.LB = self.sb("LB", [128, 32, 16], F32)
        self.LT = self.t5[2][:, :].rearrange("p (g h) -> p g h", h=16)
        self.LAp = self.sb("LAp", [128, 1152], BF16)
        self.LBp = self.sb("LBp", [128, 1152], BF16)
        self.Lp_b = Buf("Lp")
        self.W12 = self.sb("W12", [128, 2, 1024], BF16)
        self.W12_b = Buf("W12")
        self.dsk = self.sb("dsk", [128, 8], F32)
        self.dsk_b = Buf("dsk")
        self.wglu = self.sb("wglu", [128, 4, 512], BF16)
        self.wglu_b = Buf("wglu")

    def _prep_l1(self):
        fw = self.fw
        fw.dma("sp", self.g16_1[:], self.g1, writes=[self.g16_1_b])
        fw.dma("sp", self.cvw[:], self.cvw_d, writes=[self.cl_b])
        fw.dma("sp", self.cl[:, 0:4, :], self.cl_d, writes=[self.cl_b])
        fw.dma("sp", self.s5in[:], self.s5in_d, writes=[self.s5_b])
        fw.dma("sp", self.sgn[:], self.sgn_d, writes=[self.s5_b])
        fw.dma("sp", self.Cst, self.Cst_d, writes=[self.s5_b])
        fw.dma("sp", self.Csw, self.Csw_d, writes=[self.s5_b])
        fw.dma("sp", self.dsk[:], self.dsk_d, writes=[self.dsk_b])
        fw.dma("pool", self.wglu[:], self.wglu_d.rearrange("(k p) c -> p k c", p=128), writes=[self.wglu_b])
        fw.op("dve", lambda e: e.memset(self.LAp[:], 0.0), writes=[self.Lp_b])
        fw.op("dve", lambda e: e.memset(self.LBp[:], 0.0), writes=[self.Lp_b])
        cl, clb = self.cl, [self.cl_b]
        PI = math.pi

        def act(out, in_, func, b, scale=1.0, bias=None):
            kw = {}
            rd = list(b)
            if bias is not None:
                kw["bias"] = bias
                rd.append(self.onec_b)
            fw.op("act", lambda e: e.activation(out=out, in_=in_, func=func, scale=scale, **kw), reads=rd, writes=b)

        def ts(out, in0, s1, op0, b, s2=None, op1=None):
            if op1 is None:
                fw.op("dve", lambda e: e.tensor_scalar(out=out, in0=in0, scalar1=s1, scalar2=None, op0=op0), reads=b, writes=b)
            else:
                fw.op("dve", lambda e: e.tensor_scalar(out=out, in0=in0, scalar1=s1, scalar2=s2, op0=op0, op1=op1), reads=b, writes=b)

        def tt(out, in0, in1, op, b):
            fw.op("dve", lambda e: e.tensor_tensor(out=out, in0=in0, in1=in1, op=op), reads=b, writes=b)

        lam, z, w, cc, cc2, t6, t7 = cl[:, 3, :], cl[:, 4, :], cl[:, 5, :], cl[:, 4, :], cl[:, 5, :], cl[:, 6, :], cl[:, 7, :]
        act(t6, lam, AF.Exp, clb, scale=-1.0)
        ts(t7, t6, 1.0, ALU.add, clb)
        act(w, t7, AF.Ln, clb)
        ts(t7, t7, -1.0, ALU.add, clb, 1e-30, ALU.max)
        fw.op("dve", lambda e: e.reciprocal(out=t7, in_=t7), reads=clb, writes=clb)
        tt(t7, t7, t6, ALU.mult, clb)
        tt(t7, t7, w, ALU.mult, clb)
        ts(cc, t7, -8.0, ALU.mult, clb)
        ts(cc2, t7, -16.0, ALU.mult, clb)

        sw, sb_ = self.s5w, [self.s5_b]
        are, aim, ldt = self.s5in[:, 0, :], self.s5in[:, 1, :], self.s5in[:, 2, :]
        R = lambda i: sw[:, i, :]
        dt, th, ea, y1, y2, tq, sn, cs, nr, ni, den, cr, ci, kA1, nci, kB2, nrm = [R(i) for i in range(17)]
        act(dt, ldt, AF.Exp, sb_)
        tt(th, aim, dt, ALU.mult, sb_)
        tt(tq, are, dt, ALU.mult, sb_)
        act(ea, tq, AF.Exp, sb_)

        def reduce_to(dst, add):
            ts(dst, th, add, ALU.add, sb_)
            ts(R(17), dst, 0.0, ALU.add, sb_)
            for m in range(1, 12):
                ts(tq, R(17), (2 * m - 1) * PI, ALU.is_gt, sb_, -2.0 * PI, ALU.mult)
                tt(dst, dst, tq, ALU.add, sb_)
            ts(dst, dst, -3.141592, ALU.max, sb_, 3.141592, ALU.min)

        reduce_to(y1, 0.0)
        reduce_to(y2, PI / 2)
        act(sn, y1, AF.Sin, sb_)
        act(cs, y2, AF.Sin, sb_)
        tt(nr, ea, cs, ALU.mult, sb_)
        ts(nr, nr, -1.0, ALU.add, sb_)
        tt(ni, ea, sn, ALU.mult, sb_)
        tt(den, are, are, ALU.mult, sb_)
        tt(tq, aim, aim, ALU.mult, sb_)
        tt(den, den, tq, ALU.add, sb_)
        fw.op("dve", lambda e: e.reciprocal(out=den, in_=den), reads=sb_, writes=sb_)
        tt(cr, nr, are, ALU.mult, sb_)
        tt(tq, ni, aim, ALU.mult, sb_)
        tt(cr, cr, tq, ALU.add, sb_)
        tt(cr, cr, den, ALU.mult, sb_)
        tt(ci, ni, are, ALU.mult, sb_)
        tt(tq, nr, aim, ALU.mult, sb_)
        tt(ci, ci, tq, ALU.subtract, sb_)
        tt(ci, ci, den, ALU.mult, sb_)
        fw.op("dve", lambda e: e.tensor_scalar(out=kA1, in0=cr, scalar1=self.sgn[:, 0:1], scalar2=None, op0=ALU.mult), reads=sb_, writes=sb_)
        ts(nci, ci, -1.0, ALU.mult, sb_)
        ts(kB2, kA1, -1.0, ALU.mult, sb_)
        bc = lambda a: a.unsqueeze(2).to_broadcast([128, 32, 16])
        tt(self.LA[:], self.Cst, bc(kA1), ALU.mult, sb_)
        tt(self.LT, self.Csw, bc(nci), ALU.mult, sb_)
        tt(self.LA[:], self.LA[:], self.LT, ALU.add, sb_)
        tt(self.LB[:], self.Cst, bc(nci), ALU.mult, sb_)
        tt(self.LT, self.Csw, bc(kB2), ALU.mult, sb_)
        tt(self.LB[:], self.LB[:], self.LT, ALU.add, sb_)
        c0, s0 = self.csK[:, 0, :], self.snK[:, 0, :]
        fw.op("dve", lambda e: e.tensor_scalar(out=s0, in0=sn, scalar1=self.sgn[:, 0:1], scalar2=None, op0=ALU.mult), reads=sb_, writes=sb_)
        tt(nrm, cs, cs, ALU.mult, sb_)
        tt(tq, sn, sn, ALU.mult, sb_)
        tt(nrm, nrm, tq, ALU.add, sb_)
        act(nrm, nrm, AF.Sqrt, sb_)
        fw.op("dve", lambda e: e.reciprocal(out=nrm, in_=nrm), reads=sb_, writes=sb_)
        tt(c0, cs, nrm, ALU.mult, sb_)
        tt(s0, s0, nrm, ALU.mult, sb_)
        self.NLEV = int(round(math.log2(self.S)))
        for k in range(self.NLEV - 1):
            ck, sk = self.csK[:, k, :], self.snK[:, k, :]
            cn, sn_ = self.csK[:, k + 1, :], self.snK[:, k + 1, :]
            tt(cn, ck, ck, ALU.mult, sb_)
            tt(tq, sk, sk, ALU.mult, sb_)
            tt(cn, cn, tq, ALU.subtract, sb_)
            tt(sn_, ck, sk, ALU.mult, sb_)
            ts(sn_, sn_, 2.0, ALU.mult, sb_)
        ts(self.absA[:], ea, 0.0, ALU.add, sb_)
        fw.barrier()

    def mixer_C(self, s, n):
        fw, S = self.fw, self.S
        F, Fb, H, Hb = self.F, self.F_b, self.H, self.H_b
        cinp, cinp_b = F[0], Fb[0]
        U, U_b = [F[1], F[2]], [Fb[1], Fb[2]]
        r, r_b, ii, ii_b, a, a_b = F[3], Fb[3], F[4], Fb[4], F[5], Fb[5]
        UB, UB_b = [H[1], H[2]], [Hb[1], Hb[2]]
        yb, yb_b = H[3], Hb[3]
        cl, clb = self.cl, self.cl_b
        import os
        kc = int(os.environ.get("KC", "99"))
        if kc not in (-1, -4):
          fw.dma("pool", self.wax[:, 0], self.lru_wa_d[n].rearrange("(k p) j -> p k j", p=128), writes=[self.wax_b])
          fw.dma("pool", self.wax[:, 1], self.lru_wx_d[n].rearrange("(k p) j -> p k j", p=128), writes=[self.wax_b])
        if kc not in (-2, -4):
            fw.op("dve", lambda e: e.memset(cinp[:, 0:3], 0.0), writes=[cinp_b])
        if kc in (-3, -4):
            return
        for j in range(2):
            ct = 2 * n + j
            for tb in range(self.NTB):
                i = tb % 2
                pp, ppb = self.psP[i], self.psP_b[i]
                self.proj_fm(pp, ppb, 0, j, tb)
                fw.op("act", lambda e: e.activation(out=cinp[:, 3 + tb * 512:3 + (tb + 1) * 512], in_=pp[:], func=AF.Copy),
                      reads=[ppb], writes=[cinp_b])
            u = U[j]
            if kc < 1:
                continue
            fw.op("dve", lambda e: e.tensor_scalar(out=u[:, 0:S], in0=cinp[:, 0:S], scalar1=self.cvw[:, ct, 0:1],
                                                   scalar2=cl[:, 0, ct:ct + 1], op0=ALU.mult, op1=ALU.add),
                  reads=[cinp_b, clb], writes=[U_b[j]])
            if kc < 2:
                continue
            for tap in range(1, 4):
                fw.op("dve", lambda e: e.scalar_tensor_tensor(out=u[:, 0:S], in0=cinp[:, tap:tap + S],
                                                              scalar=self.cvw[:, ct, tap:tap + 1], in1=u[:, 0:S],
                                                              op0=ALU.mult, op1=ALU.add),
                      reads=[cinp_b, clb, U_b[j]], writes=[U_b[j]])
            fw.op("act", lambda e: e.activation(out=UB[j][:, 0:S], in_=u[:, 0:S], func=AF.Copy),
                  reads=[U_b[j]], writes=[UB_b[j]])
        if kc < 3:
            return
        for jo in range(2):
            ct = 2 * n + jo
            for gi, (dst, dst_b, brow) in enumerate(((r, r_b, 1), (ii, ii_b, 2))):
                for tb in range(self.NTB):
                    i = tb % 2
                    pp, ppb = self.psS[i], self.psS_b[i]
                    for k in range(2):
                        fw.op("pe", lambda e: e.matmul(pp[:], lhsT=self.wax[:, gi, k, jo * 128:(jo + 1) * 128],
                                                       rhs=UB[k][:, tb * 512:(tb + 1) * 512], start=(k == 0), stop=(k == 1)),
                              reads=[self.wax_b, UB_b[k]], writes=[ppb], inc=(k == 1))
                    fw.op("act", lambda e: e.activation(out=dst[:, tb * 512:(tb + 1) * 512], in_=pp[:], func=AF.Sigmoid,
                                                        bias=cl[:, brow, ct:ct + 1]),
                          reads=[ppb, clb], writes=[dst_b])
            if kc < 4:
                continue
            fw.op("act", lambda e: e.activation(out=a[:, 0:S], in_=r[:, 0:S], func=AF.Exp, scale=cl[:, 4, ct:ct + 1]),
                  reads=[r_b, clb], writes=[a_b])
            fw.op("act", lambda e: e.activation(out=r[:, 0:S], in_=r[:, 0:S], func=AF.Exp, scale=cl[:, 5, ct:ct + 1]),
                  reads=[r_b, clb], writes=[r_b])
            fw.op("act", lambda e: e.activation(out=r[:, 0:S], in_=r[:, 0:S], func=AF.Sqrt, scale=-1.0, bias=self.onec[:, 0:1]),
                  reads=[r_b, self.onec_b], writes=[r_b])
            if kc < 5:
                continue
            fw.op("dve", lambda e: e.tensor_tensor(out=ii[:, 0:S], in0=ii[:, 0:S], in1=U[jo][:, 0:S], op=ALU.mult),
                  reads=[ii_b, U_b[jo]], writes=[ii_b])
            fw.op("dve", lambda e: e.tensor_tensor(out=ii[:, 0:S], in0=ii[:, 0:S], in1=r[:, 0:S], op=ALU.mult),
                  reads=[ii_b, r_b], writes=[ii_b])
            fw.op("dve", lambda e: e.tensor_tensor_scan(out=r[:, 0:S], data0=a[:, 0:S], data1=ii[:, 0:S], initial=0.0,
                                                        op0=ALU.mult, op1=ALU.add),
                  reads=[a_b, ii_b, r_b], writes=[r_b])
            for tb in range(self.NTB):
                i = tb % 2
                pp, ppb = self.psP[i], self.psP_b[i]
                self.proj_fm(pp, ppb, 1, jo, tb)
                fw.op("act", lambda e: e.activation(out=a[:, tb * 512:(tb + 1) * 512], in_=pp[:], func=AF.Silu),
                      reads=[ppb], writes=[a_b])
            fw.op("dve", lambda e: e.tensor_tensor(out=yb[:, 0:S], in0=r[:, 0:S], in1=a[:, 0:S], op=ALU.mult),
                  reads=[r_b, a_b], writes=[yb_b])
            fw.dma("sp", self.mixs[s, ct], yb[:, 0:S], reads=[yb_b], writes=[self.mixs_b[ct]])

    def mixer_D_tile(self, s, ct):
        fw, S = self.fw, self.S
        F, Fb, H, Hb = self.F, self.F_b, self.H, self.H_b
        dinf, dinf_b = F[0], Fb[0]
        COS, COS_b, SIN, SIN_b = F[1], Fb[1], F[2], Fb[2]
        bu, bu_b, Rr, Rr_b = F[3], Fb[3], F[4], Fb[4]
        dinb, dinb_b = H[1], Hb[1]
        Zc, Zc_b, Zs, Zs_b = H[2], Hb[2], H[5], Hb[5]
        gel, gel_b = self.gelb[ct], self.gelb_b[ct]
        t5, t5b = self.t5, self.t5_b
        yacc = [self.psO[0], self.psO[1], self.psO[2], self.psS[1]]
        yacc_b = [self.psO_b[0], self.psO_b[1], self.psO_b[2], self.psS_b[1]]
        import os
        for tb in range(self.NTB):
            i = tb % 2
            pp, ppb = self.psP[i], self.psP_b[i]
            ke = os.environ.get("KE", "")
            self.proj_fm(pp, ppb, 0 if "s" in ke else 2, ct % 2, tb)
            if "c" not in ke:
                fw.op("act", lambda e: e.activation(out=dinf[:, tb * 512:(tb + 1) * 512], in_=pp[:], func=AF.Copy),
                      reads=[ppb], writes=[dinf_b])
            if "d" not in ke:
                fw.op("dve", lambda e: e.tensor_copy(out=dinb[:, tb * 512:(tb + 1) * 512], in_=pp[:]),
                      reads=[ppb], writes=[dinb_b])
        import os
        kd = int(os.environ.get("KD", "99"))
        if kd < 1:
            return
        fw.dma("pool", self.W12[:, 0, :], self.W1_d[ct], writes=[self.W12_b])
        fw.dma("pool", self.W12[:, 1, :], self.W2_d[ct], writes=[self.W12_b])
        if kd < 2:
            return
        diagA = self.LAp[:, 0:8 * 144].rearrange("p (g c) -> p g c", c=144)[:, :, 0:16]
        diagB = self.LBp[:, 0:8 * 144].rearrange("p (g c) -> p g c", c=144)[:, :, 0:16]
        fw.op("dve", lambda e: e.tensor_copy(out=diagA, in_=self.LA[:, 8 * ct:8 * ct + 8, :]),
              reads=[self.s5_b], writes=[self.Lp_b])
        fw.op("dve", lambda e: e.tensor_copy(out=diagB, in_=self.LB[:, 8 * ct:8 * ct + 8, :]),
              reads=[self.s5_b], writes=[self.Lp_b])
        if kd < 3:
            return
        for gl in range(8):
            g = 8 * ct + gl
            fw.op("dve", lambda e: e.memset(COS[:, 0:1], 1.0), writes=[COS_b])
            fw.op("dve", lambda e: e.memset(SIN[:, 0:1], 0.0), writes=[SIN_b])
            for k in range(self.NLEV):
                nn = 1 << k
                ck = self.csK[:, k, g:g + 1]
                sk = self.snK[:, k, g:g + 1]
                fw.op("dve", lambda e: e.tensor_scalar(out=Rr[:, 0:nn], in0=SIN[:, 0:nn], scalar1=sk, scalar2=None, op0=ALU.mult),
                      reads=[SIN_b, self.s5_b], writes=[Rr_b])
                fw.op("dve", lambda e: e.tensor_scalar(out=bu[:, 0:nn], in0=COS[:, 0:nn], scalar1=sk, scalar2=None, op0=ALU.mult),
                      reads=[COS_b, self.s5_b], writes=[bu_b])
                fw.op("dve", lambda e: e.scalar_tensor_tensor(out=COS[:, nn:2 * nn], in0=COS[:, 0:nn], scalar=ck, in1=Rr[:, 0:nn],
                                                              op0=ALU.mult, op1=ALU.subtract),
                      reads=[COS_b, Rr_b, self.s5_b], writes=[COS_b])
                fw.op("dve", lambda e: e.scalar_tensor_tensor(out=SIN[:, nn:2 * nn], in0=SIN[:, 0:nn], scalar=ck, in1=bu[:, 0:nn],
                                                              op0=ALU.mult, op1=ALU.add),
                      reads=[SIN_b, bu_b, self.s5_b], writes=[SIN_b])
            if kd < 4:
                continue
            for tb in range(self.NTB):
                sl = slice(tb * 512, (tb + 1) * 512)
                p1, p1b, p2, p2b = self.psP[0], self.psP_b[0], self.psP[1], self.psP_b[1]
                fw.op("pe", lambda e: e.matmul(p1[:], lhsT=self.W12[:, 0, gl * 128:(gl + 1) * 128], rhs=dinb[:, sl], start=True, stop=True),
                      reads=[self.W12_b, dinb_b], writes=[p1b])
                fw.op("pe", lambda e: e.matmul(p2[:], lhsT=self.W12[:, 1, gl * 128:(gl + 1) * 128], rhs=dinb[:, sl], start=True, stop=True),
                      reads=[self.W12_b, dinb_b], writes=[p2b])
                fw.op("dve", lambda e: e.tensor_tensor(out=bu[:, sl], in0=p1[:], in1=COS[:, sl], op=ALU.mult),
                      reads=[p1b, COS_b], writes=[bu_b])
                fw.op("dve", lambda e: e.tensor_tensor(out=t5[0][:], in0=p2[:], in1=SIN[:, sl], op=ALU.mult),
                      reads=[p2b, SIN_b], writes=[t5b[0]])
                fw.op("dve", lambda e: e.tensor_tensor(out=bu[:, sl], in0=bu[:, sl], in1=t5[0][:], op=ALU.add),
                      reads=[bu_b, t5b[0]], writes=[bu_b])
            if kd < 5:
                continue
            fw.op("dve", lambda e: e.tensor_tensor_scan(out=Rr[:, 0:S], data0=self.absA[:, g:g + 1].to_broadcast([128, S]),
                                                        data1=bu[:, 0:S], initial=0.0, op0=ALU.mult, op1=ALU.add),
                  reads=[self.s5_b, bu_b, Rr_b], writes=[Rr_b])
            fw.op("dve", lambda e: e.tensor_tensor(out=Zc[:, 0:S], in0=Rr[:, 0:S], in1=COS[:, 0:S], op=ALU.mult),
                  reads=[Rr_b, COS_b], writes=[Zc_b])
            fw.op("dve", lambda e: e.tensor_tensor(out=Zs[:, 0:S], in0=Rr[:, 0:S], in1=SIN[:, 0:S], op=ALU.mult),
                  reads=[Rr_b, SIN_b], writes=[Zs_b])
            if kd < 6:
                continue
            for tb in range(self.NTB):
                sl = slice(tb * 512, (tb + 1) * 512)
                fw.op("pe", lambda e: e.matmul(yacc[tb][:], lhsT=self.LAp[:, gl * 128:(gl + 1) * 128], rhs=Zc[:, sl],
                                               start=(gl == 0), stop=False),
                      reads=[self.Lp_b, Zc_b], writes=[yacc_b[tb]], inc=False)
                fw.op("pe", lambda e: e.matmul(yacc[tb][:], lhsT=self.LBp[:, gl * 128:(gl + 1) * 128], rhs=Zs[:, sl],
                                               start=False, stop=(gl == 7)),
                      reads=[self.Lp_b, Zs_b], writes=[yacc_b[tb]])
        if kd < 7:
            return
        for tb in range(self.NTB):
            sl = slice(tb * 512, (tb + 1) * 512)
            fw.op("dve", lambda e: e.scalar_tensor_tensor(out=t5[1][:], in0=dinf[:, sl], scalar=self.dsk[:, ct:ct + 1],
                                                          in1=yacc[tb][:], op0=ALU.mult, op1=ALU.add),
                  reads=[dinf_b, self.dsk_b, yacc_b[tb]], writes=[t5b[1]])
            fw.op("act", lambda e: e.activation(out=gel[:, sl], in_=t5[1][:], func=AF.Gelu_apprx_tanh),
                  reads=[t5b[1]], writes=[gel_b])

    def mixer_D_glu(self, s):
        fw, S = self.fw, self.S
        t5, t5b = self.t5, self.t5_b
        for jo in range(4):
            for tb in range(self.NTB):
                sl = slice(tb * 512, (tb + 1) * 512)
                pp, ppb = self.psS[0], self.psS_b[0]
                for k in range(4):
                    fw.op("pe", lambda e: e.matmul(pp[:], lhsT=self.wglu[:, k, jo * 128:(jo + 1) * 128], rhs=self.gelb[k][:, sl],
                                                   start=(k == 0), stop=(k == 3)),
                          reads=[self.wglu_b, self.gelb_b[k]], writes=[ppb], inc=(k == 3))
                fw.op("act", lambda e: e.activation(out=t5[0][:], in_=pp[:], func=AF.Sigmoid, bias=self.dsk[:, 4 + jo:5 + jo]),
                      reads=[ppb, self.dsk_b], writes=[t5b[0]])
                pg, pgb = self.psP[tb % 2], self.psP_b[tb % 2]
                self.proj_fm(pg, pgb, jo // 2, jo % 2, tb)
                fw.op("act", lambda e: e.activation(out=t5[1][:], in_=pg[:], func=AF.Silu),
                      reads=[pgb], writes=[t5b[1]])
                fw.op("dve", lambda e: e.tensor_tensor(out=t5[2][:], in0=t5[0][:], in1=self.gelb[jo][:, sl], op=ALU.mult),
                      reads=[t5b[0], self.gelb_b[jo]], writes=[t5b[2]])
                yi = self.yctr % 2
                self.yctr += 1
                fw.op("dve", lambda e: e.tensor_tensor(out=self.yst[yi][:], in0=t5[2][:], in1=t5[1][:], op=ALU.mult),
                      reads=[t5b[2], t5b[1]], writes=[self.yst_b[yi]])
                self.spill_y(s, 12 + jo, tb * 512, 512, yi)

    def layer1(self, s):
        import os
        dbg = int(os.environ.get("KDBG", "9"))
        S = self.S
        xsrc = self.x1[s * S:(s + 1) * S, :]
        srcb = [self.x1_b[(s, tt)] for tt in range(self.NTT)] if self.do_l0 else None
        if dbg < 1:
            return
        self.norm_phase(xsrc, self.g16_1, self.g16_1_b, src_bufs=srcb)
        if dbg < 2:
            return
        self.gelb = [self.H[3], self.H[4], self.H[6], self.H[0]]
        self.gelb_b = [self.H_b[3], self.H_b[4], self.H_b[6], self.H_b[0]]
        if "L" not in os.environ.get("KSKIP", ""):
            self.load_w(0, self.w_in1, 0)
            self.load_w(1, self.w_in1, 1536)
        if "M" in os.environ.get("KSKIP", ""):
            return
        for n in range(6):
            if n + 1 < 6:
                self.load_w(2, self.w_in1, (n + 1) * 256) if False else None
            self.mixer_C(s, n)
            if n + 1 < 6:
                self.load_w(0, self.w_in1, (n + 1) * 256)
                self.load_w(1, self.w_in1, 1536 + (n + 1) * 256)
        if dbg < 3:
            return
        for ct in range(int(os.environ.get("KN", "4"))):
            ke = os.environ.get("KE", "")
            if ct % 2 == 0 and "n" not in ke:
                self.load_w(2, self.w_in1, 3072 + (ct // 2) * 256)
            if "a" in ke:
                continue
            self.mixer_D_tile(s, ct)
        if dbg < 4:
            return
        self.load_w(0, self.w_in1, 3584)
        self.load_w(1, self.w_in1, 3584 + 256)
        self.mixer_D_glu(s)
        if dbg < 5:
            return
        dst = self.out[s * S:(s + 1) * S, :]
        self.outproj_phase(s, self.w_out1, xsrc, dst, is_output=True, src_bufs=srcb)


def _prep_inputs_l0(inp):
    rel = np.asarray(inp["a_rel_bias"][0], np.float32)
    GA, MA = _tables_A(rel)
    TB, _ = _tables_B()
    aq = np.asarray(inp["a_q_g"][0], np.float32)
    ak = np.asarray(inp["a_k_g"][0], np.float32)
    bq = np.asarray(inp["b_q_g"][0], np.float32)
    bk = np.asarray(inp["b_k_g"][0], np.float32)
    qkg = np.stack([aq, aq, ak, ak, bq[0], bq[1], bk[0], bk[1]], axis=1)
    lamv = np.stack([inp["b_lam_q1"][0], inp["b_lam_k1"][0], inp["b_lam_q2"][0], inp["b_lam_k2"][0]], axis=1)
    subg = np.asarray(inp["b_subln_g"][0], np.float32).reshape(2, 128).T
    return {
        "w_in0": np.ascontiguousarray(inp["attn_w_in"][0], np.float32),
        "w_out0": np.ascontiguousarray(inp["attn_w_out"][0], np.float32),
        "g0": np.ascontiguousarray(np.asarray(inp["attn_norm_g"][0], np.float32).reshape(16, 128).T),
        "qkg": np.ascontiguousarray(qkg, np.float32),
        "subg": np.ascontiguousarray(subg, np.float32),
        "lamv": np.ascontiguousarray(lamv, np.float32),
        "GA": GA, "MA": MA, "TB": TB,
    }


def _prep_inputs_l1(inp):
    f = lambda k: np.asarray(inp[k][0], np.float32)
    cvw = f("lru_conv_w").reshape(4, 12, 128).transpose(2, 1, 0)
    t12 = lambda v: v.reshape(12, 128).T
    cl = np.stack([t12(f("lru_conv_b")), t12(f("lru_b_a").reshape(-1)), t12(f("lru_b_x").reshape(-1)),
                   t12(f("lru_lambda"))], axis=1)
    dup = lambda a: np.concatenate([a.T, a.T], axis=0)
    s5in = np.stack([dup(f("ssm_a_re")), dup(f("ssm_a_im")),
                     np.broadcast_to(f("ssm_log_dt")[None, :], (128, 32))], axis=1)
    sgn = np.concatenate([np.ones(64), -np.ones(64)]).astype(np.float32).reshape(128, 1)
    cre = f("ssm_c_re").transpose(2, 0, 1)
    cim = f("ssm_c_im").transpose(2, 0, 1)
    Cst = np.concatenate([cre, cim], axis=0)
    Csw = np.concatenate([cim, cre], axis=0)
    bre = f("ssm_b_re").transpose(0, 2, 1)
    bim = f("ssm_b_im").transpose(0, 2, 1)
    W1 = np.zeros((4, 128, 8, 128), np.float32)
    W2 = np.zeros((4, 128, 8, 128), np.float32)
    for g in range(32):
        ct, gl = g // 8, g % 8
        W1[ct, 16 * gl:16 * gl + 16, gl, 0:64] = bre[g]
        W1[ct, 16 * gl:16 * gl + 16, gl, 64:128] = bim[g]
        W2[ct, 16 * gl:16 * gl + 16, gl, 0:64] = bim[g]
        W2[ct, 16 * gl:16 * gl + 16, gl, 64:128] = bre[g]
    dsk = np.concatenate([f("ssm_d").reshape(4, 128).T, f("ssm_b_glu").reshape(4, 128).T], axis=1)
    c = np.ascontiguousarray
    return {
        "w_in1": c(f("rec_w_in")), "w_out1": c(f("rec_w_out")),
        "g1": c(f("rec_norm_g").reshape(16, 128).T),
        "cvw": c(cvw), "cl": c(cl), "lru_wa": c(f("lru_w_a")), "lru_wx": c(f("lru_w_x")),
        "s5in": c(s5in), "sgn": sgn, "Cst": c(Cst), "Csw": c(Csw),
        "W1": c(W1.reshape(4, 128, 1024)), "W2": c(W2.reshape(4, 128, 1024)),
        "dsk": c(dsk), "wglu": c(f("ssm_w_glu")),
    }


def run(inp, S, NSEQ, n_cores, do_l0=True, do_l1=True):
    prog = Prog(S, NSEQ, do_l0, do_l1)
    nc = prog.build()
    x = np.asarray(inp["x"], np.float32)
    shared = {"ident": np.eye(128, dtype=np.float32)}
    if do_l0:
        shared.update(_prep_inputs_l0(inp))
    if do_l1:
        shared.update(_prep_inputs_l1(inp))
    in_maps = []
    for c in range(n_cores):
        m = dict(shared)
        m["x"] = np.ascontiguousarray(x[c * NSEQ:(c + 1) * NSEQ].reshape(NSEQ * S, D))
        in_maps.append(m)
    res = run_bass_kernel_spmd(nc, in_maps, core_ids=list(range(n_cores)))
    outs = [np.asarray(r["out"]).reshape(NSEQ, S, D) for r in res.results]
    return np.concatenate(outs, axis=0).astype(np.float32)


def kernel(**inputs):
    return run(inputs, SEQ, BATCH // N_CORES, N_CORES)
```
